# Optimizing a Trainium2 kernel written in Bass

```python
import jax, jax.numpy as jnp
from jax import lax
import numpy as np

D_MODEL = 1024
BATCH = 4
SEQ = 8192
DEPTH = 1
DEC_BATCH = 16
DEC_SEQ = 32
PAST_LEN = 2048

CHUNK = 64
EPS = 1e-6
ROPE_BASE = 10000.0
H_RET = 4
W_RET = D_MODEL // 2
DH_RET = W_RET // H_RET
H_ML = 4
W_ML = D_MODEL // 2
DH_ML = W_ML // H_ML
CONV_W = 4
PEER_HEADS = 8
N_KEYS = 128
N_EXPERTS = N_KEYS * N_KEYS
PEER_TOPK = 16
PEER_DQ = 256
PEER_DQ_HALF = PEER_DQ // 2
PEER_BLOCK = 256
D_PLE = 256
N_IN = 4 * W_RET + 3 * W_ML + 2 * H_ML + 2 * D_MODEL

kernel_name = "hybrid_retention_mlstm_peer_stream_step"


def rmsnorm(x, g):
    xf = x.astype(jnp.float32)
    y = xf * lax.rsqrt(jnp.mean(xf * xf, axis=-1, keepdims=True) + EPS)
    return (y * g.astype(jnp.float32)).astype(x.dtype)


def head_rmsnorm(y, g):
    B, T, H, d = y.shape
    y = y * lax.rsqrt(jnp.mean(y * y, axis=-1, keepdims=True) + EPS)
    return y.reshape(B, T, H * d) * g.astype(jnp.float32)


def rope(x, pos):
    half = x.shape[-1] // 2
    inv = ROPE_BASE ** (-jnp.arange(half, dtype=jnp.float32) / half)
    ang = pos.astype(jnp.float32)[:, None] * inv[None, :]
    cos = jnp.cos(ang)[None, :, None, :]
    sin = jnp.sin(ang)[None, :, None, :]
    x1, x2 = x[..., :half], x[..., half:]
    return jnp.concatenate([x1 * cos - x2 * sin, x1 * sin + x2 * cos], axis=-1)


def split_cols(z):
    sizes = (W_RET, W_RET, W_RET, W_RET, W_ML, W_ML, W_ML, H_ML, H_ML, D_MODEL, D_MODEL)
    out = []
    off = 0
    for s in sizes:
        out.append(z[..., off:off + s])
        off += s
    return out


def causal_conv(x, buf, w, b):
    T = x.shape[1]
    xp = jnp.concatenate([buf, x], axis=1)
    y = b + xp[:, 0:T] * w[0]
    for j in range(1, CONV_W):
        y = y + xp[:, j:j + T] * w[j]
    return y, xp[:, T:]


def retention_chunkwise(q, k, v, s0):
    B, T, H, dk = q.shape
    dv = v.shape[-1]
    L = min(CHUNK, T)
    nc = T // L
    to_c = lambda t: t.reshape(B, nc, L, H, t.shape[-1]).transpose(0, 3, 1, 2, 4)
    q, k, v = to_c(q), to_c(k), to_c(v)
    lg = jnp.log1p(-jnp.exp2(-5.0 - jnp.arange(H, dtype=jnp.float32)))
    idx = jnp.arange(L, dtype=jnp.float32)
    diff = idx[:, None] - idx[None, :]
    dmask = jnp.where(diff >= 0, jnp.exp(lg[:, None, None] * jnp.maximum(diff, 0.0)), 0.0)
    scores = jnp.einsum('bhcid,bhcsd->bhcis', q, k) * dmask[None, :, None]
    o = jnp.einsum('bhcis,bhcse->bhcie', scores, v)
    w_end = jnp.exp(lg[:, None] * (L - 1 - idx)[None, :])
    ds = jnp.einsum('bhcsd,bhcse->bhcde', k * w_end[None, :, None, :, None], v)
    g_chunk = jnp.exp(lg * L)[None, :, None, None]

    def step(s, ds_c):
        return g_chunk * s + ds_c, s

    s_fin, s_prev = lax.scan(step, s0, jnp.moveaxis(ds, 2, 0))
    s_prev = jnp.moveaxis(s_prev, 0, 2)
    w_read = jnp.exp(lg[:, None] * (idx + 1.0)[None, :])
    o = o + jnp.einsum('bhcid,bhcde->bhcie', q * w_read[None, :, None, :, None], s_prev)
    return o.transpose(0, 2, 3, 1, 4).reshape(B, T, H, dv), s_fin


def mlstm_chunkwise(q, k, v, ig, fg, c0, n0, m0):
    B, T, H, dk = q.shape
    L = min(CHUNK, T)
    nc = T // L
    to_c = lambda t: t.reshape(B, nc, L, H, t.shape[-1]).transpose(1, 0, 3, 2, 4)
    to_cg = lambda t: t.reshape(B, nc, L, H).transpose(1, 0, 3, 2)
    causal = jnp.tril(jnp.ones((L, L), dtype=bool))

    def step(carry, xs):
        c, n, m = carry
        qc, kc, vc, ic, lfc = xs
        F = jnp.cumsum(lfc, axis=-1)
        log_d = jnp.where(causal, ic[..., None, :] + F[..., :, None] - F[..., None, :], -jnp.inf)
        inter = m[..., None] + F
        m_t = jnp.maximum(inter, jnp.max(log_d, axis=-1))
        dw = jnp.exp(log_d - m_t[..., None])
        a = jnp.exp(inter - m_t)
        s = jnp.einsum('bhid,bhsd->bhis', qc, kc) * dw
        num = jnp.einsum('bhis,bhse->bhie', s, vc) + a[..., None] * jnp.einsum('bhid,bhde->bhie', qc, c)
        den = jnp.sum(s, axis=-1) + a * jnp.einsum('bhid,bhd->bhi', qc, n)
        hc = num / jnp.maximum(jnp.abs(den), jnp.exp(-m_t))[..., None]
        m_new = m_t[..., -1]
        w_end = jnp.exp(ic + F[..., -1:] - F - m_new[..., None])
        a_end = jnp.exp(m + F[..., -1] - m_new)
        kw = kc * w_end[..., None]
        c_new = a_end[..., None, None] * c + jnp.einsum('bhsd,bhse->bhde', kw, vc)
        n_new = a_end[..., None] * n + jnp.sum(kw, axis=-2)
        return (c_new, n_new, m_new), hc

    logf = jax.nn.log_sigmoid(fg)
    (c, n, m), h = lax.scan(step, (c0, n0, m0), (to_c(q), to_c(k), to_c(v), to_cg(ig), to_cg(logf)))
    h = h.transpose(1, 0, 3, 2, 4).reshape(B, T, H, -1)
    return h, c, n, m


def peer_ffn(h, w_pq, peer_keys, peer_u, peer_v):
    B, T, D = h.shape
    n = B * T
    nb = -(-n // PEER_BLOCK)
    flat = jnp.pad(h.reshape(n, D), ((0, nb * PEER_BLOCK - n), (0, 0)))

    def one_block(xb):
        q = (xb @ w_pq).reshape(PEER_BLOCK, PEER_HEADS, 2, PEER_DQ_HALF)
        s = jnp.einsum('tnhd,nhkd->tnhk', q, peer_keys).astype(jnp.float32)
        s1, i1 = lax.top_k(s[:, :, 0], PEER_TOPK)
        s2, i2 = lax.top_k(s[:, :, 1], PEER_TOPK)
        cand = (s1[..., :, None] + s2[..., None, :]).reshape(PEER_BLOCK, PEER_HEADS, PEER_TOPK * PEER_TOPK)
        sc, ci = lax.top_k(cand, PEER_TOPK)
        e = (jnp.take_along_axis(i1, ci // PEER_TOPK, axis=-1) * N_KEYS
             + jnp.take_along_axis(i2, ci % PEER_TOPK, axis=-1))
        g = jax.nn.softmax(sc, axis=-1)
        act = jax.nn.gelu(jnp.einsum('td,tnkd->tnk', xb, peer_u[e]).astype(jnp.float32), approximate=False)
        return jnp.einsum('tnk,tnkd->td', (g * act).astype(xb.dtype), peer_v[e])

    out = lax.map(one_block, flat.reshape(nb, PEER_BLOCK, D))
    return out.reshape(nb * PEER_BLOCK, D)[:n].reshape(B, T, D)


def hybrid_layer(x, p, pos, s_ret, c_ml, n_ml, m_ml, conv_buf,
                 g_mix, w_in, g_ret_gn, w_mq, w_mk, conv_w, conv_b, b_i, b_f, g_ml_gn, w_skip,
                 w_up_r, w_up_m, w_out, g_ffn, w_pq, peer_keys, peer_u, peer_v, g_ple, w_pg, w_ple):
    B, T, _ = x.shape
    f32 = jnp.float32
    dt = x.dtype
    h = rmsnorm(x, g_mix)
    z = h @ w_in
    q_r, k_r, v_r, gt_r, xm, v_m, o_m, i_m, f_m, gate_r, gate_m = split_cols(z)
    heads = lambda t, H: t.astype(f32).reshape(B, T, H, -1)
    q_r = rope(heads(q_r, H_RET), pos)
    k_r = rope(heads(k_r, H_RET), pos) * (DH_RET ** -0.5)
    o_ret, s_ret_new = retention_chunkwise(q_r, k_r, heads(v_r, H_RET), s_ret.astype(f32))
    y_r = jax.nn.silu(gt_r.astype(f32)) * head_rmsnorm(o_ret, g_ret_gn)
    xc, conv_new = causal_conv(xm, conv_buf.astype(dt), conv_w, conv_b)
    c = jax.nn.silu(xc.astype(f32))
    ch = c.reshape(B, T, H_ML, DH_ML)
    q_m = jnp.einsum('bthd,hde->bthe', ch, w_mq.astype(f32))
    k_m = jnp.einsum('bthd,hde->bthe', ch, w_mk.astype(f32)) * (DH_ML ** -0.5)
    ig = i_m.astype(f32) + b_i.astype(f32)
    fg = f_m.astype(f32) + b_f.astype(f32)
    h_m, c_new, n_new, m_new = mlstm_chunkwise(q_m, k_m, heads(v_m, H_ML), ig, fg,
                                               c_ml.astype(f32), n_ml.astype(f32), m_ml.astype(f32))
    y_m = jax.nn.sigmoid(o_m.astype(f32)) * (head_rmsnorm(h_m, g_ml_gn) + w_skip.astype(f32) * c)
    merged = (jax.nn.sigmoid(gate_r) * (y_r.astype(dt) @ w_up_r)
              + jax.nn.sigmoid(gate_m) * (y_m.astype(dt) @ w_up_m))
    x = x + merged @ w_out
    x = x + peer_ffn(rmsnorm(x, g_ffn), w_pq, peer_keys, peer_u, peer_v)
    x = x + (p @ w_ple) * jax.nn.sigmoid(rmsnorm(x, g_ple) @ w_pg)
    return x, s_ret_new, c_new, n_new, m_new, conv_new


def setup_inputs(seed: int = 0) -> dict:
    key = jax.random.key(seed)
    ks = jax.random.split(key, 32)
    f32 = jnp.float32
    nrm = lambda k, shape, scale: scale * jax.random.normal(k, shape, f32)
    gain = lambda k, shape: 1.0 + 0.05 * jax.random.normal(k, shape, f32)
    return {
        "x_prompt": nrm(ks[0], (BATCH, SEQ, D_MODEL), 1.0),
        "x_sample": nrm(ks[1], (DEC_BATCH, DEC_SEQ, D_MODEL), 1.0),
        "p_prompt": nrm(ks[2], (DEPTH, BATCH, SEQ, D_PLE), 1.0),
        "p_sample": nrm(ks[3], (DEPTH, DEC_BATCH, DEC_SEQ, D_PLE), 1.0),
        "state_ret": nrm(ks[4], (DEPTH, DEC_BATCH, H_RET, DH_RET, DH_RET), 0.5),
        "state_mlstm_C": nrm(ks[5], (DEPTH, DEC_BATCH, H_ML, DH_ML, DH_ML), 0.3),
        "state_mlstm_n": nrm(ks[6], (DEPTH, DEC_BATCH, H_ML, DH_ML), 0.3),
        "state_mlstm_m": nrm(ks[7], (DEPTH, DEC_BATCH, H_ML), 1.0),
        "state_conv": nrm(ks[8], (DEPTH, DEC_BATCH, CONV_W - 1, W_ML), 1.0),
        "g_mix": gain(ks[9], (DEPTH, D_MODEL)),
        "w_in": nrm(ks[10], (DEPTH, D_MODEL, N_IN), D_MODEL ** -0.5),
        "g_ret_gn": gain(ks[11], (DEPTH, W_RET)),
        "w_mq": nrm(ks[12], (DEPTH, H_ML, DH_ML, DH_ML), DH_ML ** -0.5),
        "w_mk": nrm(ks[13], (DEPTH, H_ML, DH_ML, DH_ML), DH_ML ** -0.5),
        "conv_w": nrm(ks[14], (DEPTH, CONV_W, W_ML), CONV_W ** -0.5),
        "conv_b": nrm(ks[15], (DEPTH, W_ML), 0.02),
        "b_i": nrm(ks[16], (DEPTH, H_ML), 0.1),
        "b_f": jnp.linspace(3.0, 6.0, H_ML, dtype=f32)[None, :] + nrm(ks[17], (DEPTH, H_ML), 0.1),
        "g_ml_gn": gain(ks[18], (DEPTH, W_ML)),
        "w_skip": gain(ks[19], (DEPTH, W_ML)),
        "w_up_r": nrm(ks[20], (DEPTH, W_RET, D_MODEL), W_RET ** -0.5),
        "w_up_m": nrm(ks[21], (DEPTH, W_ML, D_MODEL), W_ML ** -0.5),
        "w_out": nrm(ks[22], (DEPTH, D_MODEL, D_MODEL), D_MODEL ** -0.5),
        "g_ffn": gain(ks[23], (DEPTH, D_MODEL)),
        "w_pq": nrm(ks[24], (DEPTH, D_MODEL, PEER_HEADS * PEER_DQ), D_MODEL ** -0.5),
        "peer_keys": nrm(ks[25], (DEPTH, PEER_HEADS, 2, N_KEYS, PEER_DQ_HALF), PEER_DQ_HALF ** -0.5),
        "peer_u": nrm(ks[26], (DEPTH, N_EXPERTS, D_MODEL), D_MODEL ** -0.5),
        "peer_v": nrm(ks[27], (DEPTH, N_EXPERTS, D_MODEL), 0.1),
        "g_ple": gain(ks[28], (DEPTH, D_MODEL)),
        "w_pg": nrm(ks[29], (DEPTH, D_MODEL, D_MODEL), D_MODEL ** -0.5),
        "w_ple": nrm(ks[30], (DEPTH, D_PLE, D_MODEL), D_PLE ** -0.5),
        "g_final": gain(ks[31], (D_MODEL,)),
    }


def reference(x_prompt, x_sample, p_prompt, p_sample, state_ret, state_mlstm_C, state_mlstm_n,
              state_mlstm_m, state_conv, g_mix, w_in, g_ret_gn, w_mq, w_mk, conv_w, conv_b, b_i, b_f,
              g_ml_gn, w_skip, w_up_r, w_up_m, w_out, g_ffn, w_pq, peer_keys, peer_u, peer_v,
              g_ple, w_pg, w_ple, g_final):
    f32 = jnp.float32
    Bp = x_prompt.shape[0]
    pos_p = jnp.arange(x_prompt.shape[1], dtype=jnp.int32)
    pos_s = PAST_LEN + jnp.arange(x_sample.shape[1], dtype=jnp.int32)
    z_ret = jnp.zeros((Bp, H_RET, DH_RET, DH_RET), f32)
    z_c = jnp.zeros((Bp, H_ML, DH_ML, DH_ML), f32)
    z_n = jnp.zeros((Bp, H_ML, DH_ML), f32)
    z_m = jnp.zeros((Bp, H_ML), f32)
    z_buf = jnp.zeros((Bp, CONV_W - 1, W_ML), x_prompt.dtype)
    hp, hs = x_prompt, x_sample
    rp, cp, np_, mp, bp = [], [], [], [], []
    rs, cs, ns, ms, bs = [], [], [], [], []
    for l in range(DEPTH):
        lw = (g_mix[l], w_in[l], g_ret_gn[l], w_mq[l], w_mk[l], conv_w[l], conv_b[l], b_i[l], b_f[l],
              g_ml_gn[l], w_skip[l], w_up_r[l], w_up_m[l], w_out[l], g_ffn[l], w_pq[l], peer_keys[l],
              peer_u[l], peer_v[l], g_ple[l], w_pg[l], w_ple[l])
        hp, a1, a2, a3, a4, a5 = hybrid_layer(hp, p_prompt[l], pos_p, z_ret, z_c, z_n, z_m, z_buf, *lw)
        rp.append(a1); cp.append(a2); np_.append(a3); mp.append(a4); bp.append(a5)
        hs, b1, b2, b3, b4, b5 = hybrid_layer(hs, p_sample[l], pos_s, state_ret[l], state_mlstm_C[l],
                                              state_mlstm_n[l], state_mlstm_m[l], state_conv[l], *lw)
        rs.append(b1); cs.append(b2); ns.append(b3); ms.append(b4); bs.append(b5)
    y_prompt = rmsnorm(hp, g_final)
    y_sample = rmsnorm(hs, g_final)
    ret_p = jnp.stack(rp).astype(state_ret.dtype)
    c_p = jnp.stack(cp).astype(state_mlstm_C.dtype)
    n_p = jnp.stack(np_).astype(state_mlstm_n.dtype)
    m_p = jnp.stack(mp).astype(state_mlstm_m.dtype)
    conv_p = jnp.stack(bp).astype(state_conv.dtype)
    ret_s = jnp.stack(rs).astype(state_ret.dtype)
    c_s = jnp.stack(cs).astype(state_mlstm_C.dtype)
    n_s = jnp.stack(ns).astype(state_mlstm_n.dtype)
    m_s = jnp.stack(ms).astype(state_mlstm_m.dtype)
    conv_s = jnp.stack(bs).astype(state_conv.dtype)
    return (y_prompt, y_sample, ret_p, c_p, n_p, m_p, conv_p, ret_s, c_s, n_s, m_s, conv_s)
```

```python
import math
import numpy as np
from contextlib import ExitStack
import concourse.bass as bass
import concourse.mybir as mybir
from concourse.bass_utils import run_bass_kernel_spmd

F32 = mybir.dt.float32
BF16 = mybir.dt.bfloat16
AF = mybir.ActivationFunctionType
ALU = mybir.AluOpType
AX = mybir.AxisListType

D = 1024
NH = 4
DH = 128
EPS = 1e-6
NEG = -30000.0
NE = 16384
ISQ = 128.0 ** -0.5
GAM = [1.0 - 2.0 ** (-5 - h) for h in range(4)]


class Sem:
    def __init__(self, h, sid):
        self.h = h
        self.sid = sid
        self.count = 0


class Buf:
    def __init__(self, name, ap=None):
        self.name = name
        self.ap = ap
        self.w = None
        self.rs = {}
        self.dsem = None

    def __getitem__(self, k):
        return self.ap[k]


class Prog:
    def __init__(self, nc, stack):
        self.nc = nc
        self.stack = stack
        self.streams = {n: [] for n in ("pe", "dve", "act", "pool", "sp")}
        self.seen = {n: {} for n in self.streams}
        self.nsem = 0
        self.all_sems = []
        self.esem = {n: self.new_sem("prog_" + n) for n in ("pe", "dve", "act", "pool")}
        self.ninst = 0

    def new_sem(self, name):
        h = self.stack.enter_context(self.nc.semaphore(name))
        s = Sem(h, self.nsem)
        self.nsem += 1
        if hasattr(self, "all_sems"):
            self.all_sems.append(s)
        return s

    def barrier(self):
        toks = [(s_, s_.count) for s_ in self.all_sems if s_.count > 0]
        for eng in self.streams:
            seen = self.seen[eng]
            waits = []
            for s_, v in toks:
                if seen.get(s_.sid, 0) >= v:
                    continue
                if eng == "pe" and s_ is self.esem["pe"]:
                    continue
                seen[s_.sid] = v
                waits.append((s_, v))
            self.streams[eng].append((waits, None, None, 0))

    def sbuf(self, name, shape, dtype):
        t = self.stack.enter_context(self.nc.sbuf_tensor("sb_" + name, list(shape), dtype))
        nb = 4 if dtype == F32 else 2
        for d_ in shape[1:]:
            nb *= d_
        self.sbuf_bytes = getattr(self, "sbuf_bytes", 0) + (nb + 31) // 32 * 32
        return Buf(name, t)

    def _deps(self, eng, reads, writes):
        deps = {}
        own = self.esem.get(eng)

        def add(tok, waw=False):
            if tok is None:
                return
            s, v = tok
            if waw and s is own:
                return
            if s.sid not in deps or deps[s.sid][1] < v:
                deps[s.sid] = (s, v)

        for b in reads:
            add(b.w)
        for b in writes:
            add(b.w, True)
            for tok in b.rs.values():
                add(tok)
        waits = []
        seen = self.seen[eng]
        for sid, (s, v) in deps.items():
            if seen.get(sid, 0) >= v:
                continue
            if eng == "pe" and s is self.esem["pe"]:
                continue
            seen[sid] = v
            waits.append((s, v))
        return waits

    def _mark(self, tok, reads, writes):
        for b in reads:
            b.rs[tok[0].sid] = tok
        for b in writes:
            b.w = tok
            b.rs = {}

    def op(self, eng, fn, reads=(), writes=()):
        waits = self._deps(eng, reads, writes)
        s = self.esem[eng]
        s.count += 1
        tok = (s, s.count)
        self.streams[eng].append((waits, fn, s.h, 1))
        self._mark(tok, reads, writes)
        self.ninst += 1
        return tok

    def dma(self, q, out_ap, in_ap, reads=(), writes=(), sem_buf=None, **kw):
        waits = self._deps(q, reads, writes)
        b = sem_buf
        if b.dsem is None:
            b.dsem = self.new_sem("d_" + b.name)
        s = b.dsem
        s.count += 16
        tok = (s, s.count)

        def fn(e, out_ap=out_ap, in_ap=in_ap, kw=kw):
            return e.dma_start(out=out_ap, in_=in_ap, **kw)

        self.streams[q].append((waits, fn, s.h, 16))
        self._mark(tok, reads, writes)
        self.ninst += 1
        return tok

    def finish(self, out_bufs):
        waits = [b.w for b in out_bufs if b.w is not None]
        self.streams["sp"].append((waits, None, None, 0))

    def replay(self):
        nc = self.nc
        eng_of = {"pe": "tensor", "dve": "vector", "act": "scalar", "pool": "gpsimd", "sp": "sync"}
        with nc.Block() as block:
            def run(name):
                def body(e):
                    for waits, fn, sem, inc in self.streams[name]:
                        for s, v in waits:
                            e.wait_ge(s.h, v)
                        if fn is not None:
                            fn(e).then_inc(sem, inc)
                return body

            for name, attr in eng_of.items():
                getattr(block, attr)(run(name))


def MM(out, lhsT, rhs, start=True, stop=True):
    return lambda e: e.matmul(out, lhsT=lhsT, rhs=rhs, start=start, stop=stop)


def TR(out, in_, ident):
    return lambda e: e.transpose(out, in_, ident)


def ACT(out, in_, func, bias=None, scale=None, accum_out=None, alpha=None):
    kw = {}
    if alpha is not None:
        kw["alpha"] = alpha
    if bias is not None:
        kw["bias"] = bias
    if scale is not None:
        kw["scale"] = scale
    if accum_out is not None:
        kw["accum_out"] = accum_out
    return lambda e: e.activation(out=out, in_=in_, func=func, **kw)


def TT(out, in0, in1, op):
    return lambda e: e.tensor_tensor(out=out, in0=in0, in1=in1, op=op)


def TS(out, in0, s1, op0, s2=None, op1=None):
    if op1 is None:
        return lambda e: e.tensor_scalar(out=out, in0=in0, scalar1=s1, scalar2=None, op0=op0)
    return lambda e: e.tensor_scalar(out=out, in0=in0, scalar1=s1, scalar2=s2, op0=op0, op1=op1)


def STT(out, in0, scalar, in1, op0, op1):
    return lambda e: e.scalar_tensor_tensor(out=out, in0=in0, scalar=scalar, in1=in1, op0=op0, op1=op1)


def CP(out, in_):
    return lambda e: e.tensor_copy(out=out, in_=in_)


def RED(out, in_, op, axis=AX.X):
    return lambda e: e.tensor_reduce(out=out, in_=in_, axis=axis, op=op)


def MSET(ap, v):
    return lambda e: e.memset(ap, v)


class Ring:
    def __init__(self, P, name, depth, elems=4096, dtype=BF16, alloc=None):
        self.P = P
        alloc = alloc or P.sbuf
        self.bufs = [alloc("%s%d" % (name, i), [128, elems], dtype) for i in range(depth)]
        self.i = 0

    def load(self, src_ap, nelem, src_buf, q="sp"):
        b = self.bufs[self.i % len(self.bufs)]
        self.i += 1
        self.P.dma(q, b[:, 0:nelem], src_ap, reads=[src_buf], writes=[b], sem_buf=b)
        return b


def stream(ring, items, fn, L=2):
    pend = {}
    n = len(items)
    for i in range(min(L, n)):
        pend[i] = ring.load(*items[i])
    for i in range(n):
        if i + L < n:
            pend[i + L] = ring.load(*items[i + L])
        fn(i, pend.pop(i))


FM_Q, FM_QS, FM_K, FM_KS, FM_GT, FM_XM, FM_OM, FM_GR, FM_GM = 0, 4, 8, 12, 16, 20, 24, 28, 36
NFM = 44


def build_program(cfg):
    NTM = cfg["nt_main"]
    NTP = cfg["nt_pre"]
    NS = cfg.get("n_smp", 2)
    NEC = cfg.get("nec", 128)
    stop_after = cfg.get("stop_after", None)
    nc = bass.Bass("TRN2", target_bir_lowering=False)

    def din(name, shape, dt=F32):
        return nc.dram_tensor(name, list(shape), dt, kind="ExternalInput").ap()

    def dout(name, shape, dt=F32):
        return nc.dram_tensor(name, list(shape), dt, kind="ExternalOutput").ap()

    def dscr(name, shape, dt=BF16):
        return nc.dram_tensor(name, list(shape), dt, kind="Internal").ap()

    TM, TP = NTM * 128, max(NTP, 1) * 128
    I = {}
    I["x_main"] = din("x_main", [TM, D]); I["p_main"] = din("p_main", [TM, 256])
    I["x_pre"] = din("x_pre", [TP, D])
    I["x_smp"] = din("x_smp", [NS * 32, D]); I["p_smp"] = din("p_smp", [NS * 32, 256])
    I["flag"] = din("flag", [128, 1])
    I["st_ret"] = din("st_ret", [NS, 4, 128, 128]); I["st_C"] = din("st_C", [NS, 4, 128, 128])
    I["st_n"] = din("st_n", [NS, 4, 128]); I["st_m"] = din("st_m", [NS, 4]); I["st_conv"] = din("st_conv", [NS, 3, 512])
    I["w_in_fm"] = din("w_in_fm", [D, NFM * 128]); I["w_in_tm"] = din("w_in_tm", [D, 1032])
    for nm, sh in (("g_mix", [D]), ("g_ret_gn", [512]), ("w_mq", [4, 128, 128]), ("w_mk", [4, 128, 128]),
                   ("conv_w", [4, 512]), ("conv_b", [512]), ("b_if", [8]), ("g_ml_gn", [512]), ("w_skip", [512]),
                   ("w_up_r", [512, D]), ("w_up_m", [512, D]), ("w_out", [D, D]), ("g_ffn", [D]),
                   ("w_pq", [D, 2048]), ("peer_keys", [16, 128, 128]), ("peer_u", [NE, D]), ("peer_v", [NE, D]),
                   ("g_ple", [D]), ("w_pg", [D, D]), ("w_ple", [256, D]), ("g_final", [D])):
        I[nm] = din(nm, sh)
    for nm, sh in (("c_ident", [128, 128]), ("c_ones", [128, 128]), ("c_tri", [128, 128]), ("c_negU", [128, 512]),
                   ("c_negL", [128, 512]), ("c_dmT", [128, 512]), ("c_wread", [128, 4]), ("c_wend128", [128, 4]),
                   ("c_wend32", [128, 4]), ("c_sel128", [128, 128]), ("c_sel32", [128, 128]),
                   ("c_cs_main", [128, TM]), ("c_sn_main", [128, TM]), ("c_cs_pre", [128, TP]), ("c_sn_pre", [128, TP]),
                   ("c_cs_smp", [128, 32]), ("c_sn_smp", [128, 32])):
        I[nm] = din(nm, sh)
    O = {}
    O["y_main"] = dout("y_main", [TM, D]); O["y_smp"] = dout("y_smp", [NS * 32, D])
    O["ret_p"] = dout("ret_p", [4, 128, 128]); O["C_p"] = dout("C_p", [4, 128, 128]); O["n_p"] = dout("n_p", [4, 128])
    O["m_p"] = dout("m_p", [1, 4]); O["conv_p"] = dout("conv_p", [3, 512])
    O["ret_s"] = dout("ret_s", [NS, 4, 128, 128]); O["C_s"] = dout("C_s", [NS, 4, 128, 128]); O["n_s"] = dout("n_s", [NS, 4, 128])
    O["m_s"] = dout("m_s", [NS, 4]); O["conv_s"] = dout("conv_s", [NS, 3, 512])
    if stop_after:
        O["dbg"] = dout("dbg", [TM, D])
    S_fm = dscr("s_fm", [NFM // 4, 128, 4 * 8 * 128])
    S_tm = dscr("s_tm", [2, 128, 8 * 512])
    S_up = dscr("s_up", [2, 128, 4 * 1024])
    S_out = dscr("s_out", [2, 128, 8 * 512])
    S_pq = dscr("s_pq", [4, 128, 4 * 8 * 128])
    S_pg = dscr("s_pg", [2, 128, 8 * 512])
    S_ut = dscr("s_ut", [NEC, 128, 8 * 128])
    S_v = dscr("s_v", [NE, D])
    DB = {k: Buf("dram_" + k) for k in list(I) + list(O) + ["s_fm", "s_tm", "s_up", "s_out", "s_pq", "s_pg", "s_ut", "s_v"]}

    st = ExitStack()
    with st:
        P = Prog(nc, st)
        psum_all = st.enter_context(nc.psum_tensor("psum_all", [128, 4096], F32))
        PB = [Buf("psb%d" % i) for i in range(8)]

        def ps(b, n=128, w=512):
            return psum_all[0:n, b * 512:b * 512 + w]

        def psb16(b, n=128, w=1024):
            return psum_all[:, b * 512:(b + 1) * 512].bitcast(BF16)[0:n, 0:w]

        def cload(name, shape, src, dt=F32, q="sp"):
            b = P.sbuf(name, shape, dt)
            P.dma(q, b[:], src, writes=[b], sem_buf=b)
            return b

        ident = cload("ident", [128, 128], I["c_ident"])
        identb = cload("identb", [128, 128], I["c_ident"], BF16, "pool")
        ones = cload("ones", [128, 128], I["c_ones"])
        tri = cload("tri", [128, 128], I["c_tri"])
        negU = cload("negU", [128, 512], I["c_negU"])
        negL = cload("negL", [128, 512], I["c_negL"])
        dmT = cload("dmT", [128, 512], I["c_dmT"])
        wread = cload("wread", [128, 4], I["c_wread"])
        wend = {128: cload("wend128", [128, 4], I["c_wend128"]), 32: cload("wend32", [128, 4], I["c_wend32"])}
        sel = {128: cload("sel128", [128, 128], I["c_sel128"]), 32: cload("sel32", [128, 128], I["c_sel32"])}
        flag = cload("flag", [128, 1], I["flag"])

        def col_load(name, src1d, ncol):
            b = P.sbuf(name, [128, ncol], F32)
            P.dma("sp", b[:], src1d.rearrange("(c p) -> p c", p=128), writes=[b], sem_buf=b, allow_slow_non_contiguous=True)
            return b

        def bc_load(name, src1d, n):
            b = P.sbuf(name, [128, n], F32)
            P.dma("sp", b[:], src1d.partition_broadcast(128), writes=[b], sem_buf=b)
            return b

        gmix_c = col_load("gmix_c", I["g_mix"], 8)
        gffn_c = col_load("gffn_c", I["g_ffn"], 8)
        gple_c = col_load("gple_c", I["g_ple"], 8)
        gret_c = col_load("gret_c", I["g_ret_gn"], 4)
        gml_c = col_load("gml_c", I["g_ml_gn"], 4)
        wskip_c = col_load("wskip_c", I["w_skip"], 4)
        convb_c = col_load("convb_c", I["conv_b"], 4)
        convw_c = P.sbuf("convw_c", [128, 4, 4], F32)
        P.dma("sp", convw_c[:], I["conv_w"].rearrange("j (c p) -> p j c", p=128), writes=[convw_c], sem_buf=convw_c, allow_slow_non_contiguous=True)
        bif_b = bc_load("bif_b", I["b_if"], 8)
        gfin_b = bc_load("gfin_b", I["g_final"], D)

        NB = 256
        ARENA_BYTES = 132 * 1024
        arena_t = P.sbuf("arena", [128, ARENA_BYTES // 2], BF16)
        aoff = [0]

        def arena_reset():
            aoff[0] = 0

        def A(name, shape, dtype):
            esz = 4 if dtype == F32 else 2
            nel = 1
            for d_ in shape[1:]:
                nel *= d_
            nb_ = (nel * esz + 31) // 32 * 32
            o_ = aoff[0]
            aoff[0] += nb_
            assert aoff[0] <= ARENA_BYTES, (name, aoff[0])
            ap = arena_t[:, o_ // 2:(o_ + nel * esz) // 2]
            if dtype == F32:
                ap = ap.bitcast(F32)
            if len(shape) == 3:
                ap = ap.rearrange("p (a b) -> p a b", a=shape[1])
            elif len(shape) == 4:
                ap = ap.rearrange("p (a b c) -> p a b c", a=shape[1], b=shape[2])
            return Buf(name, ap)

        xt = [P.sbuf("xt%d" % j, [128, D], F32) for j in range(2)]
        hb = P.sbuf("hb", [128, D], BF16)
        junk = hb
        hT = P.sbuf("hT", [128, 8, NB], BF16)
        st1 = P.sbuf("st1", [128, 8], F32)
        rstd = P.sbuf("rstd", [128, 1], F32)
        ring = Ring(P, "wr", 3)
        xmh = P.sbuf("xmh", [128, 4, 3], F32)
        wmq = P.sbuf("wmq", [128, 4, 128], BF16)
        wmk = P.sbuf("wmk", [128, 4, 128], BF16)
        wif = P.sbuf("wif", [128, 8, 8], BF16)
        v1 = [P.sbuf("v1_%d" % j, [128, 4, 129], BF16) for j in range(2)]
        gs = {k: P.sbuf("gs_" + k, [128, 4], F32) for k in
              ("e1", "sp", "g", "M", "mm", "nmm", "mt", "a", "enm", "wend", "den", "d2", "ss", "rs", "ssr", "rsr")}
        mtmm = P.sbuf("mtmm", [128, 8], F32)
        lastb = P.sbuf("lastb", [128, 8], F32)
        aend = P.sbuf("aend", [128, 4], F32)
        mprev = P.sbuf("mprev", [128, 4], F32)
        S32 = P.sbuf("S32", [128, 4, 128], F32)
        Sbf = P.sbuf("Sbf", [128, 4, 128], BF16)
        Cn32 = P.sbuf("Cn32", [128, 4, 129], F32)
        Cnbf = P.sbuf("Cnbf", [128, 4, 129], BF16)
        arena_reset()
        rp_t2 = A("rp_t2", [128, NB], F32)
        cs_t = A("cs_t", [128, NB], F32)
        sn_t = A("sn_t", [128, NB], F32)
        qT = A("qT", [128, 4, NB], BF16)
        kT = A("kT", [128, 4, NB], BF16)
        sgt = A("sgt", [128, 4, NB], BF16)
        sgo = A("sgo", [128, 4, NB], BF16)
        sgr = A("sgr", [128, 8, NB], BF16)
        sgm = A("sgm", [128, 8, NB], BF16)
        xp = [A("xp%d" % j, [128, 4, 131], F32) for j in range(2)]
        cva = A("cva", [128, 4, 128], F32)
        cT = A("cT", [128, 4, NB], BF16)
        qmT = A("qmT", [128, 4, NB], BF16)
        kmT = A("kmT", [128, 4, NB], BF16)
        vr = [A("vr%d" % j, [128, 4, 128], BF16) for j in range(2)]
        gpre = [A("gpre%d" % j, [128, 8], F32) for j in range(2)]
        diag = A("diag", [128, 4, 128], F32)
        DT = A("DT", [128, 4, 128], F32)
        wTr = A("wTr", [128, 4, 128], BF16)
        wTm = A("wTm", [128, 4, 128], BF16)
        kwr = A("kwr", [128, 4, 128], BF16)
        kwm = A("kwm", [128, 4, 128], BF16)
        otmp = A("otmp", [128, 4, 129], F32)
        otmp2 = A("otmp2", [128, 4, 129], F32)
        sqt = A("sqt", [128, 4, 128], F32)
        onb = A("onb", [128, 4, 128], BF16)
        hnb = A("hnb", [128, 4, 128], BF16)
        yrT = A("yrT", [128, 4, NB], BF16)
        ymT = A("ymT", [128, 4, NB], BF16)
        ytmp = A("ytmp", [128, 4, 128], F32)
        ytmp2 = A("ytmp2", [128, 4, 128], F32)
        mg1 = A("mg1", [128, NB], F32)
        mg2 = A("mg2", [128, NB], F32)
        mgT = A("mgT", [128, 8, NB], BF16)
        ropeC = [A("ropeC%d" % i, [128, 4, NB], F32) for i in range(2)]
        print("[kernel] arena mixer bytes", aoff[0], flush=True)

        for j in range(2):
            P.op("pool", MSET(v1[j][:, :, 128:129], 1.0), writes=[v1[j]])

        arena_reset()
        stg = [A("stg%d" % i, [128, 8, 512], F32) for i in range(2)]
        stgb = [A("stgb%d" % i, [128, 4096], BF16) for i in range(2)]
        sti = [0]

        def fold_panel(src, ncols, gcol, dst_ap, dst_buf, grouped):
            i = sti[0] % 2
            sti[0] += 1
            a, b = stg[i], stgb[i]
            P.dma("sp", a[:, :, 0:ncols], src.rearrange("(kc p) c -> p kc c", p=128), writes=[a], sem_buf=a)
            if grouped:
                ov = b[:, 0:4096].rearrange("p (g kc c) -> p kc g c", g=4, kc=8, c=128)
                iv = a[:, :, 0:512].rearrange("p kc (g c) -> p kc g c", g=4)
            else:
                ov = b[:, 0:8 * ncols].rearrange("p (kc c) -> p kc c", kc=8)
                iv = a[:, :, 0:ncols]
            for kc in range(8):
                eng = "dve" if kc % 2 == 0 else "pool"
                if gcol is None:
                    P.op(eng, CP(ov[:, kc], iv[:, kc]), reads=[a], writes=[b])
                else:
                    P.op(eng, TS(ov[:, kc], iv[:, kc], gcol[:, kc:kc + 1], ALU.mult), reads=[a, gcol], writes=[b])
            P.dma("sp", dst_ap, b[:, 0:8 * ncols], reads=[b], writes=[dst_buf], sem_buf=b)

        for qd in range(NFM // 4):
            fold_panel(I["w_in_fm"][:, qd * 512:(qd + 1) * 512], 512, gmix_c, S_fm[qd], DB["s_fm"], True)
        for t in range(2):
            fold_panel(I["w_in_tm"][:, t * 512:(t + 1) * 512], 512, gmix_c, S_tm[t], DB["s_tm"], False)
        wif32 = A("wif32", [128, 8, 8], F32)
        P.dma("sp", wif32[:], I["w_in_tm"][:, 1024:1032].rearrange("(kc p) c -> p kc c", p=128), writes=[wif32], sem_buf=wif32)
        for kc in range(8):
            P.op("dve", TS(wif[:, kc], wif32[:, kc], gmix_c[:, kc:kc + 1], ALU.mult), reads=[wif32, gmix_c], writes=[wif])
        for qd in range(4):
            fold_panel(I["w_pq"][:, qd * 512:(qd + 1) * 512], 512, gffn_c, S_pq[qd], DB["s_pq"], True)
        for t in range(2):
            fold_panel(I["w_pg"][:, t * 512:(t + 1) * 512], 512, gple_c, S_pg[t], DB["s_pg"], False)
            fold_panel(I["w_out"][:, t * 512:(t + 1) * 512], 512, None, S_out[t], DB["s_out"], False)
        for t, nm in enumerate(("w_up_r", "w_up_m")):
            P.dma("pool", S_up[t].rearrange("p (cc c) -> p cc c", cc=4), I[nm].rearrange("(cc p) c -> p cc c", p=128),
                  writes=[DB["s_up"]], sem_buf=DB["s_up"])
        P.dma("pool", wmq[:], I["w_mq"].rearrange("h d e -> d h e"), writes=[wmq], sem_buf=wmq)
        P.dma("pool", wmk[:], I["w_mk"].rearrange("h d e -> d h e"), writes=[wmk], sem_buf=wmk)
        wple = P.sbuf("wple", [128, 2, D], BF16)
        P.dma("pool", wple[:], I["w_ple"].rearrange("(kc p) c -> p kc c", p=128), writes=[wple], sem_buf=wple)
        for r0 in range(0, NE, 2048):
            P.dma("pool", S_v[r0:r0 + 2048, :], I["peer_v"][r0:r0 + 2048, :], writes=[DB["s_v"]], sem_buf=DB["s_v"])
        keysT = P.sbuf("keysT", [128, 16, 128], BF16)
        kraw = A("kraw", [128, 16, 128], BF16)
        P.dma("pool", kraw[:], I["peer_keys"].rearrange("g k d -> k g d"), writes=[kraw], sem_buf=kraw)
        for g4 in range(2):
            for g in range(8):
                P.op("pe", TR(psb16(g4)[:, g * 128:(g + 1) * 128], kraw[:, g4 * 8 + g, :], identb[:]), reads=[kraw, identb], writes=[PB[g4]])
            P.op("dve", CP(keysT[:, g4 * 8:(g4 + 1) * 8, :], psb16(g4).rearrange("p (g k) -> p g k", g=8)), reads=[PB[g4]], writes=[keysT])
        ustg = [A("ustg%d" % i, [128, D], F32) for i in range(2)]
        for qd in range(NEC // 4):
            pb_ = stgb[sti[0] % 2]
            sti[0] += 1
            ov = pb_[:, 0:4096].rearrange("p (g kc e) -> p g kc e", g=4, kc=8)
            for g in range(4):
                ec = qd * 4 + g
                us = ustg[ec % 2]
                P.dma("sp", us[:], I["peer_u"][ec * 128:(ec + 1) * 128, :], writes=[us], sem_buf=us)
                for hf in range(2):
                    bank = 2 + (ec % 2) * 2 + hf
                    for k4 in range(4):
                        kc = hf * 4 + k4
                        P.op("pe", TR(ps(bank)[:, k4 * 128:(k4 + 1) * 128], us[:, kc * 128:(kc + 1) * 128], ident[:]),
                             reads=[us, ident], writes=[PB[bank]])
                    for k4 in range(4):
                        kc = hf * 4 + k4
                        eng = "dve" if k4 % 2 == 0 else "act"
                        if eng == "dve":
                            P.op("dve", TS(ov[:, g, kc, :], ps(bank)[:, k4 * 128:(k4 + 1) * 128], gffn_c[:, kc:kc + 1], ALU.mult),
                                 reads=[PB[bank], gffn_c], writes=[pb_])
                        else:
                            P.op("act", ACT(ov[:, g, kc, :], ps(bank)[:, k4 * 128:(k4 + 1) * 128], AF.Copy, scale=gffn_c[:, kc:kc + 1]),
                                 reads=[PB[bank], gffn_c], writes=[pb_])
            P.dma("sp", S_ut[qd * 4:(qd + 1) * 4].rearrange("g p x -> p g x"), pb_[:, 0:4096].rearrange("p (g x) -> p g x", g=4),
                  reads=[pb_], writes=[DB["s_ut"]], sem_buf=pb_)

        def rms_rstd(xtile, n):
            P.op("act", ACT(junk[:n, :], xtile[:n, :], AF.Square, accum_out=st1[:n, 0:1]), reads=[xtile], writes=[junk, st1])
            P.op("dve", TS(st1[:n, 1:2], st1[:n, 0:1], 1.0 / D, ALU.mult, EPS, ALU.add), reads=[st1], writes=[st1])
            P.op("act", ACT(st1[:n, 2:3], st1[:n, 1:2], AF.Sqrt), reads=[st1], writes=[st1])
            P.op("dve", lambda e: e.reciprocal(out=rstd[:n, :], in_=st1[:n, 2:3]), reads=[st1], writes=[rstd])

        def norm_T(xtile, n, dstT, off, bank):
            rms_rstd(xtile, n)
            P.op("act", ACT(hb[:n, :], xtile[:n, :], AF.Copy, scale=rstd[:n, 0:1]), reads=[xtile, rstd], writes=[hb])
            for kc in range(8):
                P.op("pe", TR(psb16(bank)[:, kc * 128:kc * 128 + n], hb[:n, kc * 128:(kc + 1) * 128], identb[:n, :n]),
                     reads=[hb, identb], writes=[PB[bank]])
            src = psb16(bank).rearrange("p (kc t) -> p kc t", kc=8)[:, :, 0:n]
            P.op("dve", CP(dstT[:, :, off:off + n], src), reads=[PB[bank]], writes=[dstT])

        def mixer_block(tiles, full):
            NT = len(tiles)
            offs = []
            o = 0
            for t in tiles:
                offs.append(o)
                o += t["n"]
            NTOT = o
            for j, t in enumerate(tiles):
                n = t["n"]
                P.dma("sp", cs_t[:, offs[j]:offs[j] + n], t["cs"], writes=[cs_t], sem_buf=cs_t)
                P.dma("sp", sn_t[:, offs[j]:offs[j] + n], t["sn"], writes=[sn_t], sem_buf=sn_t)
            for j, t in enumerate(tiles):
                n = t["n"]
                P.dma("sp", xt[j][:n, :], t["x"], reads=[DB[t["xkey"]]], writes=[xt[j]], sem_buf=xt[j])
                norm_T(xt[j], n, hT, offs[j], 0)
            quads = list(range(NFM // 4)) if full else [2, 3, 5]
            items = [(S_fm[qd], 4096, DB["s_fm"]) for qd in quads]
            fmbank = [1, 2]
            cnt = [0]

            def fm_group(g, panel):
                gi = g % 4
                bank = fmbank[cnt[0] % 2]
                cnt[0] += 1
                pv = panel[:, 0:4096].rearrange("p (g kc c) -> p g kc c", g=4, kc=8)
                for kc in range(8):
                    P.op("pe", MM(ps(bank)[:, 0:NTOT], pv[:, gi, kc, :], hT[:, kc, 0:NTOT], kc == 0, kc == 7),
                         reads=[panel, hT], writes=[PB[bank]])
                return bank

            def do_quad(i, panel):
                qd = quads[i]
                for gi in range(4):
                    g = qd * 4 + gi
                    bank = fm_group(g, panel)
                    src = ps(bank)[:, 0:NTOT]
                    if g < FM_GT:
                        which = g // 4
                        h = g % 4
                        if which in (0, 2):
                            P.op("dve", TT(ropeC[which // 2][:, h, 0:NTOT], src, cs_t[:, 0:NTOT], ALU.mult),
                                 reads=[PB[bank], cs_t], writes=[ropeC[which // 2]])
                        else:
                            dst = qT if which == 1 else kT
                            P.op("dve", TT(rp_t2[:, 0:NTOT], src, sn_t[:, 0:NTOT], ALU.mult), reads=[PB[bank], sn_t], writes=[rp_t2])
                            P.op("pool", TT(dst[:, h, 0:NTOT], rp_t2[:, 0:NTOT], ropeC[which // 2][:, h, 0:NTOT], ALU.add),
                                 reads=[rp_t2, ropeC[which // 2]], writes=[dst])
                    elif g < FM_XM:
                        P.op("act", ACT(sgt[:, g - FM_GT, 0:NTOT], src, AF.Silu), reads=[PB[bank]], writes=[sgt])
                    elif g < FM_OM:
                        cc = g - FM_XM
                        for j, t in enumerate(tiles):
                            n = t["n"]
                            P.op("act", ACT(xp[j][:, cc, 3:3 + n], ps(bank)[:, offs[j]:offs[j] + n], AF.Copy), reads=[PB[bank]], writes=[xp[j]])
                    elif g < FM_GR:
                        P.op("act", ACT(sgo[:, g - FM_OM, 0:NTOT], src, AF.Sigmoid), reads=[PB[bank]], writes=[sgo])
                    elif g < FM_GM:
                        P.op("act", ACT(sgr[:, g - FM_GR, 0:NTOT], src, AF.Sigmoid), reads=[PB[bank]], writes=[sgr])
                    else:
                        P.op("act", ACT(sgm[:, g - FM_GM, 0:NTOT], src, AF.Sigmoid), reads=[PB[bank]], writes=[sgm])

            stream(ring, items, do_quad, L=2)
            tm_items = [(S_tm[t], 4096, DB["s_tm"]) for t in ((0, 1) if full else (0, 1))]

            def do_tm(i, panel):
                pv = panel[:, 0:4096].rearrange("p (kc c) -> p kc c", kc=8)
                for j, t in enumerate(tiles):
                    n = t["n"]
                    bank = 3 + j
                    for kc in range(8):
                        P.op("pe", MM(ps(bank)[:n, :], hT[:, kc, offs[j]:offs[j] + n], pv[:, kc, :], kc == 0, kc == 7),
                             reads=[panel, hT], writes=[PB[bank]])
                    src = ps(bank)[:n, :].rearrange("p (h e) -> p h e", h=4)
                    if i == 0:
                        P.op("act", ACT(vr[j][:n], src, AF.Copy), reads=[PB[bank]], writes=[vr[j]])
                    else:
                        P.op("act", ACT(v1[j][:n, :, 0:128], src, AF.Copy), reads=[PB[bank]], writes=[v1[j]])

            stream(ring, tm_items, do_tm, L=2)
            for j, t in enumerate(tiles):
                n = t["n"]
                bank = 5
                for kc in range(8):
                    P.op("pe", MM(ps(bank)[:n, 0:8], hT[:, kc, offs[j]:offs[j] + n], wif[:, kc, :], kc == 0, kc == 7),
                         reads=[wif, hT], writes=[PB[bank]])
                P.op("dve", TT(gpre[j][:n, :], ps(bank)[:n, 0:8], bif_b[:n, :], ALU.add), reads=[PB[bank], bif_b], writes=[gpre[j]])
            for j, t in enumerate(tiles):
                n = t["n"]
                if "load_conv" in t:
                    t["load_conv"]()
                P.op("pool", CP(xp[j][:, :, 0:3], xmh[:]), reads=[xmh], writes=[xp[j]])
                for cc in range(4):
                    P.op("dve", TS(cva[:, cc, 0:n], xp[j][:, cc, 3:3 + n], convw_c[:, 3, cc:cc + 1], ALU.mult, convb_c[:, cc:cc + 1], ALU.add),
                         reads=[xp[j], convw_c, convb_c], writes=[cva])
                    for tap in range(3):
                        P.op("dve", STT(cva[:, cc, 0:n], xp[j][:, cc, tap:tap + n], convw_c[:, tap, cc:cc + 1], cva[:, cc, 0:n], ALU.mult, ALU.add),
                             reads=[xp[j], convw_c, cva], writes=[cva])
                P.op("pool", CP(xmh[:], xp[j][:, :, n:n + 3]), reads=[xp[j]], writes=[xmh])
                if "store_conv" in t:
                    t["store_conv"]()
                P.op("act", ACT(cT[:, :, offs[j]:offs[j] + n], cva[:, :, 0:n], AF.Silu), reads=[cva], writes=[cT])
            for h in range(4):
                if full:
                    P.op("pe", MM(ps(1)[:, 0:NTOT], wmq[:, h, :], cT[:, h, 0:NTOT]), reads=[wmq, cT], writes=[PB[1]])
                    P.op("act", ACT(qmT[:, h, 0:NTOT], ps(1)[:, 0:NTOT], AF.Copy), reads=[PB[1]], writes=[qmT])
                    P.op("pe", MM(ps(2)[:, 0:NTOT], wmk[:, h, :], cT[:, h, 0:NTOT]), reads=[wmk, cT], writes=[PB[2]])
                    P.op("act", ACT(kmT[:, h, 0:NTOT], ps(2)[:, 0:NTOT], AF.Copy, scale=ISQ), reads=[PB[2]], writes=[kmT])
            for j, t in enumerate(tiles):
                core_tile(t, j, offs[j], full)
            return offs, NTOT


        def bc3(ap2, n, k):
            return ap2.unsqueeze(2).to_broadcast([n, 4, k])

        def core_tile(t, j, off, full):
            n = t["n"]
            if "load_state" in t:
                t["load_state"]()
            G = gs
            gp = gpre[j]
            P.op("act", ACT(G["e1"][:n], gp[:n, 4:8], AF.Exp, scale=-1.0), reads=[gp], writes=[G["e1"]])
            P.op("act", ACT(G["sp"][:n], G["e1"][:n], AF.Ln, bias=1.0), reads=[G["e1"]], writes=[G["sp"]])
            P.op("pe", MM(ps(5)[:n, 0:4], tri[:n, :n], G["sp"][:n]), reads=[tri, G["sp"]], writes=[PB[5]])
            P.op("dve", TT(G["g"][:n], gp[:n, 0:4], ps(5)[:n, 0:4], ALU.add), reads=[gp, PB[5]], writes=[G["g"]])
            idb_ = ident[:n, :n].unsqueeze(1).to_broadcast([n, 4, n])
            P.op("dve", TT(diag[:n, :, :n], idb_, bc3(G["g"][:n], n, n), ALU.mult), reads=[ident, G["g"]], writes=[diag])
            negUv = negU[:n].rearrange("p (h s) -> p h s", h=4)[:, :, :n]
            negLv = negL[:n].rearrange("p (h s) -> p h s", h=4)[:, :, :n]
            gview = ps(6)[:n, :].rearrange("p (h s) -> p h s", h=4)[:, :, :n]
            for h in range(4):
                P.op("pe", MM(gview[:, h, :], ones[:n, :n], diag[:n, h, :n], True, False), reads=[ones, diag], writes=[PB[6]])
                P.op("pe", MM(gview[:, h, :], ident[:n, :n], negUv[:, h, :], False, True), reads=[ident, negU], writes=[PB[6]])
            P.op("dve", RED(G["M"][:n], gview, ALU.max), reads=[PB[6]], writes=[G["M"]])
            P.op("dve", TT(G["mm"][:n], G["M"][:n], mprev[:n], ALU.max), reads=[G["M"], mprev], writes=[G["mm"]])
            P.op("dve", TS(G["nmm"][:n], G["mm"][:n], -1.0, ALU.mult), reads=[G["mm"]], writes=[G["nmm"]])
            P.op("dve", TT(mtmm[:n, 0:4], G["mm"][:n], ps(5)[:n, 0:4], ALU.subtract), reads=[G["mm"], PB[5]], writes=[mtmm])
            P.op("pool", CP(mtmm[:n, 4:8], G["mm"][:n]), reads=[G["mm"]], writes=[mtmm])
            P.op("pe", MM(ps(5)[:, 8:16], sel[n][:n, :], mtmm[:n, :]), reads=[sel[n], mtmm], writes=[PB[5]])
            P.op("dve", CP(lastb[:], ps(5)[:, 8:16]), reads=[PB[5]], writes=[lastb])
            P.op("dve", TT(G["wend"][:n], G["g"][:n], lastb[:n, 4:8], ALU.subtract), reads=[G["g"], lastb], writes=[G["wend"]])
            P.op("act", ACT(G["wend"][:n], G["wend"][:n], AF.Exp), reads=[G["wend"]], writes=[G["wend"]])
            P.op("dve", TT(aend[:], mprev[:], lastb[:, 4:8], ALU.subtract), reads=[mprev, lastb], writes=[aend])
            P.op("act", ACT(aend[:], aend[:], AF.Exp), reads=[aend], writes=[aend])
            if full:
                P.op("dve", TT(G["a"][:n], mprev[:n], G["mm"][:n], ALU.subtract), reads=[mprev, G["mm"]], writes=[G["a"]])
                P.op("act", ACT(G["a"][:n], G["a"][:n], AF.Exp), reads=[G["a"]], writes=[G["a"]])
                P.op("act", ACT(G["enm"][:n], mtmm[:n, 0:4], AF.Exp, scale=-1.0), reads=[mtmm], writes=[G["enm"]])
                P.op("dve", TT(diag[:n, :, :n], idb_, bc3(G["nmm"][:n], n, n), ALU.mult), reads=[ident, G["nmm"]], writes=[diag])
                dview = ps(7)[:n, :].rearrange("p (h s) -> p h s", h=4)[:, :, :n]
                for h in range(4):
                    P.op("pe", MM(dview[:, h, :], ones[:n, :n], diag[:n, h, :n], True, False), reads=[ones, diag], writes=[PB[7]])
                    P.op("pe", MM(dview[:, h, :], ident[:n, :n], negLv[:, h, :], False, True), reads=[ident, negL], writes=[PB[7]])
                for h in range(4):
                    P.op("act", ACT(DT[:n, h, :n], dview[:, h, :], AF.Exp, bias=G["g"][:n, h:h + 1]), reads=[PB[7], G["g"]], writes=[DT])
            P.op("pool", CP(mprev[:], lastb[:, 0:4]), reads=[lastb], writes=[mprev])
            cs = slice(off, off + n)
            for h in range(4):
                P.op("pe", TR(psb16(6)[:n, h * 128:(h + 1) * 128], kT[:, h, cs], identb[:]), reads=[kT, identb], writes=[PB[6]])
            P.op("dve", TT(kwr[:n], psb16(6)[:n, 0:512].rearrange("p (h d) -> p h d", h=4), bc3(wend[n][:n], n, 128), ALU.mult),
                 reads=[PB[6], wend[n]], writes=[kwr])
            for h in range(4):
                P.op("pe", MM(ps(7)[:n, h * 128:(h + 1) * 128], cT[:, h, cs], wmk[:, h, :]), reads=[cT, wmk], writes=[PB[7]])
            P.op("dve", STT(kwm[:n], ps(7)[:n, :].rearrange("p (h d) -> p h d", h=4), ISQ, bc3(G["wend"][:n], n, 128), ALU.mult, ALU.mult),
                 reads=[PB[7], G["wend"]], writes=[kwm])
            if full:
                scv = ps(5)[:n, :].rearrange("p (h s) -> p h s", h=4)[:, :, :n]
                for h in range(4):
                    P.op("pe", MM(scv[:, h, :], kT[:, h, cs], qT[:, h, cs]), reads=[kT, qT], writes=[PB[5]])
                dmv = dmT[:n].rearrange("p (h s) -> p h s", h=4)[:, :, :n]
                P.op("dve", TT(wTr[:n, :, :n], scv, dmv, ALU.mult), reads=[PB[5], dmT], writes=[wTr])
                for h in range(4):
                    P.op("pe", MM(ps(6)[:n, h * 128:(h + 1) * 128], wTr[:n, h, :n], vr[j][:n, h, :]), reads=[wTr, vr[j]], writes=[PB[6]])
                for h in range(4):
                    P.op("pe", MM(ps(7)[:n, h * 128:(h + 1) * 128], qT[:, h, cs], Sbf[:, h, :]), reads=[qT, Sbf], writes=[PB[7]])
                o1 = otmp[:n, :, 0:128]
                o2 = otmp2[:n, :, 0:128]
                P.op("act", ACT(o1, ps(6)[:n, :].rearrange("p (h e) -> p h e", h=4), AF.Copy), reads=[PB[6]], writes=[otmp])
                P.op("dve", TT(o2, ps(7)[:n, :].rearrange("p (h e) -> p h e", h=4), bc3(wread[:n], n, 128), ALU.mult), reads=[PB[7], wread], writes=[otmp2])
                P.op("pool", TT(o1, o1, o2, ALU.add), reads=[otmp, otmp2], writes=[otmp])
                P.op("dve", TT(sqt[:n], o1, o1, ALU.mult), reads=[otmp], writes=[sqt])
                P.op("dve", RED(G["ssr"][:n], sqt[:n], ALU.add), reads=[sqt], writes=[G["ssr"]])
                P.op("dve", TS(G["ssr"][:n], G["ssr"][:n], 1.0 / 128, ALU.mult, EPS, ALU.add), reads=[G["ssr"]], writes=[G["ssr"]])
                P.op("act", ACT(G["ssr"][:n], G["ssr"][:n], AF.Sqrt), reads=[G["ssr"]], writes=[G["ssr"]])
                P.op("dve", lambda e: e.reciprocal(out=G["rsr"][:n], in_=G["ssr"][:n]), reads=[G["ssr"]], writes=[G["rsr"]])
                P.op("dve", TT(onb[:n], o1, bc3(G["rsr"][:n], n, 128), ALU.mult), reads=[otmp, G["rsr"]], writes=[onb])
                for h in range(4):
                    P.op("pe", TR(psb16(5)[:, h * 128:h * 128 + n], onb[:n, h, :], identb[:n, :n]), reads=[onb, identb], writes=[PB[5]])
                tv = psb16(5)[:, 0:512].rearrange("p (h t) -> p h t", h=4)[:, :, 0:n]
                P.op("dve", TT(ytmp[:, :, 0:n], tv, bc3(gret_c[:], 128, n), ALU.mult), reads=[PB[5], gret_c], writes=[ytmp])
                P.op("pool", TT(yrT[:, :, cs], ytmp[:, :, 0:n], sgt[:, :, cs], ALU.mult), reads=[ytmp, sgt], writes=[yrT])
                scm = ps(6)[:n, :].rearrange("p (h s) -> p h s", h=4)[:, :, :n]
                for h in range(4):
                    P.op("pe", MM(scm[:, h, :], kmT[:, h, cs], qmT[:, h, cs]), reads=[kmT, qmT], writes=[PB[6]])
                P.op("dve", TT(wTm[:n, :, :n], scm, DT[:n, :, :n], ALU.mult), reads=[PB[6], DT], writes=[wTm])
                ndA = psum_all[:n, 0:1024].rearrange("p (h e) -> p h e", h=4)[:, :, 0:129]
                ndB = psum_all[:n, 1024:2048].rearrange("p (h e) -> p h e", h=4)[:, :, 0:129]
                for h in range(4):
                    P.op("pe", MM(ndA[:, h, :], wTm[:n, h, :n], v1[j][:n, h, :]), reads=[wTm, v1[j]], writes=[PB[0], PB[1]])
                for h in range(4):
                    P.op("pe", MM(ndB[:, h, :], qmT[:, h, cs], Cnbf[:, h, :]), reads=[qmT, Cnbf], writes=[PB[2], PB[3]])
                P.op("act", ACT(otmp[:n], ndA, AF.Copy), reads=[PB[0], PB[1]], writes=[otmp])
                P.op("dve", TT(otmp2[:n], ndB, bc3(G["a"][:n], n, 129), ALU.mult), reads=[PB[2], PB[3], G["a"]], writes=[otmp2])
                P.op("pool", TT(otmp[:n], otmp[:n], otmp2[:n], ALU.add), reads=[otmp, otmp2], writes=[otmp])
                num = otmp[:n, :, 0:128]
                P.op("dve", STT(G["den"][:n], otmp[:n, :, 128], -1.0, otmp[:n, :, 128], ALU.mult, ALU.max), reads=[otmp], writes=[G["den"]])
                P.op("dve", TT(G["den"][:n], G["den"][:n], G["enm"][:n], ALU.max), reads=[G["den"], G["enm"]], writes=[G["den"]])
                P.op("dve", TT(G["d2"][:n], G["den"][:n], G["den"][:n], ALU.mult), reads=[G["den"]], writes=[G["d2"]])
                P.op("dve", TT(sqt[:n], num, num, ALU.mult), reads=[otmp], writes=[sqt])
                P.op("dve", RED(G["ss"][:n], sqt[:n], ALU.add), reads=[sqt], writes=[G["ss"]])
                P.op("dve", TS(G["ss"][:n], G["ss"][:n], 1.0 / 128, ALU.mult), reads=[G["ss"]], writes=[G["ss"]])
                P.op("dve", STT(G["ss"][:n], G["d2"][:n], EPS, G["ss"][:n], ALU.mult, ALU.add), reads=[G["d2"], G["ss"]], writes=[G["ss"]])
                P.op("act", ACT(G["ss"][:n], G["ss"][:n], AF.Sqrt), reads=[G["ss"]], writes=[G["ss"]])
                P.op("dve", lambda e: e.reciprocal(out=G["rs"][:n], in_=G["ss"][:n]), reads=[G["ss"]], writes=[G["rs"]])
                P.op("dve", TT(hnb[:n], num, bc3(G["rs"][:n], n, 128), ALU.mult), reads=[otmp, G["rs"]], writes=[hnb])
                for h in range(4):
                    P.op("pe", TR(psb16(7)[:, h * 128:h * 128 + n], hnb[:n, h, :], identb[:n, :n]), reads=[hnb, identb], writes=[PB[7]])
                tv = psb16(7)[:, 0:512].rearrange("p (h t) -> p h t", h=4)[:, :, 0:n]
                P.op("dve", TT(ytmp[:, :, 0:n], tv, bc3(gml_c[:], 128, n), ALU.mult), reads=[PB[7], gml_c], writes=[ytmp])
                P.op("pool", TT(ytmp2[:, :, 0:n], cT[:, :, cs], bc3(wskip_c[:], 128, n), ALU.mult), reads=[cT, wskip_c], writes=[ytmp2])
                P.op("pool", TT(ytmp[:, :, 0:n], ytmp[:, :, 0:n], ytmp2[:, :, 0:n], ALU.add), reads=[ytmp, ytmp2], writes=[ytmp])
                P.op("pool", TT(ymT[:, :, cs], ytmp[:, :, 0:n], sgo[:, :, cs], ALU.mult), reads=[ytmp, sgo], writes=[ymT])
            for h in range(4):
                P.op("pe", MM(ps(4)[:, h * 128:(h + 1) * 128], kwr[:n, h, :], vr[j][:n, h, :]), reads=[kwr, vr[j]], writes=[PB[4]])
            for h in range(4):
                P.op("dve", STT(S32[:, h, :], S32[:, h, :], GAM[h] ** n, ps(4)[:, h * 128:(h + 1) * 128], ALU.mult, ALU.add),
                     reads=[S32, PB[4]], writes=[S32])
            P.op("act", ACT(Sbf[:], S32[:], AF.Copy), reads=[S32], writes=[Sbf])
            cuA = psum_all[:, 0:1024].rearrange("p (h e) -> p h e", h=4)[:, :, 0:129]
            for h in range(4):
                P.op("pe", MM(cuA[:, h, :], kwm[:n, h, :], v1[j][:n, h, :]), reads=[kwm, v1[j]], writes=[PB[0], PB[1]])
            P.op("pool", TT(Cn32[:], Cn32[:], bc3(aend[:], 128, 129), ALU.mult), reads=[Cn32, aend], writes=[Cn32])
            P.op("dve", TT(Cn32[:], Cn32[:], cuA, ALU.add), reads=[Cn32, PB[0], PB[1]], writes=[Cn32])
            P.op("act", ACT(Cnbf[:], Cn32[:], AF.Copy), reads=[Cn32], writes=[Cnbf])
            if "store_state" in t:
                t["store_state"]()

        def zero_state():
            P.op("pool", MSET(S32[:], 0.0), writes=[S32])
            P.op("pool", MSET(Sbf[:], 0.0), writes=[Sbf])
            P.op("pool", MSET(Cn32[:], 0.0), writes=[Cn32])
            P.op("pool", MSET(Cnbf[:], 0.0), writes=[Cnbf])
            P.op("pool", MSET(mprev[:], 0.0), writes=[mprev])
            P.op("pool", MSET(xmh[:], 0.0), writes=[xmh])

        def flag_state():
            fl = flag[:, 0:1]
            P.op("dve", TS(S32[:], S32[:], fl, ALU.mult), reads=[S32, flag], writes=[S32])
            P.op("dve", TS(Cn32[:], Cn32[:], fl, ALU.mult), reads=[Cn32, flag], writes=[Cn32])
            P.op("dve", TS(mprev[:], mprev[:], fl, ALU.mult), reads=[mprev, flag], writes=[mprev])
            P.op("dve", TS(xmh[:], xmh[:], fl, ALU.mult), reads=[xmh, flag], writes=[xmh])
            P.op("act", ACT(Sbf[:], S32[:], AF.Copy), reads=[S32], writes=[Sbf])
            P.op("act", ACT(Cnbf[:], Cn32[:], AF.Copy), reads=[Cn32], writes=[Cnbf])

        def store_state(o_ret, o_C, o_n, o_m, keys):
            P.dma("sp", o_ret.rearrange("h d e -> d h e"), S32[:], reads=[S32], writes=[DB[keys[0]]], sem_buf=S32)
            P.dma("sp", o_C.rearrange("h d e -> d h e"), Cn32[:, :, 0:128], reads=[Cn32], writes=[DB[keys[1]]], sem_buf=Cn32)
            P.dma("sp", o_n.rearrange("h d -> d h"), Cn32[:, :, 128], reads=[Cn32], writes=[DB[keys[2]]], sem_buf=Cn32, allow_slow_non_contiguous=True)
            P.dma("sp", o_m, mprev[0:1, :], reads=[mprev], writes=[DB[keys[3]]], sem_buf=mprev)

        def store_conv(o_conv, key):
            for cc in range(4):
                P.dma("sp", o_conv[:, cc * 128:(cc + 1) * 128].rearrange("j p -> p j"), xmh[:, cc, :], reads=[xmh], writes=[DB[key]],
                      sem_buf=xmh, allow_slow_non_contiguous=True)

        def load_state(s):
            P.dma("sp", S32[:], I["st_ret"][s].rearrange("h d e -> d h e"), writes=[S32], sem_buf=S32)
            P.dma("sp", Cn32[:, :, 0:128], I["st_C"][s].rearrange("h d e -> d h e"), writes=[Cn32], sem_buf=Cn32)
            P.dma("sp", Cn32[:, :, 128], I["st_n"][s].rearrange("h d -> d h"), writes=[Cn32], sem_buf=Cn32, allow_slow_non_contiguous=True)
            P.dma("sp", mprev[:], I["st_m"][s].partition_broadcast(128), writes=[mprev], sem_buf=mprev)
            P.op("act", ACT(Sbf[:], S32[:], AF.Copy), reads=[S32], writes=[Sbf])
            P.op("act", ACT(Cnbf[:], Cn32[:], AF.Copy), reads=[Cn32], writes=[Cnbf])

        def load_conv(s):
            for cc in range(4):
                P.dma("sp", xmh[:, cc, :], I["st_conv"][s][:, cc * 128:(cc + 1) * 128].rearrange("j p -> p j"), writes=[xmh],
                      sem_buf=xmh, allow_slow_non_contiguous=True)

        def merge_block(tiles, offs, NTOT):
            up = [ring.load(S_up[t], 4096, DB["s_up"]) for t in range(2)]
            upv = [u[:, 0:4096].rearrange("p (cc c) -> p cc c", cc=4) for u in up]
            for dmc in range(8):
                for t, yT in enumerate((yrT, ymT)):
                    bank = 1 + t
                    for cc in range(4):
                        P.op("pe", MM(ps(bank)[:, 0:NTOT], upv[t][:, cc, dmc * 128:(dmc + 1) * 128], yT[:, cc, 0:NTOT], cc == 0, cc == 3),
                             reads=[up[t], yT], writes=[PB[bank]])
                P.op("dve", TT(mg1[:, 0:NTOT], ps(1)[:, 0:NTOT], sgr[:, dmc, 0:NTOT], ALU.mult), reads=[PB[1], sgr], writes=[mg1])
                P.op("dve", TT(mg2[:, 0:NTOT], ps(2)[:, 0:NTOT], sgm[:, dmc, 0:NTOT], ALU.mult), reads=[PB[2], sgm], writes=[mg2])
                P.op("pool", TT(mgT[:, dmc, 0:NTOT], mg1[:, 0:NTOT], mg2[:, 0:NTOT], ALU.add), reads=[mg1, mg2], writes=[mgT])
            wo = [ring.load(S_out[t], 4096, DB["s_out"]) for t in range(2)]
            for j, t in enumerate(tiles):
                n = t["n"]
                for hf in range(2):
                    bank = 3 + hf
                    wv = wo[hf][:, 0:4096].rearrange("p (kc c) -> p kc c", kc=8)
                    for kc in range(8):
                        P.op("pe", MM(ps(bank)[:n, :], mgT[:, kc, offs[j]:offs[j] + n], wv[:, kc, :], kc == 0, kc == 7),
                             reads=[wo[hf], mgT], writes=[PB[bank]])
                    xs = xt[j][:n, hf * 512:(hf + 1) * 512]
                    P.op("dve", TT(xs, xs, ps(bank)[:n, :], ALU.add), reads=[xt[j], PB[bank]], writes=[xt[j]])

        xnT = hT
        arena_reset()
        qpT = A("qpT", [128, 16, NB], BF16)
        sc_s = [A("sc_s%d" % j, [128, 16, 128], F32) for j in range(2)]
        nb1 = [A("nb1_%d" % j, [128, 8, 128], F32) for j in range(2)]
        E1 = [A("E1_%d" % j, [128, 8, 128], F32) for j in range(2)]
        E2 = [A("E2_%d" % j, [128, 8, 128], BF16) for j in range(2)]
        tk = A("tk", [128, 16, 16], F32)
        tkw = A("tkw", [128, 128], F32)
        cand = A("cand", [128, 8, 256], F32)
        candw = A("candw", [128, 256], F32)
        ctop = A("ctop", [128, 8, 24], F32)
        pst = {k: A("pst_" + k, [128, 8], F32) for k in ("th", "Z", "nm1", "nm2", "rz", "lz")}
        cexp = A("cexp", [128, 8, 16], F32)
        biasx = [A("biasx%d" % j, [128, 8], F32) for j in range(2)]
        Dbuf = [Buf("Dbuf%d" % i, cand.ap[:, 2 * i:2 * i + 2, :].rearrange("p a b -> p (a b)")) for i in range(4)]
        IG = 4
        NG = NEC // IG
        Wn = [[[A("Wn%d_%d_%d" % (b, j, h_), [128, IG * 128], BF16) for h_ in range(8)] for j in range(2)] for b in range(2)]
        NMK = 16
        Mk = [A("Mk%d" % i, [128, IG * 128], BF16) for i in range(NMK)]
        Gt = [A("Gt%d" % i, [128, NB], BF16) for i in range(2)]
        WH = [A("WH%d" % i, [128, NB], BF16) for i in range(2)]
        uring = Ring(P, "ur", 4, elems=1024, alloc=A)
        vring = Ring(P, "vr_", 4, elems=1024, alloc=A)
        pt = A("pt", [128, 256], BF16)
        pT = A("pT", [128, 2, 128], BF16)
        pgs = A("pgs", [128, 512], F32)
        yo = [A("yo%d" % j, [128, D], F32) for j in range(1)]
        print("[kernel] arena peer bytes", aoff[0], flush=True)

        def peer_prep_tile(t, j, off):
            n = t["n"]
            for g4 in range(4):
                bank = 4 + g4
                for g in range(4):
                    gg = g4 * 4 + g
                    P.op("pe", MM(ps(bank)[:n, g * 128:(g + 1) * 128], qpT[:, gg, off:off + n], keysT[:, gg, :]), reads=[qpT, keysT], writes=[PB[bank]])
                eng = "act" if g4 % 2 == 0 else "dve"
                dst = sc_s[j][:n, g4 * 4:(g4 + 1) * 4, :]
                src = ps(bank)[:n, :].rearrange("p (g k) -> p g k", g=4)
                if eng == "act":
                    P.op("act", ACT(dst, src, AF.Copy), reads=[PB[bank]], writes=[sc_s[j]])
                else:
                    P.op("dve", CP(dst, src), reads=[PB[bank]], writes=[sc_s[j]])
            S = sc_s[j]
            for g in range(16):
                P.op("dve", lambda e, g=g: e.max(out=tk[:n, g, 0:8], in_=S[:n, g, :]), reads=[S], writes=[tk])
                P.op("dve", lambda e, g=g: e.match_replace(out=tkw[:n, :], in_to_replace=tk[:n, g, 0:8], in_values=S[:n, g, :], imm_value=-1e30),
                     reads=[S, tk], writes=[tkw])
                P.op("dve", lambda e, g=g: e.max(out=tk[:n, g, 8:16], in_=tkw[:n, :]), reads=[tkw], writes=[tk])
            tkv = tk[:n].rearrange("p (hd hf) k -> p hd hf k", hf=2)
            for hd in range(8):
                a_ = tkv[:, hd, 0, :].unsqueeze(2).to_broadcast([n, 16, 16])
                b_ = tkv[:, hd, 1, :].unsqueeze(1).to_broadcast([n, 16, 16])
                P.op("dve", TT(cand[:n, hd, :].rearrange("p (a b) -> p a b", a=16), a_, b_, ALU.add), reads=[tk], writes=[cand])
            for hd in range(8):
                P.op("dve", lambda e, hd=hd: e.max(out=ctop[:n, hd, 0:8], in_=cand[:n, hd, :]), reads=[cand], writes=[ctop])
                P.op("dve", lambda e, hd=hd: e.match_replace(out=candw[:n, :], in_to_replace=ctop[:n, hd, 0:8], in_values=cand[:n, hd, :], imm_value=-1e30),
                     reads=[cand, ctop], writes=[candw])
                P.op("dve", lambda e, hd=hd: e.max(out=ctop[:n, hd, 8:16], in_=candw[:n, :]), reads=[candw], writes=[ctop])
                P.op("dve", lambda e, hd=hd: e.match_replace(out=candw[:n, :], in_to_replace=ctop[:n, hd, 8:16], in_values=candw[:n, :], imm_value=-1e30),
                     reads=[candw, ctop], writes=[candw])
                P.op("dve", lambda e, hd=hd: e.max(out=ctop[:n, hd, 16:24], in_=candw[:n, :]), reads=[candw], writes=[ctop])
            P.op("dve", TT(pst["th"][:n], ctop[:n, :, 15], ctop[:n, :, 16], ALU.add), reads=[ctop], writes=[pst["th"]])
            P.op("dve", TS(pst["th"][:n], pst["th"][:n], 0.5, ALU.mult), reads=[pst["th"]], writes=[pst["th"]])
            P.op("dve", TT(cexp[:n], ctop[:n, :, 0:16], ctop[:n, :, 0:1].to_broadcast([n, 8, 16]), ALU.subtract), reads=[ctop], writes=[cexp])
            P.op("act", ACT(cexp[:n], cexp[:n], AF.Exp), reads=[cexp], writes=[cexp])
            P.op("dve", RED(pst["Z"][:n], cexp[:n], ALU.add), reads=[cexp], writes=[pst["Z"]])
            P.op("act", ACT(pst["lz"][:n], pst["Z"][:n], AF.Ln), reads=[pst["Z"]], writes=[pst["lz"]])
            Sv = S[:n].rearrange("p (hd hf) k -> p hd hf k", hf=2)
            P.op("dve", TT(biasx[j][:n], pst["th"][:n], ctop[:n, :, 0], ALU.subtract), reads=[pst["th"], ctop], writes=[biasx[j]])
            P.op("dve", TT(biasx[j][:n], biasx[j][:n], pst["lz"][:n], ALU.subtract), reads=[biasx[j], pst["lz"]], writes=[biasx[j]])
            P.op("dve", TT(nb1[j][:n], pst["th"][:n].unsqueeze(2).to_broadcast([n, 8, 128]), Sv[:, :, 0, :], ALU.subtract),
                 reads=[pst["th"], S], writes=[nb1[j]])
            P.op("dve", TT(cand[:n, :, 0:128], Sv[:, :, 0, :], tkv[:, :, 0, 0:1].to_broadcast([n, 8, 128]), ALU.subtract), reads=[S, tk], writes=[cand])
            P.op("act", ACT(E1[j][:n], cand[:n, :, 0:128], AF.Exp), reads=[cand], writes=[E1[j]])
            P.op("dve", TT(pst["nm2"][:n], tkv[:, :, 1, 0], pst["lz"][:n], ALU.add), reads=[tk, pst["lz"]], writes=[pst["nm2"]])
            P.op("dve", TT(cand[:n, :, 128:256], Sv[:, :, 1, :], pst["nm2"][:n].unsqueeze(2).to_broadcast([n, 8, 128]), ALU.subtract),
                 reads=[S, pst["nm2"]], writes=[cand])
            P.op("act", ACT(E2[j][:n], cand[:n, :, 128:256], AF.Exp), reads=[cand], writes=[E2[j]])

        mkctr = [0]

        def mask_slices(tiles, grp, buf, nsl):
            items = [(j, t["n"], hd) for j, t in enumerate(tiles) for hd in range(8)]
            per = (len(items) + nsl - 1) // nsl
            i0 = grp * IG
            slices = []
            newk = []
            for si in range(nsl):
                chunk = [(kl, items[kl]) for kl in range(si * per, min((si + 1) * per, len(items)))]

                def f(chunk=chunk, last=(si == nsl - 1)):
                    for kl, (j, n, hd) in chunk:
                        mk = Mk[kl]
                        mkv = mk[:n, :].rearrange("p (a b) -> p a b", a=IG)
                        Sv = sc_s[j][:n].rearrange("p (hd hf) k -> p hd hf k", hf=2)
                        s2b = Sv[:, hd, 1, :].unsqueeze(1).to_broadcast([n, IG, 128])
                        thb = nb1[j][:n, hd, i0:i0 + IG].unsqueeze(2).to_broadcast([n, IG, 128])
                        wb__ = Wn[buf][j][hd]
                        if kl % 4 == 3:
                            P.op("dve", TT(mkv, s2b, thb, ALU.is_ge), reads=[sc_s[j], nb1[j]], writes=[mk])
                            e2b = E2[j][:n, hd, :].unsqueeze(1).to_broadcast([n, IG, 128])
                            P.op("dve", TT(mkv, mkv, e2b, ALU.mult), reads=[mk, E2[j]], writes=[mk])
                            e1b = E1[j][:n, hd, i0:i0 + IG].unsqueeze(2).to_broadcast([n, IG, 128])
                            P.op("dve", TT(wb__[:n, :].rearrange("p (a b) -> p a b", a=IG), mkv, e1b, ALU.mult), reads=[mk, E1[j]], writes=[wb__])
                        else:
                            d_ = Dbuf[mkctr[0] % 4]
                            mkctr[0] += 1
                            P.op("dve", TT(d_[:n, :].rearrange("p (a b) -> p a b", a=IG), s2b, thb, ALU.subtract), reads=[sc_s[j], nb1[j]], writes=[d_])
                            P.op("act", ACT(mk[:n, :], d_[:n, :], AF.Prelu, alpha=1e30), reads=[d_], writes=[mk])
                            newk.append((kl, j, n, hd))
                    if last:
                        for kl, j, n, hd in newk:
                            P.op("act", ACT(Wn[buf][j][hd][:n, :], Mk[kl][:n, :], AF.Exp, bias=biasx[j][:n, hd:hd + 1]),
                                 reads=[Mk[kl], biasx[j]], writes=[Wn[buf][j][hd]])
                slices.append(f)
            return slices

        def peer_block(tiles, offs, NTOT):
            for j, t in enumerate(tiles):
                norm_T(xt[j], t["n"], xnT, offs[j], 0)
            items = [(S_pq[qd], 4096, DB["s_pq"]) for qd in range(4)]

            def do_q(i, panel):
                pv = panel[:, 0:4096].rearrange("p (g kc c) -> p g kc c", g=4, kc=8)
                for gi in range(4):
                    bank = 1 + (gi % 2)
                    for kc in range(8):
                        P.op("pe", MM(ps(bank)[:, 0:NTOT], pv[:, gi, kc, :], xnT[:, kc, 0:NTOT], kc == 0, kc == 7), reads=[panel, xnT], writes=[PB[bank]])
                    P.op("act", ACT(qpT[:, i * 4 + gi, 0:NTOT], ps(bank)[:, 0:NTOT], AF.Copy), reads=[PB[bank]], writes=[qpT])

            stream(ring, items, do_q, L=2)
            for j, t in enumerate(tiles):
                peer_prep_tile(t, j, offs[j])
            for d_ in Dbuf:
                d_.w = cand.w
                d_.rs = dict(cand.rs)
            for f in mask_slices(tiles, 0, 0, IG - 1):
                f()
            ybank = [[0, 1], [2, 3]]
            upend = {}
            vpend = {}

            def uload(ec):
                upend[ec] = uring.load(S_ut[ec], 1024, DB["s_ut"])

            def vload(ec):
                vpend[ec] = vring.load(S_v[ec * 128:(ec + 1) * 128, :], 1024, DB["s_v"])

            for e0 in range(min(3, NEC)):
                uload(e0)
                vload(e0)

            def emit_U(ec):
                up_ = upend.pop(ec)
                uv = up_[:, 0:1024].rearrange("p (kc e) -> p kc e", kc=8)
                hb_ = 4 + (ec % 2)
                for kc in range(8):
                    P.op("pe", MM(ps(hb_)[:, 0:NTOT], uv[:, kc, :], xnT[:, kc, 0:NTOT], kc == 0, kc == 7), reads=[up_, xnT], writes=[PB[hb_]])
                gt = Gt[ec % 2]
                P.op("act", ACT(gt[:, 0:NTOT], ps(hb_)[:, 0:NTOT], AF.Gelu), reads=[PB[hb_]], writes=[gt])

            def emit_T(ec):
                grp, gi = ec // IG, ec % IG
                wb_ = 6 + (ec % 2)
                for j, t in enumerate(tiles):
                    n = t["n"]
                    for hd in range(8):
                        P.op("pe", MM(ps(wb_)[:, offs[j]:offs[j] + n], Wn[grp % 2][j][hd][:n, gi * 128:(gi + 1) * 128], identb[:n, :n], hd == 0, hd == 7),
                             reads=[Wn[grp % 2][j][hd], identb], writes=[PB[wb_]])

            def emit_WH(ec):
                wb_ = 6 + (ec % 2)
                P.op("dve", TT(WH[ec % 2][:, 0:NTOT], ps(wb_)[:, 0:NTOT], Gt[ec % 2][:, 0:NTOT], ALU.mult), reads=[PB[wb_], Gt[ec % 2]], writes=[WH[ec % 2]])

            def emit_V(ec):
                vp = vpend.pop(ec)
                wh = WH[ec % 2]
                for j, t in enumerate(tiles):
                    n = t["n"]
                    for hf in range(2):
                        yb = ybank[j][hf]
                        P.op("pe", MM(ps(yb)[:n, :], wh[:, offs[j]:offs[j] + n], vp[:, hf * 512:(hf + 1) * 512], ec == 0, ec == NEC - 1),
                             reads=[wh, vp], writes=[PB[yb]])

            emit_U(0)
            emit_T(0)
            emit_WH(0)
            nxt = []
            for ec in range(NEC):
                grp, gi = ec // IG, ec % IG
                if gi == 0:
                    nxt = mask_slices(tiles, grp + 1, (grp + 1) % 2, IG - 1) if grp + 1 < NG else []
                if ec + 3 < NEC:
                    uload(ec + 3)
                    vload(ec + 3)
                if ec + 1 < NEC:
                    emit_U(ec + 1)
                if gi < len(nxt):
                    nxt[gi]()
                if ec + 1 < NEC:
                    emit_T(ec + 1)
                emit_V(ec)
                if ec + 1 < NEC:
                    emit_WH(ec + 1)
            for j, t in enumerate(tiles):
                n = t["n"]
                for hf in range(2):
                    xs = xt[j][:n, hf * 512:(hf + 1) * 512]
                    P.op("dve", TT(xs, xs, ps(ybank[j][hf])[:n, :], ALU.add), reads=[xt[j], PB[ybank[j][hf]]], writes=[xt[j]])


        def ple_block(tiles, offs, NTOT):
            for j, t in enumerate(tiles):
                norm_T(xt[j], t["n"], xnT, offs[j], 0)
            wg = [ring.load(S_pg[t], 4096, DB["s_pg"]) for t in range(2)]
            for j, t in enumerate(tiles):
                n = t["n"]
                P.dma("pool", pt[:n, :], t["p"], reads=[DB[t["pkey"]]], writes=[pt], sem_buf=pt)
                for kc in range(2):
                    P.op("pe", TR(psb16(1)[:, kc * 128:kc * 128 + n], pt[:n, kc * 128:(kc + 1) * 128], identb[:n, :n]), reads=[pt, identb], writes=[PB[1]])
                P.op("dve", CP(pT[:, :, 0:n], psb16(1)[:, 0:256].rearrange("p (kc t) -> p kc t", kc=2)[:, :, 0:n]), reads=[PB[1]], writes=[pT])
                for hf in range(2):
                    wv = wg[hf][:, 0:4096].rearrange("p (kc c) -> p kc c", kc=8)
                    for kc in range(8):
                        P.op("pe", MM(ps(2)[:n, :], xnT[:, kc, offs[j]:offs[j] + n], wv[:, kc, :], kc == 0, kc == 7), reads=[wg[hf], xnT], writes=[PB[2]])
                    P.op("act", ACT(pgs[:n, :], ps(2)[:n, :], AF.Sigmoid), reads=[PB[2]], writes=[pgs])
                    for kc in range(2):
                        P.op("pe", MM(ps(3)[:n, :], pT[:, kc, 0:n], wple[:, kc, hf * 512:(hf + 1) * 512], kc == 0, kc == 1), reads=[pT, wple], writes=[PB[3]])
                    P.op("dve", TT(pgs[:n, :], pgs[:n, :], ps(3)[:n, :], ALU.mult), reads=[pgs, PB[3]], writes=[pgs])
                    xs = xt[j][:n, hf * 512:(hf + 1) * 512]
                    P.op("pool", TT(xs, xs, pgs[:n, :], ALU.add), reads=[xt[j], pgs], writes=[xt[j]])
                rms_rstd(xt[j], n)
                P.op("dve", STT(yo[0][:n, :], xt[j][:n, :], rstd[:n, 0:1], gfin_b[:n, :], ALU.mult, ALU.mult), reads=[xt[j], rstd, gfin_b], writes=[yo[0]])
                P.dma("sp", t["y"], yo[0][:n, :], reads=[yo[0]], writes=[DB[t["ykey"]]], sem_buf=yo[0])

        P.barrier()
        zero_state()
        if NTP > 0:
            for b in range(NTP // 2):
                tiles = []
                for jj in range(2):
                    ti = b * 2 + jj
                    tiles.append(dict(n=128, x=I["x_pre"][ti * 128:(ti + 1) * 128, :], xkey="x_pre",
                                      cs=I["c_cs_pre"][:, ti * 128:(ti + 1) * 128], sn=I["c_sn_pre"][:, ti * 128:(ti + 1) * 128]))
                mixer_block(tiles, False)
            flag_state()
        for b in range(NTM // 2):
            tiles = []
            for jj in range(2):
                ti = b * 2 + jj
                rows = slice(ti * 128, (ti + 1) * 128)
                tiles.append(dict(n=128, x=I["x_main"][rows, :], xkey="x_main", p=I["p_main"][rows, :], pkey="p_main",
                                  y=O["y_main"][rows, :], ykey="y_main",
                                  cs=I["c_cs_main"][:, rows], sn=I["c_sn_main"][:, rows]))
            offs, NTOT = mixer_block(tiles, True)
            if b == NTM // 2 - 1:
                store_state(O["ret_p"], O["C_p"], O["n_p"], O["m_p"], ("ret_p", "C_p", "n_p", "m_p"))
                store_conv(O["conv_p"], "conv_p")
            merge_block(tiles, offs, NTOT)
            P.barrier()
            if stop_after == "mixer":
                for j, t in enumerate(tiles):
                    P.dma("sp", O["dbg"][(b * 2 + j) * 128:(b * 2 + j + 1) * 128, :], xt[j][:], reads=[xt[j]], writes=[DB["dbg"]], sem_buf=xt[j])
                continue
            peer_block(tiles, offs, NTOT)
            if stop_after == "peer":
                for j, t in enumerate(tiles):
                    P.dma("sp", O["dbg"][(b * 2 + j) * 128:(b * 2 + j + 1) * 128, :], xt[j][:], reads=[xt[j]], writes=[DB["dbg"]], sem_buf=xt[j])
                continue
            ple_block(tiles, offs, NTOT)
            P.barrier()
        if NS > 0 and not stop_after:
            tiles = []
            for s in range(NS):
                rows = slice(s * 32, (s + 1) * 32)
                tiles.append(dict(
                    n=32, x=I["x_smp"][rows, :], xkey="x_smp", p=I["p_smp"][rows, :], pkey="p_smp", y=O["y_smp"][rows, :], ykey="y_smp",
                    cs=I["c_cs_smp"], sn=I["c_sn_smp"],
                    load_state=(lambda s=s: load_state(s)), load_conv=(lambda s=s: load_conv(s)),
                    store_state=(lambda s=s: store_state(O["ret_s"][s], O["C_s"][s], O["n_s"][s], O["m_s"][s:s + 1, :], ("ret_s", "C_s", "n_s", "m_s"))),
                    store_conv=(lambda s=s: store_conv(O["conv_s"][s], "conv_s"))))
            offs, NTOT = mixer_block(tiles, True)
            merge_block(tiles, offs, NTOT)
            for s in range(1, NS):
                P.dma("sp", xt[0][s * 32:(s + 1) * 32, :], xt[s][0:32, :], reads=[xt[s]], writes=[xt[0]], sem_buf=xt[0])
            P.barrier()
            nn = NS * 32
            tiles2 = [dict(n=nn, p=I["p_smp"][0:nn, :], pkey="p_smp", y=O["y_smp"][0:nn, :], ykey="y_smp")]
            peer_block(tiles2, [0], nn)
            ple_block(tiles2, [0], nn)
        P.finish([DB[k] for k in O])
        P.replay()
        print("[kernel] sbuf bytes/partition:", P.sbuf_bytes, flush=True)
        print("[kernel] instructions:", P.ninst, {k: len(v) for k, v in P.streams.items()}, "sems:", P.nsem, flush=True)
    return nc


def _consts(pos_main, pos_pre, pos_smp):
    c = {}
    c["c_ident"] = np.eye(128, dtype=np.float32)
    c["c_ones"] = np.ones((128, 128), np.float32)
    pi = np.arange(128)[:, None]
    fi = np.arange(128)[None, :]
    c["c_tri"] = (pi <= fi).astype(np.float32)
    negU = np.where(fi > pi, NEG, 0.0).astype(np.float32)
    negL = np.where(pi > fi, NEG, 0.0).astype(np.float32)
    c["c_negU"] = np.tile(negU, (1, 4))
    c["c_negL"] = np.tile(negL, (1, 4))
    dm = np.zeros((128, 4, 128), np.float64)
    for h in range(4):
        dm[:, h, :] = np.where(fi >= pi, GAM[h] ** np.maximum(fi - pi, 0), 0.0) * ISQ
    c["c_dmT"] = dm.reshape(128, 512).astype(np.float32)
    c["c_wread"] = np.stack([GAM[h] ** (np.arange(128) + 1.0) for h in range(4)], 1).astype(np.float32)
    c["c_wend128"] = (np.stack([GAM[h] ** (127.0 - np.arange(128)) for h in range(4)], 1) * ISQ).astype(np.float32)
    c["c_wend32"] = (np.stack([GAM[h] ** np.maximum(31.0 - np.arange(128), 0.0) for h in range(4)], 1) * ISQ).astype(np.float32)
    s128 = np.zeros((128, 128), np.float32); s128[127, :] = 1.0
    s32 = np.zeros((128, 128), np.float32); s32[31, :] = 1.0
    c["c_sel128"] = s128; c["c_sel32"] = s32

    def rope_tabs(pos):
        half = 64
        inv = (10000.0 ** (-np.arange(half, dtype=np.float32) / half)).astype(np.float32)
        ang = pos.astype(np.float32)[:, None] * inv[None, :]
        cos = np.cos(ang).T.astype(np.float32)
        sin = np.sin(ang).T.astype(np.float32)
        return (np.ascontiguousarray(np.concatenate([cos, cos], 0)),
                np.ascontiguousarray(np.concatenate([-sin, sin], 0)))

    c["c_cs_main"], c["c_sn_main"] = rope_tabs(pos_main)
    c["c_cs_pre"], c["c_sn_pre"] = rope_tabs(pos_pre)
    c["c_cs_smp"], c["c_sn_smp"] = rope_tabs(pos_smp)
    return c


def _perm_w_in(w_in):
    q, k, v, gt = w_in[:, 0:512], w_in[:, 512:1024], w_in[:, 1024:1536], w_in[:, 1536:2048]
    xm, vm, om = w_in[:, 2048:2560], w_in[:, 2560:3072], w_in[:, 3072:3584]
    gi, gf = w_in[:, 3584:3588], w_in[:, 3588:3592]
    gr, gm = w_in[:, 3592:4616], w_in[:, 4616:5640]

    def swap(w):
        w4 = w.reshape(1024, 4, 2, 64)
        return w4[:, :, ::-1, :].reshape(1024, 512)

    fm = np.concatenate([q, swap(q), k, swap(k), gt, xm, om, gr, gm], axis=1)
    tm = np.concatenate([v, vm, gi, gf], axis=1)
    return np.ascontiguousarray(fm), np.ascontiguousarray(tm)


_CACHE = {}


def kernel(x_prompt, x_sample, p_prompt, p_sample, state_ret, state_mlstm_C, state_mlstm_n,
           state_mlstm_m, state_conv, g_mix, w_in, g_ret_gn, w_mq, w_mk, conv_w, conv_b, b_i, b_f,
           g_ml_gn, w_skip, w_up_r, w_up_m, w_out, g_ffn, w_pq, peer_keys, peer_u, peer_v,
           g_ple, w_pg, w_ple, g_final, _cfg=None):
    f = lambda a: np.ascontiguousarray(np.asarray(a, dtype=np.float32))
    x_prompt, x_sample, p_prompt, p_sample = f(x_prompt), f(x_sample), f(p_prompt), f(p_sample)
    B, SEQ, _ = x_prompt.shape
    DB_, DS = x_sample.shape[0], x_sample.shape[1]
    assert B == 4 and DS == 32 and DB_ == 16
    HALF = SEQ // 2
    cfg = dict(nt_main=HALF // 128, nt_pre=HALF // 128, n_smp=2, nec=128)
    if _cfg:
        cfg.update(_cfg)
    key = tuple(sorted(cfg.items()))
    if key not in _CACHE:
        _CACHE[key] = build_program(cfg)
    nc = _CACHE[key]
    past_len = 2048
    fm, tm = _perm_w_in(f(w_in)[0])
    shared = {
        "w_in_fm": fm, "w_in_tm": tm, "g_mix": f(g_mix)[0], "g_ret_gn": f(g_ret_gn)[0], "w_mq": f(w_mq)[0], "w_mk": f(w_mk)[0],
        "conv_w": f(conv_w)[0], "conv_b": f(conv_b)[0], "b_if": np.concatenate([f(b_i)[0], f(b_f)[0]]),
        "g_ml_gn": f(g_ml_gn)[0], "w_skip": f(w_skip)[0], "w_up_r": f(w_up_r)[0], "w_up_m": f(w_up_m)[0], "w_out": f(w_out)[0],
        "g_ffn": f(g_ffn)[0], "w_pq": f(w_pq)[0], "peer_keys": f(peer_keys)[0].reshape(16, 128, 128), "peer_u": f(peer_u)[0],
        "peer_v": f(peer_v)[0], "g_ple": f(g_ple)[0], "w_pg": f(w_pg)[0], "w_ple": f(w_ple)[0], "g_final": f(g_final),
    }
    in_maps = []
    for c in range(8):
        b, half = c // 2, c % 2
        rows = slice(half * HALF, (half + 1) * HALF)
        pos_main = np.arange(half * HALF, (half + 1) * HALF)
        pos_pre = np.arange(0, HALF)
        pos_smp = past_len + np.arange(DS)
        m = dict(shared)
        m.update(_consts(pos_main, pos_pre, pos_smp))
        m["x_main"] = x_prompt[b, rows]
        m["p_main"] = p_prompt[0, b, rows]
        m["x_pre"] = x_prompt[b, 0:HALF]
        m["flag"] = np.full((128, 1), float(half), np.float32)
        ss = slice(2 * c, 2 * c + 2)
        m["x_smp"] = x_sample[ss].reshape(64, 1024)
        m["p_smp"] = p_sample[0, ss].reshape(64, 256)
        m["st_ret"] = f(state_ret)[0, ss]
        m["st_C"] = f(state_mlstm_C)[0, ss]
        m["st_n"] = f(state_mlstm_n)[0, ss]
        m["st_m"] = f(state_mlstm_m)[0, ss]
        m["st_conv"] = f(state_conv)[0, ss]
        in_maps.append({k: np.ascontiguousarray(v) for k, v in m.items()})
    res = run_bass_kernel_spmd(nc, in_maps, core_ids=list(range(8)))
    R = res.results
    y_prompt = np.stack([np.concatenate([R[2 * b]["y_main"], R[2 * b + 1]["y_main"]], 0) for b in range(4)], 0)
    y_sample = np.concatenate([R[c]["y_smp"].reshape(2, 32, 1024) for c in range(8)], 0)
    gp = lambda k: np.stack([R[2 * b + 1][k] for b in range(4)], 0)[None]
    ret_p, C_p, n_p = gp("ret_p"), gp("C_p"), gp("n_p")
    m_p = np.stack([R[2 * b + 1]["m_p"][0] for b in range(4)], 0)[None]
    conv_p = gp("conv_p")
    gsm = lambda k: np.concatenate([R[c][k] for c in range(8)], 0)[None]
    outs = (y_prompt, y_sample, ret_p, C_p, n_p, m_p, conv_p, gsm("ret_s"), gsm("C_s"), gsm("n_s"), gsm("m_s"), gsm("conv_s"))
    if _cfg and _cfg.get("stop_after"):
        return outs, [R[c]["dbg"] for c in range(8)]
    return tuple(np.ascontiguousarray(o, dtype=np.float32) for o in outs)
```

```python
import math
import numpy as np
from contextlib import ExitStack
import concourse.bass as bass
import concourse.mybir as mybir
from concourse.bass_utils import run_bass_kernel_spmd

F32 = mybir.dt.float32
BF16 = mybir.dt.bfloat16
AF = mybir.ActivationFunctionType
ALU = mybir.AluOpType
AX = mybir.AxisListType

D = 1024
NH = 4
DH = 128
EPS = 1e-6
NEG = -30000.0
NE = 16384
ISQ = 128.0 ** -0.5
GAM = [1.0 - 2.0 ** (-5 - h) for h in range(4)]


class Sem:
    def __init__(self, h, sid):
        self.h = h
        self.sid = sid
        self.count = 0


class Buf:
    def __init__(self, name, ap=None):
        self.name = name
        self.ap = ap
        self.w = None
        self.rs = {}
        self.dsem = None

    def __getitem__(self, k):
        return self.ap[k]


class Prog:
    def __init__(self, nc, stack):
        self.nc = nc
        self.stack = stack
        self.streams = {n: [] for n in ("pe", "dve", "act", "pool", "sp")}
        self.seen = {n: {} for n in self.streams}
        self.nsem = 0
        self.all_sems = []
        self.esem = {n: self.new_sem("prog_" + n) for n in ("pe", "dve", "act", "pool")}
        self.ninst = 0

    def new_sem(self, name):
        h = self.stack.enter_context(self.nc.semaphore(name))
        s = Sem(h, self.nsem)
        self.nsem += 1
        if hasattr(self, "all_sems"):
            self.all_sems.append(s)
        return s

    def barrier(self):
        toks = [(s_, s_.count) for s_ in self.all_sems if s_.count > 0]
        for eng in self.streams:
            seen = self.seen[eng]
            waits = []
            for s_, v in toks:
                if seen.get(s_.sid, 0) >= v:
                    continue
                if eng == "pe" and s_ is self.esem["pe"]:
                    continue
                seen[s_.sid] = v
                waits.append((s_, v))
            self.streams[eng].append((waits, None, None, 0))

    def sbuf(self, name, shape, dtype):
        t = self.stack.enter_context(self.nc.sbuf_tensor("sb_" + name, list(shape), dtype))
        return Buf(name, t)

    def _deps(self, eng, reads, writes):
        deps = {}
        own = self.esem.get(eng)

        def add(tok, waw=False):
            if tok is None:
                return
            s, v = tok
            if waw and s is own:
                return
            if s.sid not in deps or deps[s.sid][1] < v:
                deps[s.sid] = (s, v)

        for b in reads:
            add(b.w)
        for b in writes:
            add(b.w, True)
            for tok in b.rs.values():
                add(tok)
        waits = []
        seen = self.seen[eng]
        for sid, (s, v) in deps.items():
            if seen.get(sid, 0) >= v:
                continue
            if eng == "pe" and s is self.esem["pe"]:
                continue
            seen[sid] = v
            waits.append((s, v))
        return waits

    def _mark(self, tok, reads, writes):
        for b in reads:
            b.rs[tok[0].sid] = tok
        for b in writes:
            b.w = tok
            b.rs = {}

    def op(self, eng, fn, reads=(), writes=()):
        waits = self._deps(eng, reads, writes)
        s = self.esem[eng]
        s.count += 1
        tok = (s, s.count)
        self.streams[eng].append((waits, fn, s.h, 1))
        self._mark(tok, reads, writes)
        self.ninst += 1
        return tok

    def dma(self, q, out_ap, in_ap, reads=(), writes=(), sem_buf=None, **kw):
        waits = self._deps(q, reads, writes)
        b = sem_buf
        if b.dsem is None:
            b.dsem = self.new_sem("d_" + b.name)
        s = b.dsem
        s.count += 16
        tok = (s, s.count)

        def fn(e, out_ap=out_ap, in_ap=in_ap, kw=kw):
            return e.dma_start(out=out_ap, in_=in_ap, **kw)

        self.streams[q].append((waits, fn, s.h, 16))
        self._mark(tok, reads, writes)
        self.ninst += 1
        return tok

    def finish(self, out_bufs):
        waits = [b.w for b in out_bufs if b.w is not None]
        self.streams["sp"].append((waits, None, None, 0))

    def replay(self):
        nc = self.nc
        eng_of = {"pe": "tensor", "dve": "vector", "act": "scalar", "pool": "gpsimd", "sp": "sync"}
        with nc.Block() as block:
            def run(name):
                def body(e):
                    for waits, fn, sem, inc in self.streams[name]:
                        for s, v in waits:
                            e.wait_ge(s.h, v)
                        if fn is not None:
                            fn(e).then_inc(sem, inc)
                return body

            for name, attr in eng_of.items():
                getattr(block, attr)(run(name))


def MM(out, lhsT, rhs, start=True, stop=True):
    return lambda e: e.matmul(out, lhsT=lhsT, rhs=rhs, start=start, stop=stop)


def TR(out, in_, ident):
    return lambda e: e.transpose(out, in_, ident)


def ACT(out, in_, func, bias=None, scale=None, accum_out=None):
    kw = {}
    if bias is not None:
        kw["bias"] = bias
    if scale is not None:
        kw["scale"] = scale
    if accum_out is not None:
        kw["accum_out"] = accum_out
    return lambda e: e.activation(out=out, in_=in_, func=func, **kw)


def TT(out, in0, in1, op):
    return lambda e: e.tensor_tensor(out=out, in0=in0, in1=in1, op=op)


def TS(out, in0, s1, op0, s2=None, op1=None):
    if op1 is None:
        return lambda e: e.tensor_scalar(out=out, in0=in0, scalar1=s1, scalar2=None, op0=op0)
    return lambda e: e.tensor_scalar(out=out, in0=in0, scalar1=s1, scalar2=s2, op0=op0, op1=op1)


def STT(out, in0, scalar, in1, op0, op1):
    return lambda e: e.scalar_tensor_tensor(out=out, in0=in0, scalar=scalar, in1=in1, op0=op0, op1=op1)


def CP(out, in_):
    return lambda e: e.tensor_copy(out=out, in_=in_)


def RED(out, in_, op, axis=AX.X):
    return lambda e: e.tensor_reduce(out=out, in_=in_, axis=axis, op=op)


def MSET(ap, v):
    return lambda e: e.memset(ap, v)


class Ring:
    def __init__(self, P, name, depth, elems=4096, dtype=BF16, alloc=None):
        self.P = P
        alloc = alloc or P.sbuf
        self.bufs = [alloc("%s%d" % (name, i), [128, elems], dtype) for i in range(depth)]
        self.i = 0

    def load(self, src_ap, nelem, src_buf, q="sp"):
        b = self.bufs[self.i % len(self.bufs)]
        self.i += 1
        self.P.dma(q, b[:, 0:nelem], src_ap, reads=[src_buf], writes=[b], sem_buf=b)
        return b


def stream(ring, items, fn, L=2):
    pend = {}
    n = len(items)
    for i in range(min(L, n)):
        pend[i] = ring.load(*items[i])
    for i in range(n):
        if i + L < n:
            pend[i + L] = ring.load(*items[i + L])
        fn(i, pend.pop(i))


FM_Q, FM_QS, FM_K, FM_KS, FM_GT, FM_XM, FM_OM, FM_GR, FM_GM = 0, 4, 8, 12, 16, 20, 24, 28, 36
NFM = 44


def build_program(cfg):
    NTM = cfg["nt_main"]
    NTP = cfg["nt_pre"]
    NS = cfg.get("n_smp", 2)
    NEC = cfg.get("nec", 128)
    stop_after = cfg.get("stop_after", None)
    nc = bass.Bass("TRN2", target_bir_lowering=False)

    def din(name, shape, dt=F32):
        return nc.dram_tensor(name, list(shape), dt, kind="ExternalInput").ap()

    def dout(name, shape, dt=F32):
        return nc.dram_tensor(name, list(shape), dt, kind="ExternalOutput").ap()

    def dscr(name, shape, dt=BF16):
        return nc.dram_tensor(name, list(shape), dt, kind="Internal").ap()

    TM, TP = NTM * 128, max(NTP, 1) * 128
    I = {}
    I["x_main"] = din("x_main", [TM, D]); I["p_main"] = din("p_main", [TM, 256])
    I["x_pre"] = din("x_pre", [TP, D])
    I["x_smp"] = din("x_smp", [NS * 32, D]); I["p_smp"] = din("p_smp", [NS * 32, 256])
    I["flag"] = din("flag", [128, 1])
    I["st_ret"] = din("st_ret", [NS, 4, 128, 128]); I["st_C"] = din("st_C", [NS, 4, 128, 128])
    I["st_n"] = din("st_n", [NS, 4, 128]); I["st_m"] = din("st_m", [NS, 4]); I["st_conv"] = din("st_conv", [NS, 3, 512])
    I["w_in_fm"] = din("w_in_fm", [D, NFM * 128]); I["w_in_tm"] = din("w_in_tm", [D, 1032])
    for nm, sh in (("g_mix", [D]), ("g_ret_gn", [512]), ("w_mq", [4, 128, 128]), ("w_mk", [4, 128, 128]),
                   ("conv_w", [4, 512]), ("conv_b", [512]), ("b_if", [8]), ("g_ml_gn", [512]), ("w_skip", [512]),
                   ("w_up_r", [512, D]), ("w_up_m", [512, D]), ("w_out", [D, D]), ("g_ffn", [D]),
                   ("w_pq", [D, 2048]), ("peer_keys", [16, 128, 128]), ("peer_u", [NE, D]), ("peer_v", [NE, D]),
                   ("g_ple", [D]), ("w_pg", [D, D]), ("w_ple", [256, D]), ("g_final", [D])):
        I[nm] = din(nm, sh)
    for nm, sh in (("c_ident", [128, 128]), ("c_ones", [128, 128]), ("c_tri", [128, 128]), ("c_negU", [128, 512]),
                   ("c_negL", [128, 512]), ("c_dmT", [128, 512]), ("c_wread", [128, 4]), ("c_wend128", [128, 4]),
                   ("c_wend32", [128, 4]), ("c_sel128", [128, 128]), ("c_sel32", [128, 128]),
                   ("c_cs_main", [128, TM]), ("c_sn_main", [128, TM]), ("c_cs_pre", [128, TP]), ("c_sn_pre", [128, TP]),
                   ("c_cs_smp", [128, 32]), ("c_sn_smp", [128, 32])):
        I[nm] = din(nm, sh)
    O = {}
    O["y_main"] = dout("y_main", [TM, D]); O["y_smp"] = dout("y_smp", [NS * 32, D])
    O["ret_p"] = dout("ret_p", [4, 128, 128]); O["C_p"] = dout("C_p", [4, 128, 128]); O["n_p"] = dout("n_p", [4, 128])
    O["m_p"] = dout("m_p", [1, 4]); O["conv_p"] = dout("conv_p", [3, 512])
    O["ret_s"] = dout("ret_s", [NS, 4, 128, 128]); O["C_s"] = dout("C_s", [NS, 4, 128, 128]); O["n_s"] = dout("n_s", [NS, 4, 128])
    O["m_s"] = dout("m_s", [NS, 4]); O["conv_s"] = dout("conv_s", [NS, 3, 512])
    if stop_after:
        O["dbg"] = dout("dbg", [TM, D])
    S_fm = dscr("s_fm", [NFM // 4, 128, 4 * 8 * 128])
    S_tm = dscr("s_tm", [2, 128, 8 * 512])
    S_up = dscr("s_up", [2, 128, 4 * 1024])
    S_out = dscr("s_out", [2, 128, 8 * 512])
    S_pq = dscr("s_pq", [4, 128, 4 * 8 * 128])
    S_pg = dscr("s_pg", [2, 128, 8 * 512])
    S_ut = dscr("s_ut", [NEC, 128, 8 * 128])
    S_v = dscr("s_v", [NE, D])
    DB = {k: Buf("dram_" + k) for k in list(I) + list(O) + ["s_fm", "s_tm", "s_up", "s_out", "s_pq", "s_pg", "s_ut", "s_v"]}

    st = ExitStack()
    with st:
        P = Prog(nc, st)
        psum_all = st.enter_context(nc.psum_tensor("psum_all", [128, 4096], F32))
        PB = [Buf("psb%d" % i) for i in range(8)]

        def ps(b, n=128, w=512):
            return psum_all[0:n, b * 512:b * 512 + w]

        def psb16(b, n=128, w=1024):
            return psum_all[:, b * 512:(b + 1) * 512].bitcast(BF16)[0:n, 0:w]

        def cload(name, shape, src, dt=F32, q="sp"):
            b = P.sbuf(name, shape, dt)
            P.dma(q, b[:], src, writes=[b], sem_buf=b)
            return b

        ident = cload("ident", [128, 128], I["c_ident"])
        identb = cload("identb", [128, 128], I["c_ident"], BF16, "pool")
        ones = cload("ones", [128, 128], I["c_ones"])
        tri = cload("tri", [128, 128], I["c_tri"])
        negU = cload("negU", [128, 512], I["c_negU"])
        negL = cload("negL", [128, 512], I["c_negL"])
        dmT = cload("dmT", [128, 512], I["c_dmT"])
        wread = cload("wread", [128, 4], I["c_wread"])
        wend = {128: cload("wend128", [128, 4], I["c_wend128"]), 32: cload("wend32", [128, 4], I["c_wend32"])}
        sel = {128: cload("sel128", [128, 128], I["c_sel128"]), 32: cload("sel32", [128, 128], I["c_sel32"])}
        flag = cload("flag", [128, 1], I["flag"])

        def col_load(name, src1d, ncol):
            b = P.sbuf(name, [128, ncol], F32)
            P.dma("sp", b[:], src1d.rearrange("(c p) -> p c", p=128), writes=[b], sem_buf=b, allow_slow_non_contiguous=True)
            return b

        def bc_load(name, src1d, n):
            b = P.sbuf(name, [128, n], F32)
            P.dma("sp", b[:], src1d.partition_broadcast(128), writes=[b], sem_buf=b)
            return b

        gmix_c = col_load("gmix_c", I["g_mix"], 8)
        gffn_c = col_load("gffn_c", I["g_ffn"], 8)
        gple_c = col_load("gple_c", I["g_ple"], 8)
        gret_c = col_load("gret_c", I["g_ret_gn"], 4)
        gml_c = col_load("gml_c", I["g_ml_gn"], 4)
        wskip_c = col_load("wskip_c", I["w_skip"], 4)
        convb_c = col_load("convb_c", I["conv_b"], 4)
        convw_c = P.sbuf("convw_c", [128, 4, 4], F32)
        P.dma("sp", convw_c[:], I["conv_w"].rearrange("j (c p) -> p j c", p=128), writes=[convw_c], sem_buf=convw_c, allow_slow_non_contiguous=True)
        bif_b = bc_load("bif_b", I["b_if"], 8)
        gfin_b = bc_load("gfin_b", I["g_final"], D)

        NB = 256
        ARENA_BYTES = 130 * 1024
        arena_t = P.sbuf("arena", [128, ARENA_BYTES // 2], BF16)
        aoff = [0]

        def arena_reset():
            aoff[0] = 0

        def A(name, shape, dtype):
            esz = 4 if dtype == F32 else 2
            nel = 1
            for d_ in shape[1:]:
                nel *= d_
            nb_ = (nel * esz + 31) // 32 * 32
            o_ = aoff[0]
            aoff[0] += nb_
            assert aoff[0] <= ARENA_BYTES, (name, aoff[0])
            ap = arena_t[:, o_ // 2:(o_ + nel * esz) // 2]
            if dtype == F32:
                ap = ap.bitcast(F32)
            if len(shape) == 3:
                ap = ap.rearrange("p (a b) -> p a b", a=shape[1])
            elif len(shape) == 4:
                ap = ap.rearrange("p (a b c) -> p a b c", a=shape[1], b=shape[2])
            return Buf(name, ap)

        xt = [P.sbuf("xt%d" % j, [128, D], F32) for j in range(2)]
        hb = P.sbuf("hb", [128, D], BF16)
        junk = hb
        hT = P.sbuf("hT", [128, 8, NB], BF16)
        st1 = P.sbuf("st1", [128, 8], F32)
        rstd = P.sbuf("rstd", [128, 1], F32)
        ring = Ring(P, "wr", 3)
        xmh = P.sbuf("xmh", [128, 4, 3], F32)
        wmq = P.sbuf("wmq", [128, 4, 128], BF16)
        wmk = P.sbuf("wmk", [128, 4, 128], BF16)
        wif = P.sbuf("wif", [128, 8, 8], BF16)
        v1 = [P.sbuf("v1_%d" % j, [128, 4, 129], BF16) for j in range(2)]
        gs = {k: P.sbuf("gs_" + k, [128, 4], F32) for k in
              ("e1", "sp", "g", "M", "mm", "nmm", "mt", "a", "enm", "wend", "den", "d2", "ss", "rs", "ssr", "rsr")}
        mtmm = P.sbuf("mtmm", [128, 8], F32)
        lastb = P.sbuf("lastb", [128, 8], F32)
        aend = P.sbuf("aend", [128, 4], F32)
        mprev = P.sbuf("mprev", [128, 4], F32)
        S32 = P.sbuf("S32", [128, 4, 128], F32)
        Sbf = P.sbuf("Sbf", [128, 4, 128], BF16)
        Cn32 = P.sbuf("Cn32", [128, 4, 129], F32)
        Cnbf = P.sbuf("Cnbf", [128, 4, 129], BF16)
        arena_reset()
        rp_t2 = A("rp_t2", [128, NB], F32)
        cs_t = A("cs_t", [128, NB], F32)
        sn_t = A("sn_t", [128, NB], F32)
        qT = A("qT", [128, 4, NB], BF16)
        kT = A("kT", [128, 4, NB], BF16)
        sgt = A("sgt", [128, 4, NB], BF16)
        sgo = A("sgo", [128, 4, NB], BF16)
        sgr = A("sgr", [128, 8, NB], BF16)
        sgm = A("sgm", [128, 8, NB], BF16)
        xp = [A("xp%d" % j, [128, 4, 131], F32) for j in range(2)]
        cva = A("cva", [128, 4, 128], F32)
        cT = A("cT", [128, 4, NB], BF16)
        qmT = A("qmT", [128, 4, NB], BF16)
        kmT = A("kmT", [128, 4, NB], BF16)
        vr = [A("vr%d" % j, [128, 4, 128], BF16) for j in range(2)]
        gpre = [A("gpre%d" % j, [128, 8], F32) for j in range(2)]
        diag = A("diag", [128, 4, 128], F32)
        DT = A("DT", [128, 4, 128], F32)
        wTr = A("wTr", [128, 4, 128], BF16)
        wTm = A("wTm", [128, 4, 128], BF16)
        kwr = A("kwr", [128, 4, 128], BF16)
        kwm = A("kwm", [128, 4, 128], BF16)
        otmp = A("otmp", [128, 4, 129], F32)
        otmp2 = A("otmp2", [128, 4, 129], F32)
        sqt = A("sqt", [128, 4, 128], F32)
        onb = A("onb", [128, 4, 128], BF16)
        hnb = A("hnb", [128, 4, 128], BF16)
        yrT = A("yrT", [128, 4, NB], BF16)
        ymT = A("ymT", [128, 4, NB], BF16)
        ytmp = A("ytmp", [128, 4, 128], F32)
        ytmp2 = A("ytmp2", [128, 4, 128], F32)
        mg1 = A("mg1", [128, NB], F32)
        mg2 = A("mg2", [128, NB], F32)
        mgT = A("mgT", [128, 8, NB], BF16)
        ropeC = [A("ropeC%d" % i, [128, 4, NB], F32) for i in range(2)]
        mixer_end = aoff[0]
        print("[kernel] arena mixer bytes", aoff[0], flush=True)

        for j in range(2):
            P.op("pool", MSET(v1[j][:, :, 128:129], 1.0), writes=[v1[j]])

        aoff[0] = mixer_end
        stg = [A("stg%d" % i, [128, 8, 512], F32) for i in range(2)]
        stgb = [A("stgb%d" % i, [128, 4096], BF16) for i in range(2)]
        wif32 = A("wif32", [128, 8, 8], F32)
        kraw = A("kraw", [128, 16, 128], BF16)
        wple = P.sbuf("wple", [128, 2, D], BF16)
        keysT = P.sbuf("keysT", [128, 16, 128], BF16)
        sti = [0]
        jobs_now = []
        jobs_later = []

        def fold_panel(src, ncols, gcol, dst_ap, dst_buf, grouped):
            i = sti[0] % 2
            sti[0] += 1
            a, b = stg[i], stgb[i]
            P.dma("sp", a[:, :, 0:ncols], src.rearrange("(kc p) c -> p kc c", p=128), writes=[a], sem_buf=a)
            if grouped:
                ov = b[:, 0:4096].rearrange("p (g kc c) -> p kc g c", g=4, kc=8, c=128)
                iv = a[:, :, 0:512].rearrange("p kc (g c) -> p kc g c", g=4)
            else:
                ov = b[:, 0:8 * ncols].rearrange("p (kc c) -> p kc c", kc=8)
                iv = a[:, :, 0:ncols]
            for kc in range(8):
                if kc % 2 == 0:
                    if gcol is None:
                        P.op("dve", CP(ov[:, kc], iv[:, kc]), reads=[a], writes=[b])
                    else:
                        P.op("dve", TS(ov[:, kc], iv[:, kc], gcol[:, kc:kc + 1], ALU.mult), reads=[a, gcol], writes=[b])
                else:
                    if gcol is None:
                        P.op("act", ACT(ov[:, kc], iv[:, kc], AF.Copy), reads=[a], writes=[b])
                    else:
                        P.op("act", ACT(ov[:, kc], iv[:, kc], AF.Copy, scale=gcol[:, kc:kc + 1]), reads=[a, gcol], writes=[b])
            P.dma("sp", dst_ap, b[:, 0:8 * ncols], reads=[b], writes=[dst_buf], sem_buf=b)

        def job_fm(qd):
            return lambda: fold_panel(I["w_in_fm"][:, qd * 512:(qd + 1) * 512], 512, gmix_c, S_fm[qd], DB["s_fm"], True)

        for qd in (2, 3, 5):
            jobs_now.append(job_fm(qd))
        for qd in range(NFM // 4):
            if qd not in (2, 3, 5):
                jobs_later.append(job_fm(qd))
        for t in range(2):
            jobs_now.append(lambda t=t: fold_panel(I["w_in_tm"][:, t * 512:(t + 1) * 512], 512, gmix_c, S_tm[t], DB["s_tm"], False))

        def job_wif():
            P.dma("sp", wif32[:], I["w_in_tm"][:, 1024:1032].rearrange("(kc p) c -> p kc c", p=128), writes=[wif32], sem_buf=wif32)
            for kc in range(8):
                P.op("dve", TS(wif[:, kc], wif32[:, kc], gmix_c[:, kc:kc + 1], ALU.mult), reads=[wif32, gmix_c], writes=[wif])
            P.dma("pool", wmq[:], I["w_mq"].rearrange("h d e -> d h e"), writes=[wmq], sem_buf=wmq)
            P.dma("pool", wmk[:], I["w_mk"].rearrange("h d e -> d h e"), writes=[wmk], sem_buf=wmk)

        jobs_now.append(job_wif)
        for qd in range(4):
            jobs_later.append(lambda qd=qd: fold_panel(I["w_pq"][:, qd * 512:(qd + 1) * 512], 512, gffn_c, S_pq[qd], DB["s_pq"], True))
        for t in range(2):
            jobs_later.append(lambda t=t: fold_panel(I["w_pg"][:, t * 512:(t + 1) * 512], 512, gple_c, S_pg[t], DB["s_pg"], False))
            jobs_later.append(lambda t=t: fold_panel(I["w_out"][:, t * 512:(t + 1) * 512], 512, None, S_out[t], DB["s_out"], False))

        def job_casts():
            for t, nm in enumerate(("w_up_r", "w_up_m")):
                P.dma("pool", S_up[t].rearrange("p (cc c) -> p cc c", cc=4), I[nm].rearrange("(cc p) c -> p cc c", p=128),
                      writes=[DB["s_up"]], sem_buf=DB["s_up"])
            P.dma("pool", wple[:], I["w_ple"].rearrange("(kc p) c -> p kc c", p=128), writes=[wple], sem_buf=wple)

        jobs_later.append(job_casts)
        for r0 in range(0, NE, 2048):
            jobs_later.append(lambda r0=r0: P.dma("pool", S_v[r0:r0 + 2048, :], I["peer_v"][r0:r0 + 2048, :], writes=[DB["s_v"]], sem_buf=DB["s_v"]))

        def job_keys():
            P.dma("pool", kraw[:], I["peer_keys"].rearrange("g k d -> k g d"), writes=[kraw], sem_buf=kraw)
            for g4 in range(2):
                for g in range(8):
                    P.op("pe", TR(psb16(g4)[:, g * 128:(g + 1) * 128], kraw[:, g4 * 8 + g, :], identb[:]), reads=[kraw, identb], writes=[PB[g4]])
                P.op("dve", CP(keysT[:, g4 * 8:(g4 + 1) * 8, :], psb16(g4).rearrange("p (g k) -> p g k", g=8)), reads=[PB[g4]], writes=[keysT])

        jobs_later.append(job_keys)

        def job_ut(qd):
            def f():
                pb_ = stgb[sti[0] % 2]
                sti[0] += 1
                ov = pb_[:, 0:4096].rearrange("p (g kc e) -> p g kc e", g=4, kc=8)
                for g in range(4):
                    ec = qd * 4 + g
                    usb = stg[ec % 2]
                    us = usb.ap[:, 0:2, :].rearrange("p a b -> p (a b)")
                    P.dma("sp", us, I["peer_u"][ec * 128:(ec + 1) * 128, :], writes=[usb], sem_buf=usb)
                    for hf in range(2):
                        bank = 2 + (ec % 2) * 2 + hf
                        for k4 in range(4):
                            kc = hf * 4 + k4
                            P.op("pe", TR(ps(bank)[:, k4 * 128:(k4 + 1) * 128], us[:, kc * 128:(kc + 1) * 128], ident[:]),
                                 reads=[usb, ident], writes=[PB[bank]])
                        for k4 in range(4):
                            kc = hf * 4 + k4
                            if k4 % 2 == 0:
                                P.op("dve", TS(ov[:, g, kc, :], ps(bank)[:, k4 * 128:(k4 + 1) * 128], gffn_c[:, kc:kc + 1], ALU.mult),
                                     reads=[PB[bank], gffn_c], writes=[pb_])
                            else:
                                P.op("act", ACT(ov[:, g, kc, :], ps(bank)[:, k4 * 128:(k4 + 1) * 128], AF.Copy, scale=gffn_c[:, kc:kc + 1]),
                                     reads=[PB[bank], gffn_c], writes=[pb_])
                P.dma("sp", S_ut[qd * 4:(qd + 1) * 4].rearrange("g p x -> p g x"), pb_[:, 0:4096].rearrange("p (g x) -> p g x", g=4),
                      reads=[pb_], writes=[DB["s_ut"]], sem_buf=pb_)
            return f

        for qd in range(NEC // 4):
            jobs_later.append(job_ut(qd))
        for jb in jobs_now:
            jb()

        def rms_rstd(xtile, n):
            P.op("act", ACT(junk[:n, :], xtile[:n, :], AF.Square, accum_out=st1[:n, 0:1]), reads=[xtile], writes=[junk, st1])
            P.op("dve", TS(st1[:n, 1:2], st1[:n, 0:1], 1.0 / D, ALU.mult, EPS, ALU.add), reads=[st1], writes=[st1])
            P.op("act", ACT(st1[:n, 2:3], st1[:n, 1:2], AF.Sqrt), reads=[st1], writes=[st1])
            P.op("dve", lambda e: e.reciprocal(out=rstd[:n, :], in_=st1[:n, 2:3]), reads=[st1], writes=[rstd])

        def norm_T(xtile, n, dstT, off, bank):
            rms_rstd(xtile, n)
            P.op("act", ACT(hb[:n, :], xtile[:n, :], AF.Copy, scale=rstd[:n, 0:1]), reads=[xtile, rstd], writes=[hb])
            for kc in range(8):
                P.op("pe", TR(psb16(bank)[:, kc * 128:kc * 128 + n], hb[:n, kc * 128:(kc + 1) * 128], identb[:n, :n]),
                     reads=[hb, identb], writes=[PB[bank]])
            src = psb16(bank).rearrange("p (kc t) -> p kc t", kc=8)[:, :, 0:n]
            P.op("dve", CP(dstT[:, :, off:off + n], src), reads=[PB[bank]], writes=[dstT])

        def mixer_block(tiles, full):
            NT = len(tiles)
            offs = []
            o = 0
            for t in tiles:
                offs.append(o)
                o += t["n"]
            NTOT = o
            for j, t in enumerate(tiles):
                n = t["n"]
                P.dma("sp", cs_t[:, offs[j]:offs[j] + n], t["cs"], writes=[cs_t], sem_buf=cs_t)
                P.dma("sp", sn_t[:, offs[j]:offs[j] + n], t["sn"], writes=[sn_t], sem_buf=sn_t)
            for j, t in enumerate(tiles):
                n = t["n"]
                P.dma("sp", xt[j][:n, :], t["x"], reads=[DB[t["xkey"]]], writes=[xt[j]], sem_buf=xt[j])
                norm_T(xt[j], n, hT, offs[j], 0)
            quads = list(range(NFM // 4)) if full else [2, 3, 5]
            items = [(S_fm[qd], 4096, DB["s_fm"]) for qd in quads]
            fmbank = [1, 2]
            cnt = [0]

            def fm_group(g, panel):
                gi = g % 4
                bank = fmbank[cnt[0] % 2]
                cnt[0] += 1
                pv = panel[:, 0:4096].rearrange("p (g kc c) -> p g kc c", g=4, kc=8)
                for kc in range(8):
                    P.op("pe", MM(ps(bank)[:, 0:NTOT], pv[:, gi, kc, :], hT[:, kc, 0:NTOT], kc == 0, kc == 7),
                         reads=[panel, hT], writes=[PB[bank]])
                return bank

            def do_quad(i, panel):
                qd = quads[i]
                for gi in range(4):
                    g = qd * 4 + gi
                    bank = fm_group(g, panel)
                    src = ps(bank)[:, 0:NTOT]
                    if g < FM_GT:
                        which = g // 4
                        h = g % 4
                        if which in (0, 2):
                            P.op("dve", TT(ropeC[which // 2][:, h, 0:NTOT], src, cs_t[:, 0:NTOT], ALU.mult),
                                 reads=[PB[bank], cs_t], writes=[ropeC[which // 2]])
                        else:
                            dst = qT if which == 1 else kT
                            P.op("dve", TT(rp_t2[:, 0:NTOT], src, sn_t[:, 0:NTOT], ALU.mult), reads=[PB[bank], sn_t], writes=[rp_t2])
                            P.op("pool", TT(dst[:, h, 0:NTOT], rp_t2[:, 0:NTOT], ropeC[which // 2][:, h, 0:NTOT], ALU.add),
                                 reads=[rp_t2, ropeC[which // 2]], writes=[dst])
                    elif g < FM_XM:
                        P.op("act", ACT(sgt[:, g - FM_GT, 0:NTOT], src, AF.Silu), reads=[PB[bank]], writes=[sgt])
                    elif g < FM_OM:
                        cc = g - FM_XM
                        for j, t in enumerate(tiles):
                            n = t["n"]
                            P.op("act", ACT(xp[j][:, cc, 3:3 + n], ps(bank)[:, offs[j]:offs[j] + n], AF.Copy), reads=[PB[bank]], writes=[xp[j]])
                    elif g < FM_GR:
                        P.op("act", ACT(sgo[:, g - FM_OM, 0:NTOT], src, AF.Sigmoid), reads=[PB[bank]], writes=[sgo])
                    elif g < FM_GM:
                        P.op("act", ACT(sgr[:, g - FM_GR, 0:NTOT], src, AF.Sigmoid), reads=[PB[bank]], writes=[sgr])
                    else:
                        P.op("act", ACT(sgm[:, g - FM_GM, 0:NTOT], src, AF.Sigmoid), reads=[PB[bank]], writes=[sgm])

            stream(ring, items, do_quad, L=2)
            tm_items = [(S_tm[t], 4096, DB["s_tm"]) for t in ((0, 1) if full else (0, 1))]

            def do_tm(i, panel):
                pv = panel[:, 0:4096].rearrange("p (kc c) -> p kc c", kc=8)
                for j, t in enumerate(tiles):
                    n = t["n"]
                    bank = 3 + j
                    for kc in range(8):
                        P.op("pe", MM(ps(bank)[:n, :], hT[:, kc, offs[j]:offs[j] + n], pv[:, kc, :], kc == 0, kc == 7),
                             reads=[panel, hT], writes=[PB[bank]])
                    src = ps(bank)[:n, :].rearrange("p (h e) -> p h e", h=4)
                    if i == 0:
                        P.op("act", ACT(vr[j][:n], src, AF.Copy), reads=[PB[bank]], writes=[vr[j]])
                    else:
                        P.op("act", ACT(v1[j][:n, :, 0:128], src, AF.Copy), reads=[PB[bank]], writes=[v1[j]])

            stream(ring, tm_items, do_tm, L=2)
            for j, t in enumerate(tiles):
                n = t["n"]
                bank = 5
                for kc in range(8):
                    P.op("pe", MM(ps(bank)[:n, 0:8], hT[:, kc, offs[j]:offs[j] + n], wif[:, kc, :], kc == 0, kc == 7),
                         reads=[wif, hT], writes=[PB[bank]])
                P.op("dve", TT(gpre[j][:n, :], ps(bank)[:n, 0:8], bif_b[:n, :], ALU.add), reads=[PB[bank], bif_b], writes=[gpre[j]])
            for j, t in enumerate(tiles):
                n = t["n"]
                if "load_conv" in t:
                    t["load_conv"]()
                P.op("pool", CP(xp[j][:, :, 0:3], xmh[:]), reads=[xmh], writes=[xp[j]])
                for cc in range(4):
                    P.op("dve", TS(cva[:, cc, 0:n], xp[j][:, cc, 3:3 + n], convw_c[:, 3, cc:cc + 1], ALU.mult, convb_c[:, cc:cc + 1], ALU.add),
                         reads=[xp[j], convw_c, convb_c], writes=[cva])
                    for tap in range(3):
                        P.op("dve", STT(cva[:, cc, 0:n], xp[j][:, cc, tap:tap + n], convw_c[:, tap, cc:cc + 1], cva[:, cc, 0:n], ALU.mult, ALU.add),
                             reads=[xp[j], convw_c, cva], writes=[cva])
                P.op("pool", CP(xmh[:], xp[j][:, :, n:n + 3]), reads=[xp[j]], writes=[xmh])
                if "store_conv" in t:
                    t["store_conv"]()
                P.op("act", ACT(cT[:, :, offs[j]:offs[j] + n], cva[:, :, 0:n], AF.Silu), reads=[cva], writes=[cT])
            for h in range(4):
                if full:
                    P.op("pe", MM(ps(1)[:, 0:NTOT], wmq[:, h, :], cT[:, h, 0:NTOT]), reads=[wmq, cT], writes=[PB[1]])
                    P.op("act", ACT(qmT[:, h, 0:NTOT], ps(1)[:, 0:NTOT], AF.Copy), reads=[PB[1]], writes=[qmT])
                    P.op("pe", MM(ps(2)[:, 0:NTOT], wmk[:, h, :], cT[:, h, 0:NTOT]), reads=[wmk, cT], writes=[PB[2]])
                    P.op("act", ACT(kmT[:, h, 0:NTOT], ps(2)[:, 0:NTOT], AF.Copy, scale=ISQ), reads=[PB[2]], writes=[kmT])
            for j, t in enumerate(tiles):
                core_tile(t, j, offs[j], full)
            return offs, NTOT


        def bc3(ap2, n, k):
            return ap2.unsqueeze(2).to_broadcast([n, 4, k])

        def core_tile(t, j, off, full):
            n = t["n"]
            if "load_state" in t:
                t["load_state"]()
            G = gs
            gp = gpre[j]
            P.op("act", ACT(G["e1"][:n], gp[:n, 4:8], AF.Exp, scale=-1.0), reads=[gp], writes=[G["e1"]])
            P.op("act", ACT(G["sp"][:n], G["e1"][:n], AF.Ln, bias=1.0), reads=[G["e1"]], writes=[G["sp"]])
            P.op("pe", MM(ps(5)[:n, 0:4], tri[:n, :n], G["sp"][:n]), reads=[tri, G["sp"]], writes=[PB[5]])
            P.op("dve", TT(G["g"][:n], gp[:n, 0:4], ps(5)[:n, 0:4], ALU.add), reads=[gp, PB[5]], writes=[G["g"]])
            idb_ = ident[:n, :n].unsqueeze(1).to_broadcast([n, 4, n])
            P.op("dve", TT(diag[:n, :, :n], idb_, bc3(G["g"][:n], n, n), ALU.mult), reads=[ident, G["g"]], writes=[diag])
            negUv = negU[:n].rearrange("p (h s) -> p h s", h=4)[:, :, :n]
            negLv = negL[:n].rearrange("p (h s) -> p h s", h=4)[:, :, :n]
            gview = ps(6)[:n, :].rearrange("p (h s) -> p h s", h=4)[:, :, :n]
            for h in range(4):
                P.op("pe", MM(gview[:, h, :], ones[:n, :n], diag[:n, h, :n], True, False), reads=[ones, diag], writes=[PB[6]])
                P.op("pe", MM(gview[:, h, :], ident[:n, :n], negUv[:, h, :], False, True), reads=[ident, negU], writes=[PB[6]])
            P.op("dve", RED(G["M"][:n], gview, ALU.max), reads=[PB[6]], writes=[G["M"]])
            P.op("dve", TT(G["mm"][:n], G["M"][:n], mprev[:n], ALU.max), reads=[G["M"], mprev], writes=[G["mm"]])
            P.op("dve", TS(G["nmm"][:n], G["mm"][:n], -1.0, ALU.mult), reads=[G["mm"]], writes=[G["nmm"]])
            P.op("dve", TT(mtmm[:n, 0:4], G["mm"][:n], ps(5)[:n, 0:4], ALU.subtract), reads=[G["mm"], PB[5]], writes=[mtmm])
            P.op("pool", CP(mtmm[:n, 4:8], G["mm"][:n]), reads=[G["mm"]], writes=[mtmm])
            P.op("pe", MM(ps(5)[:, 8:16], sel[n][:n, :], mtmm[:n, :]), reads=[sel[n], mtmm], writes=[PB[5]])
            P.op("dve", CP(lastb[:], ps(5)[:, 8:16]), reads=[PB[5]], writes=[lastb])
            P.op("dve", TT(G["wend"][:n], G["g"][:n], lastb[:n, 4:8], ALU.subtract), reads=[G["g"], lastb], writes=[G["wend"]])
            P.op("act", ACT(G["wend"][:n], G["wend"][:n], AF.Exp), reads=[G["wend"]], writes=[G["wend"]])
            P.op("dve", TT(aend[:], mprev[:], lastb[:, 4:8], ALU.subtract), reads=[mprev, lastb], writes=[aend])
            P.op("act", ACT(aend[:], aend[:], AF.Exp), reads=[aend], writes=[aend])
            if full:
                P.op("dve", TT(G["a"][:n], mprev[:n], G["mm"][:n], ALU.subtract), reads=[mprev, G["mm"]], writes=[G["a"]])
                P.op("act", ACT(G["a"][:n], G["a"][:n], AF.Exp), reads=[G["a"]], writes=[G["a"]])
                P.op("act", ACT(G["enm"][:n], mtmm[:n, 0:4], AF.Exp, scale=-1.0), reads=[mtmm], writes=[G["enm"]])
                P.op("dve", TT(diag[:n, :, :n], idb_, bc3(G["nmm"][:n], n, n), ALU.mult), reads=[ident, G["nmm"]], writes=[diag])
                dview = ps(7)[:n, :].rearrange("p (h s) -> p h s", h=4)[:, :, :n]
                for h in range(4):
                    P.op("pe", MM(dview[:, h, :], ones[:n, :n], diag[:n, h, :n], True, False), reads=[ones, diag], writes=[PB[7]])
                    P.op("pe", MM(dview[:, h, :], ident[:n, :n], negLv[:, h, :], False, True), reads=[ident, negL], writes=[PB[7]])
                for h in range(4):
                    P.op("act", ACT(DT[:n, h, :n], dview[:, h, :], AF.Exp, bias=G["g"][:n, h:h + 1]), reads=[PB[7], G["g"]], writes=[DT])
            P.op("pool", CP(mprev[:], lastb[:, 0:4]), reads=[lastb], writes=[mprev])
            cs = slice(off, off + n)
            for h in range(4):
                P.op("pe", TR(psb16(6)[:n, h * 128:(h + 1) * 128], kT[:, h, cs], identb[:]), reads=[kT, identb], writes=[PB[6]])
            P.op("dve", TT(kwr[:n], psb16(6)[:n, 0:512].rearrange("p (h d) -> p h d", h=4), bc3(wend[n][:n], n, 128), ALU.mult),
                 reads=[PB[6], wend[n]], writes=[kwr])
            for h in range(4):
                P.op("pe", MM(ps(7)[:n, h * 128:(h + 1) * 128], cT[:, h, cs], wmk[:, h, :]), reads=[cT, wmk], writes=[PB[7]])
            P.op("dve", STT(kwm[:n], ps(7)[:n, :].rearrange("p (h d) -> p h d", h=4), ISQ, bc3(G["wend"][:n], n, 128), ALU.mult, ALU.mult),
                 reads=[PB[7], G["wend"]], writes=[kwm])
            if full:
                scv = ps(5)[:n, :].rearrange("p (h s) -> p h s", h=4)[:, :, :n]
                for h in range(4):
                    P.op("pe", MM(scv[:, h, :], kT[:, h, cs], qT[:, h, cs]), reads=[kT, qT], writes=[PB[5]])
                dmv = dmT[:n].rearrange("p (h s) -> p h s", h=4)[:, :, :n]
                P.op("dve", TT(wTr[:n, :, :n], scv, dmv, ALU.mult), reads=[PB[5], dmT], writes=[wTr])
                for h in range(4):
                    P.op("pe", MM(ps(6)[:n, h * 128:(h + 1) * 128], wTr[:n, h, :n], vr[j][:n, h, :]), reads=[wTr, vr[j]], writes=[PB[6]])
                for h in range(4):
                    P.op("pe", MM(ps(7)[:n, h * 128:(h + 1) * 128], qT[:, h, cs], Sbf[:, h, :]), reads=[qT, Sbf], writes=[PB[7]])
                o1 = otmp[:n, :, 0:128]
                o2 = otmp2[:n, :, 0:128]
                P.op("act", ACT(o1, ps(6)[:n, :].rearrange("p (h e) -> p h e", h=4), AF.Copy), reads=[PB[6]], writes=[otmp])
                P.op("dve", TT(o2, ps(7)[:n, :].rearrange("p (h e) -> p h e", h=4), bc3(wread[:n], n, 128), ALU.mult), reads=[PB[7], wread], writes=[otmp2])
                P.op("pool", TT(o1, o1, o2, ALU.add), reads=[otmp, otmp2], writes=[otmp])
                P.op("dve", TT(sqt[:n], o1, o1, ALU.mult), reads=[otmp], writes=[sqt])
                P.op("dve", RED(G["ssr"][:n], sqt[:n], ALU.add), reads=[sqt], writes=[G["ssr"]])
                P.op("dve", TS(G["ssr"][:n], G["ssr"][:n], 1.0 / 128, ALU.mult, EPS, ALU.add), reads=[G["ssr"]], writes=[G["ssr"]])
                P.op("act", ACT(G["ssr"][:n], G["ssr"][:n], AF.Sqrt), reads=[G["ssr"]], writes=[G["ssr"]])
                P.op("dve", lambda e: e.reciprocal(out=G["rsr"][:n], in_=G["ssr"][:n]), reads=[G["ssr"]], writes=[G["rsr"]])
                P.op("dve", TT(onb[:n], o1, bc3(G["rsr"][:n], n, 128), ALU.mult), reads=[otmp, G["rsr"]], writes=[onb])
                for h in range(4):
                    P.op("pe", TR(psb16(5)[:, h * 128:h * 128 + n], onb[:n, h, :], identb[:n, :n]), reads=[onb, identb], writes=[PB[5]])
                tv = psb16(5)[:, 0:512].rearrange("p (h t) -> p h t", h=4)[:, :, 0:n]
                P.op("dve", TT(ytmp[:, :, 0:n], tv, bc3(gret_c[:], 128, n), ALU.mult), reads=[PB[5], gret_c], writes=[ytmp])
                P.op("pool", TT(yrT[:, :, cs], ytmp[:, :, 0:n], sgt[:, :, cs], ALU.mult), reads=[ytmp, sgt], writes=[yrT])
                scm = ps(6)[:n, :].rearrange("p (h s) -> p h s", h=4)[:, :, :n]
                for h in range(4):
                    P.op("pe", MM(scm[:, h, :], kmT[:, h, cs], qmT[:, h, cs]), reads=[kmT, qmT], writes=[PB[6]])
                P.op("dve", TT(wTm[:n, :, :n], scm, DT[:n, :, :n], ALU.mult), reads=[PB[6], DT], writes=[wTm])
                ndA = psum_all[:n, 0:1024].rearrange("p (h e) -> p h e", h=4)[:, :, 0:129]
                ndB = psum_all[:n, 1024:2048].rearrange("p (h e) -> p h e", h=4)[:, :, 0:129]
                for h in range(4):
                    P.op("pe", MM(ndA[:, h, :], wTm[:n, h, :n], v1[j][:n, h, :]), reads=[wTm, v1[j]], writes=[PB[0], PB[1]])
                for h in range(4):
                    P.op("pe", MM(ndB[:, h, :], qmT[:, h, cs], Cnbf[:, h, :]), reads=[qmT, Cnbf], writes=[PB[2], PB[3]])
                P.op("act", ACT(otmp[:n], ndA, AF.Copy), reads=[PB[0], PB[1]], writes=[otmp])
                P.op("dve", TT(otmp2[:n], ndB, bc3(G["a"][:n], n, 129), ALU.mult), reads=[PB[2], PB[3], G["a"]], writes=[otmp2])
                P.op("pool", TT(otmp[:n], otmp[:n], otmp2[:n], ALU.add), reads=[otmp, otmp2], writes=[otmp])
                num = otmp[:n, :, 0:128]
                P.op("dve", STT(G["den"][:n], otmp[:n, :, 128], -1.0, otmp[:n, :, 128], ALU.mult, ALU.max), reads=[otmp], writes=[G["den"]])
                P.op("dve", TT(G["den"][:n], G["den"][:n], G["enm"][:n], ALU.max), reads=[G["den"], G["enm"]], writes=[G["den"]])
                P.op("dve", TT(G["d2"][:n], G["den"][:n], G["den"][:n], ALU.mult), reads=[G["den"]], writes=[G["d2"]])
                P.op("dve", TT(sqt[:n], num, num, ALU.mult), reads=[otmp], writes=[sqt])
                P.op("dve", RED(G["ss"][:n], sqt[:n], ALU.add), reads=[sqt], writes=[G["ss"]])
                P.op("dve", TS(G["ss"][:n], G["ss"][:n], 1.0 / 128, ALU.mult), reads=[G["ss"]], writes=[G["ss"]])
                P.op("dve", STT(G["ss"][:n], G["d2"][:n], EPS, G["ss"][:n], ALU.mult, ALU.add), reads=[G["d2"], G["ss"]], writes=[G["ss"]])
                P.op("act", ACT(G["ss"][:n], G["ss"][:n], AF.Sqrt), reads=[G["ss"]], writes=[G["ss"]])
                P.op("dve", lambda e: e.reciprocal(out=G["rs"][:n], in_=G["ss"][:n]), reads=[G["ss"]], writes=[G["rs"]])
                P.op("dve", TT(hnb[:n], num, bc3(G["rs"][:n], n, 128), ALU.mult), reads=[otmp, G["rs"]], writes=[hnb])
                for h in range(4):
                    P.op("pe", TR(psb16(7)[:, h * 128:h * 128 + n], hnb[:n, h, :], identb[:n, :n]), reads=[hnb, identb], writes=[PB[7]])
                tv = psb16(7)[:, 0:512].rearrange("p (h t) -> p h t", h=4)[:, :, 0:n]
                P.op("dve", TT(ytmp[:, :, 0:n], tv, bc3(gml_c[:], 128, n), ALU.mult), reads=[PB[7], gml_c], writes=[ytmp])
                P.op("pool", TT(ytmp2[:, :, 0:n], cT[:, :, cs], bc3(wskip_c[:], 128, n), ALU.mult), reads=[cT, wskip_c], writes=[ytmp2])
                P.op("pool", TT(ytmp[:, :, 0:n], ytmp[:, :, 0:n], ytmp2[:, :, 0:n], ALU.add), reads=[ytmp, ytmp2], writes=[ytmp])
                P.op("pool", TT(ymT[:, :, cs], ytmp[:, :, 0:n], sgo[:, :, cs], ALU.mult), reads=[ytmp, sgo], writes=[ymT])
            for h in range(4):
                P.op("pe", MM(ps(4)[:, h * 128:(h + 1) * 128], kwr[:n, h, :], vr[j][:n, h, :]), reads=[kwr, vr[j]], writes=[PB[4]])
            for h in range(4):
                P.op("dve", STT(S32[:, h, :], S32[:, h, :], GAM[h] ** n, ps(4)[:, h * 128:(h + 1) * 128], ALU.mult, ALU.add),
                     reads=[S32, PB[4]], writes=[S32])
            P.op("act", ACT(Sbf[:], S32[:], AF.Copy), reads=[S32], writes=[Sbf])
            cuA = psum_all[:, 0:1024].rearrange("p (h e) -> p h e", h=4)[:, :, 0:129]
            for h in range(4):
                P.op("pe", MM(cuA[:, h, :], kwm[:n, h, :], v1[j][:n, h, :]), reads=[kwm, v1[j]], writes=[PB[0], PB[1]])
            P.op("pool", TT(Cn32[:], Cn32[:], bc3(aend[:], 128, 129), ALU.mult), reads=[Cn32, aend], writes=[Cn32])
            P.op("dve", TT(Cn32[:], Cn32[:], cuA, ALU.add), reads=[Cn32, PB[0], PB[1]], writes=[Cn32])
            P.op("act", ACT(Cnbf[:], Cn32[:], AF.Copy), reads=[Cn32], writes=[Cnbf])
            if "store_state" in t:
                t["store_state"]()

        def zero_state():
            P.op("pool", MSET(S32[:], 0.0), writes=[S32])
            P.op("pool", MSET(Sbf[:], 0.0), writes=[Sbf])
            P.op("pool", MSET(Cn32[:], 0.0), writes=[Cn32])
            P.op("pool", MSET(Cnbf[:], 0.0), writes=[Cnbf])
            P.op("pool", MSET(mprev[:], 0.0), writes=[mprev])
            P.op("pool", MSET(xmh[:], 0.0), writes=[xmh])

        def flag_state():
            fl = flag[:, 0:1]
            P.op("dve", TS(S32[:], S32[:], fl, ALU.mult), reads=[S32, flag], writes=[S32])
            P.op("dve", TS(Cn32[:], Cn32[:], fl, ALU.mult), reads=[Cn32, flag], writes=[Cn32])
            P.op("dve", TS(mprev[:], mprev[:], fl, ALU.mult), reads=[mprev, flag], writes=[mprev])
            P.op("dve", TS(xmh[:], xmh[:], fl, ALU.mult), reads=[xmh, flag], writes=[xmh])
            P.op("act", ACT(Sbf[:], S32[:], AF.Copy), reads=[S32], writes=[Sbf])
            P.op("act", ACT(Cnbf[:], Cn32[:], AF.Copy), reads=[Cn32], writes=[Cnbf])

        def store_state(o_ret, o_C, o_n, o_m, keys):
            P.dma("sp", o_ret.rearrange("h d e -> d h e"), S32[:], reads=[S32], writes=[DB[keys[0]]], sem_buf=S32)
            P.dma("sp", o_C.rearrange("h d e -> d h e"), Cn32[:, :, 0:128], reads=[Cn32], writes=[DB[keys[1]]], sem_buf=Cn32)
            P.dma("sp", o_n.rearrange("h d -> d h"), Cn32[:, :, 128], reads=[Cn32], writes=[DB[keys[2]]], sem_buf=Cn32, allow_slow_non_contiguous=True)
            P.dma("sp", o_m, mprev[0:1, :], reads=[mprev], writes=[DB[keys[3]]], sem_buf=mprev)

        def store_conv(o_conv, key):
            for cc in range(4):
                P.dma("sp", o_conv[:, cc * 128:(cc + 1) * 128].rearrange("j p -> p j"), xmh[:, cc, :], reads=[xmh], writes=[DB[key]],
                      sem_buf=xmh, allow_slow_non_contiguous=True)

        def load_state(s):
            P.dma("sp", S32[:], I["st_ret"][s].rearrange("h d e -> d h e"), writes=[S32], sem_buf=S32)
            P.dma("sp", Cn32[:, :, 0:128], I["st_C"][s].rearrange("h d e -> d h e"), writes=[Cn32], sem_buf=Cn32)
            P.dma("sp", Cn32[:, :, 128], I["st_n"][s].rearrange("h d -> d h"), writes=[Cn32], sem_buf=Cn32, allow_slow_non_contiguous=True)
            P.dma("sp", mprev[:], I["st_m"][s].partition_broadcast(128), writes=[mprev], sem_buf=mprev)
            P.op("act", ACT(Sbf[:], S32[:], AF.Copy), reads=[S32], writes=[Sbf])
            P.op("act", ACT(Cnbf[:], Cn32[:], AF.Copy), reads=[Cn32], writes=[Cnbf])

        def load_conv(s):
            for cc in range(4):
                P.dma("sp", xmh[:, cc, :], I["st_conv"][s][:, cc * 128:(cc + 1) * 128].rearrange("j p -> p j"), writes=[xmh],
                      sem_buf=xmh, allow_slow_non_contiguous=True)

        def merge_block(tiles, offs, NTOT):
            up = [ring.load(S_up[t], 4096, DB["s_up"]) for t in range(2)]
            upv = [u[:, 0:4096].rearrange("p (cc c) -> p cc c", cc=4) for u in up]
            for dmc in range(8):
                for t, yT in enumerate((yrT, ymT)):
                    bank = 1 + t
                    for cc in range(4):
                        P.op("pe", MM(ps(bank)[:, 0:NTOT], upv[t][:, cc, dmc * 128:(dmc + 1) * 128], yT[:, cc, 0:NTOT], cc == 0, cc == 3),
                             reads=[up[t], yT], writes=[PB[bank]])
                P.op("dve", TT(mg1[:, 0:NTOT], ps(1)[:, 0:NTOT], sgr[:, dmc, 0:NTOT], ALU.mult), reads=[PB[1], sgr], writes=[mg1])
                P.op("dve", TT(mg2[:, 0:NTOT], ps(2)[:, 0:NTOT], sgm[:, dmc, 0:NTOT], ALU.mult), reads=[PB[2], sgm], writes=[mg2])
                P.op("pool", TT(mgT[:, dmc, 0:NTOT], mg1[:, 0:NTOT], mg2[:, 0:NTOT], ALU.add), reads=[mg1, mg2], writes=[mgT])
            wo = [ring.load(S_out[t], 4096, DB["s_out"]) for t in range(2)]
            for j, t in enumerate(tiles):
                n = t["n"]
                for hf in range(2):
                    bank = 3 + hf
                    wv = wo[hf][:, 0:4096].rearrange("p (kc c) -> p kc c", kc=8)
                    for kc in range(8):
                        P.op("pe", MM(ps(bank)[:n, :], mgT[:, kc, offs[j]:offs[j] + n], wv[:, kc, :], kc == 0, kc == 7),
                             reads=[wo[hf], mgT], writes=[PB[bank]])
                    xs = xt[j][:n, hf * 512:(hf + 1) * 512]
                    P.op("dve", TT(xs, xs, ps(bank)[:n, :], ALU.add), reads=[xt[j], PB[bank]], writes=[xt[j]])

        xnT = hT
        arena_reset()
        qpT = A("qpT", [128, 16, NB], BF16)
        sc_s = [A("sc_s%d" % j, [128, 16, 128], F32) for j in range(2)]
        nb1 = [A("nb1_%d" % j, [128, 8, 128], F32) for j in range(2)]
        E1 = [A("E1_%d" % j, [128, 8, 128], F32) for j in range(2)]
        E2 = [A("E2_%d" % j, [128, 8, 128], BF16) for j in range(2)]
        tk = A("tk", [128, 16, 16], F32)
        tkw = A("tkw", [128, 128], F32)
        cand = A("cand", [128, 8, 256], F32)
        candw = A("candw", [128, 256], F32)
        ctop = A("ctop", [128, 8, 24], F32)
        pst = {k: A("pst_" + k, [128, 8], F32) for k in ("th", "Z", "nm1", "nm2", "rz", "lz")}
        cexp = A("cexp", [128, 8, 16], F32)
        IG = 4
        NG = NEC // IG
        Wn = [[[A("Wn%d_%d_%d" % (b, j, h_), [128, IG * 128], BF16) for h_ in range(8)] for j in range(2)] for b in range(2)]
        NMK = 12
        Mk = [A("Mk%d" % i, [128, IG * 128], BF16) for i in range(NMK)]
        Gt = [A("Gt%d" % i, [128, NB], BF16) for i in range(2)]
        WH = [A("WH%d" % i, [128, NB], BF16) for i in range(2)]
        uring = Ring(P, "ur", 4, elems=1024, alloc=A)
        vring = Ring(P, "vr_", 4, elems=1024, alloc=A)
        pt = A("pt", [128, 256], BF16)
        pT = A("pT", [128, 2, 128], BF16)
        pgs = A("pgs", [128, 512], F32)
        yo = [A("yo%d" % j, [128, D], F32) for j in range(1)]
        print("[kernel] arena peer bytes", aoff[0], flush=True)

        def peer_prep_tile(t, j, off):
            n = t["n"]
            for g4 in range(4):
                bank = 4 + g4
                for g in range(4):
                    gg = g4 * 4 + g
                    P.op("pe", MM(ps(bank)[:n, g * 128:(g + 1) * 128], qpT[:, gg, off:off + n], keysT[:, gg, :]), reads=[qpT, keysT], writes=[PB[bank]])
                eng = "act" if g4 % 2 == 0 else "dve"
                dst = sc_s[j][:n, g4 * 4:(g4 + 1) * 4, :]
                src = ps(bank)[:n, :].rearrange("p (g k) -> p g k", g=4)
                if eng == "act":
                    P.op("act", ACT(dst, src, AF.Copy), reads=[PB[bank]], writes=[sc_s[j]])
                else:
                    P.op("dve", CP(dst, src), reads=[PB[bank]], writes=[sc_s[j]])
            S = sc_s[j]
            for g in range(16):
                P.op("dve", lambda e, g=g: e.max(out=tk[:n, g, 0:8], in_=S[:n, g, :]), reads=[S], writes=[tk])
                P.op("dve", lambda e, g=g: e.match_replace(out=tkw[:n, :], in_to_replace=tk[:n, g, 0:8], in_values=S[:n, g, :], imm_value=-1e30),
                     reads=[S, tk], writes=[tkw])
                P.op("dve", lambda e, g=g: e.max(out=tk[:n, g, 8:16], in_=tkw[:n, :]), reads=[tkw], writes=[tk])
            tkv = tk[:n].rearrange("p (hd hf) k -> p hd hf k", hf=2)
            for hd in range(8):
                a_ = tkv[:, hd, 0, :].unsqueeze(2).to_broadcast([n, 16, 16])
                b_ = tkv[:, hd, 1, :].unsqueeze(1).to_broadcast([n, 16, 16])
                P.op("dve", TT(cand[:n, hd, :].rearrange("p (a b) -> p a b", a=16), a_, b_, ALU.add), reads=[tk], writes=[cand])
            for hd in range(8):
                P.op("dve", lambda e, hd=hd: e.max(out=ctop[:n, hd, 0:8], in_=cand[:n, hd, :]), reads=[cand], writes=[ctop])
                P.op("dve", lambda e, hd=hd: e.match_replace(out=candw[:n, :], in_to_replace=ctop[:n, hd, 0:8], in_values=cand[:n, hd, :], imm_value=-1e30),
                     reads=[cand, ctop], writes=[candw])
                P.op("dve", lambda e, hd=hd: e.max(out=ctop[:n, hd, 8:16], in_=candw[:n, :]), reads=[candw], writes=[ctop])
                P.op("dve", lambda e, hd=hd: e.match_replace(out=candw[:n, :], in_to_replace=ctop[:n, hd, 8:16], in_values=candw[:n, :], imm_value=-1e30),
                     reads=[candw, ctop], writes=[candw])
                P.op("dve", lambda e, hd=hd: e.max(out=ctop[:n, hd, 16:24], in_=candw[:n, :]), reads=[candw], writes=[ctop])
            P.op("dve", TT(pst["th"][:n], ctop[:n, :, 15], ctop[:n, :, 16], ALU.add), reads=[ctop], writes=[pst["th"]])
            P.op("dve", TS(pst["th"][:n], pst["th"][:n], 0.5, ALU.mult), reads=[pst["th"]], writes=[pst["th"]])
            P.op("dve", TT(cexp[:n], ctop[:n, :, 0:16], ctop[:n, :, 0:1].to_broadcast([n, 8, 16]), ALU.subtract), reads=[ctop], writes=[cexp])
            P.op("act", ACT(cexp[:n], cexp[:n], AF.Exp), reads=[cexp], writes=[cexp])
            P.op("dve", RED(pst["Z"][:n], cexp[:n], ALU.add), reads=[cexp], writes=[pst["Z"]])
            P.op("act", ACT(pst["lz"][:n], pst["Z"][:n], AF.Ln), reads=[pst["Z"]], writes=[pst["lz"]])
            Sv = S[:n].rearrange("p (hd hf) k -> p hd hf k", hf=2)
            P.op("dve", TT(nb1[j][:n], pst["th"][:n].unsqueeze(2).to_broadcast([n, 8, 128]), Sv[:, :, 0, :], ALU.subtract),
                 reads=[pst["th"], S], writes=[nb1[j]])
            P.op("dve", TT(cand[:n, :, 0:128], Sv[:, :, 0, :], tkv[:, :, 0, 0:1].to_broadcast([n, 8, 128]), ALU.subtract), reads=[S, tk], writes=[cand])
            P.op("act", ACT(E1[j][:n], cand[:n, :, 0:128], AF.Exp), reads=[cand], writes=[E1[j]])
            P.op("dve", TT(pst["nm2"][:n], tkv[:, :, 1, 0], pst["lz"][:n], ALU.add), reads=[tk, pst["lz"]], writes=[pst["nm2"]])
            P.op("dve", TT(cand[:n, :, 128:256], Sv[:, :, 1, :], pst["nm2"][:n].unsqueeze(2).to_broadcast([n, 8, 128]), ALU.subtract),
                 reads=[S, pst["nm2"]], writes=[cand])
            P.op("act", ACT(E2[j][:n], cand[:n, :, 128:256], AF.Exp), reads=[cand], writes=[E2[j]])

        mkctr = [0]

        def mask_slices(tiles, grp, buf, nsl):
            items = [(j, t["n"], hd) for j, t in enumerate(tiles) for hd in range(8)]
            per = (len(items) + nsl - 1) // nsl
            assert per <= NMK // 2
            slices = []
            for si in range(nsl):
                chunk = items[si * per:(si + 1) * per]

                def f(chunk=chunk):
                    i0 = grp * IG
                    st_ = []
                    for (j, n, hd) in chunk:
                        k = mkctr[0]
                        mkctr[0] += 1
                        mk = Mk[k % NMK]
                        mkv = mk[:n, :].rearrange("p (a b) -> p a b", a=IG)
                        Sv = sc_s[j][:n].rearrange("p (hd hf) k -> p hd hf k", hf=2)
                        s2b = Sv[:, hd, 1, :].unsqueeze(1).to_broadcast([n, IG, 128])
                        thb = nb1[j][:n, hd, i0:i0 + IG].unsqueeze(2).to_broadcast([n, IG, 128])
                        P.op("dve", TT(mkv, s2b, thb, ALU.is_ge), reads=[sc_s[j], nb1[j]], writes=[mk])
                        st_.append((j, n, hd, k, mk, mkv))
                    for (j, n, hd, k, mk, mkv) in st_:
                        e2b = E2[j][:n, hd, :].unsqueeze(1).to_broadcast([n, IG, 128])
                        eng = "dve"
                        P.op(eng, TT(mkv, mkv, e2b, ALU.mult), reads=[mk, E2[j]], writes=[mk])
                    for (j, n, hd, k, mk, mkv) in st_:
                        wb__ = Wn[buf][j][hd]
                        wv = wb__[:n, :].rearrange("p (a b) -> p a b", a=IG)
                        if k % 3 == 0:
                            e1b = E1[j][:n, hd, i0:i0 + IG].unsqueeze(2).to_broadcast([n, IG, 128])
                            P.op("dve", TT(wv, mkv, e1b, ALU.mult), reads=[mk, E1[j]], writes=[wb__])
                        else:
                            for gi_ in range(IG):
                                P.op("act", ACT(wv[:, gi_, :], mkv[:, gi_, :], AF.Copy, scale=E1[j][:n, hd, i0 + gi_:i0 + gi_ + 1]),
                                     reads=[mk, E1[j]], writes=[wb__])
                slices.append(f)
            return slices

        def peer_block(tiles, offs, NTOT):
            for j, t in enumerate(tiles):
                norm_T(xt[j], t["n"], xnT, offs[j], 0)
            items = [(S_pq[qd], 4096, DB["s_pq"]) for qd in range(4)]

            def do_q(i, panel):
                pv = panel[:, 0:4096].rearrange("p (g kc c) -> p g kc c", g=4, kc=8)
                for gi in range(4):
                    bank = 1 + (gi % 2)
                    for kc in range(8):
                        P.op("pe", MM(ps(bank)[:, 0:NTOT], pv[:, gi, kc, :], xnT[:, kc, 0:NTOT], kc == 0, kc == 7), reads=[panel, xnT], writes=[PB[bank]])
                    P.op("act", ACT(qpT[:, i * 4 + gi, 0:NTOT], ps(bank)[:, 0:NTOT], AF.Copy), reads=[PB[bank]], writes=[qpT])

            stream(ring, items, do_q, L=2)
            for j, t in enumerate(tiles):
                peer_prep_tile(t, j, offs[j])
            for f in mask_slices(tiles, 0, 0, IG):
                f()
            ybank = [[0, 1], [2, 3]]
            upend = {}
            vpend = {}

            def uload(ec):
                upend[ec] = uring.load(S_ut[ec], 1024, DB["s_ut"])

            def vload(ec):
                vpend[ec] = vring.load(S_v[ec * 128:(ec + 1) * 128, :], 1024, DB["s_v"])

            for e0 in range(min(3, NEC)):
                uload(e0)
                vload(e0)

            def emit_U(ec):
                up_ = upend.pop(ec)
                uv = up_[:, 0:1024].rearrange("p (kc e) -> p kc e", kc=8)
                hb_ = 4 + (ec % 2)
                for kc in range(8):
                    P.op("pe", MM(ps(hb_)[:, 0:NTOT], uv[:, kc, :], xnT[:, kc, 0:NTOT], kc == 0, kc == 7), reads=[up_, xnT], writes=[PB[hb_]])
                gt = Gt[ec % 2]
                P.op("act", ACT(gt[:, 0:NTOT], ps(hb_)[:, 0:NTOT], AF.Gelu), reads=[PB[hb_]], writes=[gt])

            def emit_T(ec):
                grp, gi = ec // IG, ec % IG
                wb_ = 6 + (ec % 2)
                for j, t in enumerate(tiles):
                    n = t["n"]
                    for hd in range(8):
                        P.op("pe", MM(ps(wb_)[:, offs[j]:offs[j] + n], Wn[grp % 2][j][hd][:n, gi * 128:(gi + 1) * 128], identb[:n, :n], hd == 0, hd == 7),
                             reads=[Wn[grp % 2][j][hd], identb], writes=[PB[wb_]])

            def emit_WH(ec):
                wb_ = 6 + (ec % 2)
                P.op("dve", TT(WH[ec % 2][:, 0:NTOT], ps(wb_)[:, 0:NTOT], Gt[ec % 2][:, 0:NTOT], ALU.mult), reads=[PB[wb_], Gt[ec % 2]], writes=[WH[ec % 2]])

            def emit_V(ec):
                vp = vpend.pop(ec)
                wh = WH[ec % 2]
                for j, t in enumerate(tiles):
                    n = t["n"]
                    for hf in range(2):
                        yb = ybank[j][hf]
                        P.op("pe", MM(ps(yb)[:n, :], wh[:, offs[j]:offs[j] + n], vp[:, hf * 512:(hf + 1) * 512], ec == 0, ec == NEC - 1),
                             reads=[wh, vp], writes=[PB[yb]])

            emit_U(0)
            emit_T(0)
            emit_WH(0)
            nxt = []
            for ec in range(NEC):
                grp, gi = ec // IG, ec % IG
                if gi == 0:
                    nxt = mask_slices(tiles, grp + 1, (grp + 1) % 2, IG - 1) if grp + 1 < NG else []
                if ec + 3 < NEC:
                    uload(ec + 3)
                    vload(ec + 3)
                if ec + 1 < NEC:
                    emit_U(ec + 1)
                if gi < len(nxt):
                    nxt[gi]()
                if ec + 1 < NEC:
                    emit_T(ec + 1)
                emit_V(ec)
                if ec + 1 < NEC:
                    emit_WH(ec + 1)
            for j, t in enumerate(tiles):
                n = t["n"]
                for hf in range(2):
                    xs = xt[j][:n, hf * 512:(hf + 1) * 512]
                    P.op("dve", TT(xs, xs, ps(ybank[j][hf])[:n, :], ALU.add), reads=[xt[j], PB[ybank[j][hf]]], writes=[xt[j]])


        def ple_block(tiles, offs, NTOT):
            for j, t in enumerate(tiles):
                norm_T(xt[j], t["n"], xnT, offs[j], 0)
            wg = [ring.load(S_pg[t], 4096, DB["s_pg"]) for t in range(2)]
            for j, t in enumerate(tiles):
                n = t["n"]
                P.dma("pool", pt[:n, :], t["p"], reads=[DB[t["pkey"]]], writes=[pt], sem_buf=pt)
                for kc in range(2):
                    P.op("pe", TR(psb16(1)[:, kc * 128:kc * 128 + n], pt[:n, kc * 128:(kc + 1) * 128], identb[:n, :n]), reads=[pt, identb], writes=[PB[1]])
                P.op("dve", CP(pT[:, :, 0:n], psb16(1)[:, 0:256].rearrange("p (kc t) -> p kc t", kc=2)[:, :, 0:n]), reads=[PB[1]], writes=[pT])
                for hf in range(2):
                    wv = wg[hf][:, 0:4096].rearrange("p (kc c) -> p kc c", kc=8)
                    for kc in range(8):
                        P.op("pe", MM(ps(2)[:n, :], xnT[:, kc, offs[j]:offs[j] + n], wv[:, kc, :], kc == 0, kc == 7), reads=[wg[hf], xnT], writes=[PB[2]])
                    P.op("act", ACT(pgs[:n, :], ps(2)[:n, :], AF.Sigmoid), reads=[PB[2]], writes=[pgs])
                    for kc in range(2):
                        P.op("pe", MM(ps(3)[:n, :], pT[:, kc, 0:n], wple[:, kc, hf * 512:(hf + 1) * 512], kc == 0, kc == 1), reads=[pT, wple], writes=[PB[3]])
                    P.op("dve", TT(pgs[:n, :], pgs[:n, :], ps(3)[:n, :], ALU.mult), reads=[pgs, PB[3]], writes=[pgs])
                    xs = xt[j][:n, hf * 512:(hf + 1) * 512]
                    P.op("pool", TT(xs, xs, pgs[:n, :], ALU.add), reads=[xt[j], pgs], writes=[xt[j]])
                rms_rstd(xt[j], n)
                P.op("dve", STT(yo[0][:n, :], xt[j][:n, :], rstd[:n, 0:1], gfin_b[:n, :], ALU.mult, ALU.mult), reads=[xt[j], rstd, gfin_b], writes=[yo[0]])
                P.dma("sp", t["y"], yo[0][:n, :], reads=[yo[0]], writes=[DB[t["ykey"]]], sem_buf=yo[0])

        zero_state()
        if NTP > 0:
            nblk = NTP // 2
            per_blk = (len(jobs_later) + nblk - 1) // nblk
            for b in range(nblk):
                tiles = []
                for jj in range(2):
                    ti = b * 2 + jj
                    tiles.append(dict(n=128, x=I["x_pre"][ti * 128:(ti + 1) * 128, :], xkey="x_pre",
                                      cs=I["c_cs_pre"][:, ti * 128:(ti + 1) * 128], sn=I["c_sn_pre"][:, ti * 128:(ti + 1) * 128]))
                mixer_block(tiles, False)
                for jb in jobs_later[b * per_blk:(b + 1) * per_blk]:
                    jb()
            jobs_later = jobs_later[nblk * per_blk:]
            flag_state()
        for jb in jobs_later:
            jb()
        for b in range(NTM // 2):
            tiles = []
            for jj in range(2):
                ti = b * 2 + jj
                rows = slice(ti * 128, (ti + 1) * 128)
                tiles.append(dict(n=128, x=I["x_main"][rows, :], xkey="x_main", p=I["p_main"][rows, :], pkey="p_main",
                                  y=O["y_main"][rows, :], ykey="y_main",
                                  cs=I["c_cs_main"][:, rows], sn=I["c_sn_main"][:, rows]))
            offs, NTOT = mixer_block(tiles, True)
            if b == NTM // 2 - 1:
                store_state(O["ret_p"], O["C_p"], O["n_p"], O["m_p"], ("ret_p", "C_p", "n_p", "m_p"))
                store_conv(O["conv_p"], "conv_p")
            merge_block(tiles, offs, NTOT)
            P.barrier()
            if stop_after == "mixer":
                for j, t in enumerate(tiles):
                    P.dma("sp", O["dbg"][(b * 2 + j) * 128:(b * 2 + j + 1) * 128, :], xt[j][:], reads=[xt[j]], writes=[DB["dbg"]], sem_buf=xt[j])
                continue
            peer_block(tiles, offs, NTOT)
            if stop_after == "peer":
                for j, t in enumerate(tiles):
                    P.dma("sp", O["dbg"][(b * 2 + j) * 128:(b * 2 + j + 1) * 128, :], xt[j][:], reads=[xt[j]], writes=[DB["dbg"]], sem_buf=xt[j])
                continue
            ple_block(tiles, offs, NTOT)
            P.barrier()
        if NS > 0 and not stop_after:
            tiles = []
            for s in range(NS):
                rows = slice(s * 32, (s + 1) * 32)
                tiles.append(dict(
                    n=32, x=I["x_smp"][rows, :], xkey="x_smp", p=I["p_smp"][rows, :], pkey="p_smp", y=O["y_smp"][rows, :], ykey="y_smp",
                    cs=I["c_cs_smp"], sn=I["c_sn_smp"],
                    load_state=(lambda s=s: load_state(s)), load_conv=(lambda s=s: load_conv(s)),
                    store_state=(lambda s=s: store_state(O["ret_s"][s], O["C_s"][s], O["n_s"][s], O["m_s"][s:s + 1, :], ("ret_s", "C_s", "n_s", "m_s"))),
                    store_conv=(lambda s=s: store_conv(O["conv_s"][s], "conv_s"))))
            offs, NTOT = mixer_block(tiles, True)
            merge_block(tiles, offs, NTOT)
            for s in range(1, NS):
                P.dma("sp", xt[0][s * 32:(s + 1) * 32, :], xt[s][0:32, :], reads=[xt[s]], writes=[xt[0]], sem_buf=xt[0])
            P.barrier()
            nn = NS * 32
            tiles2 = [dict(n=nn, p=I["p_smp"][0:nn, :], pkey="p_smp", y=O["y_smp"][0:nn, :], ykey="y_smp")]
            peer_block(tiles2, [0], nn)
            ple_block(tiles2, [0], nn)
        P.finish([DB[k] for k in O])
        P.replay()
        print("[kernel] instructions:", P.ninst, {k: len(v) for k, v in P.streams.items()}, "sems:", P.nsem, flush=True)
    return nc


def _consts(pos_main, pos_pre, pos_smp):
    c = {}
    c["c_ident"] = np.eye(128, dtype=np.float32)
    c["c_ones"] = np.ones((128, 128), np.float32)
    pi = np.arange(128)[:, None]
    fi = np.arange(128)[None, :]
    c["c_tri"] = (pi <= fi).astype(np.float32)
    negU = np.where(fi > pi, NEG, 0.0).astype(np.float32)
    negL = np.where(pi > fi, NEG, 0.0).astype(np.float32)
    c["c_negU"] = np.tile(negU, (1, 4))
    c["c_negL"] = np.tile(negL, (1, 4))
    dm = np.zeros((128, 4, 128), np.float64)
    for h in range(4):
        dm[:, h, :] = np.where(fi >= pi, GAM[h] ** np.maximum(fi - pi, 0), 0.0) * ISQ
    c["c_dmT"] = dm.reshape(128, 512).astype(np.float32)
    c["c_wread"] = np.stack([GAM[h] ** (np.arange(128) + 1.0) for h in range(4)], 1).astype(np.float32)
    c["c_wend128"] = (np.stack([GAM[h] ** (127.0 - np.arange(128)) for h in range(4)], 1) * ISQ).astype(np.float32)
    c["c_wend32"] = (np.stack([GAM[h] ** np.maximum(31.0 - np.arange(128), 0.0) for h in range(4)], 1) * ISQ).astype(np.float32)
    s128 = np.zeros((128, 128), np.float32); s128[127, :] = 1.0
    s32 = np.zeros((128, 128), np.float32); s32[31, :] = 1.0
    c["c_sel128"] = s128; c["c_sel32"] = s32

    def rope_tabs(pos):
        half = 64
        inv = (10000.0 ** (-np.arange(half, dtype=np.float32) / half)).astype(np.float32)
        ang = pos.astype(np.float32)[:, None] * inv[None, :]
        cos = np.cos(ang).T.astype(np.float32)
        sin = np.sin(ang).T.astype(np.float32)
        return (np.ascontiguousarray(np.concatenate([cos, cos], 0)),
                np.ascontiguousarray(np.concatenate([-sin, sin], 0)))

    c["c_cs_main"], c["c_sn_main"] = rope_tabs(pos_main)
    c["c_cs_pre"], c["c_sn_pre"] = rope_tabs(pos_pre)
    c["c_cs_smp"], c["c_sn_smp"] = rope_tabs(pos_smp)
    return c


def _perm_w_in(w_in):
    q, k, v, gt = w_in[:, 0:512], w_in[:, 512:1024], w_in[:, 1024:1536], w_in[:, 1536:2048]
    xm, vm, om = w_in[:, 2048:2560], w_in[:, 2560:3072], w_in[:, 3072:3584]
    gi, gf = w_in[:, 3584:3588], w_in[:, 3588:3592]
    gr, gm = w_in[:, 3592:4616], w_in[:, 4616:5640]

    def swap(w):
        w4 = w.reshape(1024, 4, 2, 64)
        return w4[:, :, ::-1, :].reshape(1024, 512)

    fm = np.concatenate([q, swap(q), k, swap(k), gt, xm, om, gr, gm], axis=1)
    tm = np.concatenate([v, vm, gi, gf], axis=1)
    return np.ascontiguousarray(fm), np.ascontiguousarray(tm)


_CACHE = {}


def kernel(x_prompt, x_sample, p_prompt, p_sample, state_ret, state_mlstm_C, state_mlstm_n,
           state_mlstm_m, state_conv, g_mix, w_in, g_ret_gn, w_mq, w_mk, conv_w, conv_b, b_i, b_f,
           g_ml_gn, w_skip, w_up_r, w_up_m, w_out, g_ffn, w_pq, peer_keys, peer_u, peer_v,
           g_ple, w_pg, w_ple, g_final, _cfg=None):
    f = lambda a: np.ascontiguousarray(np.asarray(a, dtype=np.float32))
    x_prompt, x_sample, p_prompt, p_sample = f(x_prompt), f(x_sample), f(p_prompt), f(p_sample)
    B, SEQ, _ = x_prompt.shape
    DB_, DS = x_sample.shape[0], x_sample.shape[1]
    assert B == 4 and DS == 32 and DB_ == 16
    HALF = SEQ // 2
    cfg = dict(nt_main=HALF // 128, nt_pre=HALF // 128, n_smp=2, nec=128)
    if _cfg:
        cfg.update(_cfg)
    key = tuple(sorted(cfg.items()))
    if key not in _CACHE:
        _CACHE[key] = build_program(cfg)
    nc = _CACHE[key]
    past_len = 2048
    fm, tm = _perm_w_in(f(w_in)[0])
    shared = {
        "w_in_fm": fm, "w_in_tm": tm, "g_mix": f(g_mix)[0], "g_ret_gn": f(g_ret_gn)[0], "w_mq": f(w_mq)[0], "w_mk": f(w_mk)[0],
        "conv_w": f(conv_w)[0], "conv_b": f(conv_b)[0], "b_if": np.concatenate([f(b_i)[0], f(b_f)[0]]),
        "g_ml_gn": f(g_ml_gn)[0], "w_skip": f(w_skip)[0], "w_up_r": f(w_up_r)[0], "w_up_m": f(w_up_m)[0], "w_out": f(w_out)[0],
        "g_ffn": f(g_ffn)[0], "w_pq": f(w_pq)[0], "peer_keys": f(peer_keys)[0].reshape(16, 128, 128), "peer_u": f(peer_u)[0],
        "peer_v": f(peer_v)[0], "g_ple": f(g_ple)[0], "w_pg": f(w_pg)[0], "w_ple": f(w_ple)[0], "g_final": f(g_final),
    }
    in_maps = []
    for c in range(8):
        b, half = c // 2, c % 2
        rows = slice(half * HALF, (half + 1) * HALF)
        pos_main = np.arange(half * HALF, (half + 1) * HALF)
        pos_pre = np.arange(0, HALF)
        pos_smp = past_len + np.arange(DS)
        m = dict(shared)
        m.update(_consts(pos_main, pos_pre, pos_smp))
        m["x_main"] = x_prompt[b, rows]
        m["p_main"] = p_prompt[0, b, rows]
        m["x_pre"] = x_prompt[b, 0:HALF]
        m["flag"] = np.full((128, 1), float(half), np.float32)
        ss = slice(2 * c, 2 * c + 2)
        m["x_smp"] = x_sample[ss].reshape(64, 1024)
        m["p_smp"] = p_sample[0, ss].reshape(64, 256)
        m["st_ret"] = f(state_ret)[0, ss]
        m["st_C"] = f(state_mlstm_C)[0, ss]
        m["st_n"] = f(state_mlstm_n)[0, ss]
        m["st_m"] = f(state_mlstm_m)[0, ss]
        m["st_conv"] = f(state_conv)[0, ss]
        in_maps.append({k: np.ascontiguousarray(v) for k, v in m.items()})
    res = run_bass_kernel_spmd(nc, in_maps, core_ids=list(range(8)))
    R = res.results
    y_prompt = np.stack([np.concatenate([R[2 * b]["y_main"], R[2 * b + 1]["y_main"]], 0) for b in range(4)], 0)
    y_sample = np.concatenate([R[c]["y_smp"].reshape(2, 32, 1024) for c in range(8)], 0)
    gp = lambda k: np.stack([R[2 * b + 1][k] for b in range(4)], 0)[None]
    ret_p, C_p, n_p = gp("ret_p"), gp("C_p"), gp("n_p")
    m_p = np.stack([R[2 * b + 1]["m_p"][0] for b in range(4)], 0)[None]
    conv_p = gp("conv_p")
    gsm = lambda k: np.concatenate([R[c][k] for c in range(8)], 0)[None]
    outs = (y_prompt, y_sample, ret_p, C_p, n_p, m_p, conv_p, gsm("ret_s"), gsm("C_s"), gsm("n_s"), gsm("m_s"), gsm("conv_s"))
    if _cfg and _cfg.get("stop_after"):
        return outs, [R[c]["dbg"] for c in range(8)]
    return tuple(np.ascontiguousarray(o, dtype=np.float32) for o in outs)
```

```python
import math
import numpy as np
from contextlib import ExitStack
import concourse.bass as bass
import concourse.mybir as mybir
from concourse.bass_utils import run_bass_kernel_spmd

F32 = mybir.dt.float32
BF16 = mybir.dt.bfloat16
AF = mybir.ActivationFunctionType
ALU = mybir.AluOpType
AX = mybir.AxisListType

D = 1024
NH = 4
DH = 128
EPS = 1e-6
NEG = -30000.0
NE = 16384
ISQ = 128.0 ** -0.5
GAM = [1.0 - 2.0 ** (-5 - h) for h in range(4)]


class Sem:
    def __init__(self, h, sid):
        self.h = h
        self.sid = sid
        self.count = 0


class Buf:
    def __init__(self, name, ap=None):
        self.name = name
        self.ap = ap
        self.w = None
        self.rs = {}
        self.dsem = None

    def __getitem__(self, k):
        return self.ap[k]


class Prog:
    def __init__(self, nc, stack):
        self.nc = nc
        self.stack = stack
        self.streams = {n: [] for n in ("pe", "dve", "act", "pool", "sp")}
        self.seen = {n: {} for n in self.streams}
        self.nsem = 0
        self.all_sems = []
        self.esem = {n: self.new_sem("prog_" + n) for n in ("pe", "dve", "act", "pool")}
        self.ninst = 0

    def new_sem(self, name):
        h = self.stack.enter_context(self.nc.semaphore(name))
        s = Sem(h, self.nsem)
        self.nsem += 1
        if hasattr(self, "all_sems"):
            self.all_sems.append(s)
        return s

    def barrier(self):
        toks = [(s_, s_.count) for s_ in self.all_sems if s_.count > 0]
        for eng in self.streams:
            seen = self.seen[eng]
            waits = []
            for s_, v in toks:
                if seen.get(s_.sid, 0) >= v:
                    continue
                if eng == "pe" and s_ is self.esem["pe"]:
                    continue
                seen[s_.sid] = v
                waits.append((s_, v))
            self.streams[eng].append((waits, None, None, 0))

    def sbuf(self, name, shape, dtype):
        t = self.stack.enter_context(self.nc.sbuf_tensor("sb_" + name, list(shape), dtype))
        return Buf(name, t)

    def _deps(self, eng, reads, writes):
        deps = {}
        own = self.esem.get(eng)

        def add(tok, waw=False):
            if tok is None:
                return
            s, v = tok
            if waw and s is own:
                return
            if s.sid not in deps or deps[s.sid][1] < v:
                deps[s.sid] = (s, v)

        for b in reads:
            add(b.w)
        for b in writes:
            add(b.w, True)
            for tok in b.rs.values():
                add(tok)
        waits = []
        seen = self.seen[eng]
        for sid, (s, v) in deps.items():
            if seen.get(sid, 0) >= v:
                continue
            if eng == "pe" and s is self.esem["pe"]:
                continue
            seen[sid] = v
            waits.append((s, v))
        return waits

    def _mark(self, tok, reads, writes):
        for b in reads:
            b.rs[tok[0].sid] = tok
        for b in writes:
            b.w = tok
            b.rs = {}

    def op(self, eng, fn, reads=(), writes=()):
        waits = self._deps(eng, reads, writes)
        s = self.esem[eng]
        s.count += 1
        tok = (s, s.count)
        self.streams[eng].append((waits, fn, s.h, 1))
        self._mark(tok, reads, writes)
        self.ninst += 1
        return tok

    def dma(self, q, out_ap, in_ap, reads=(), writes=(), sem_buf=None, **kw):
        waits = self._deps(q, reads, writes)
        b = sem_buf
        if b.dsem is None:
            b.dsem = self.new_sem("d_" + b.name)
        s = b.dsem
        s.count += 16
        tok = (s, s.count)

        def fn(e, out_ap=out_ap, in_ap=in_ap, kw=kw):
            return e.dma_start(out=out_ap, in_=in_ap, **kw)

        self.streams[q].append((waits, fn, s.h, 16))
        self._mark(tok, reads, writes)
        self.ninst += 1
        return tok

    def finish(self, out_bufs):
        waits = [b.w for b in out_bufs if b.w is not None]
        self.streams["sp"].append((waits, None, None, 0))

    def replay(self):
        nc = self.nc
        eng_of = {"pe": "tensor", "dve": "vector", "act": "scalar", "pool": "gpsimd", "sp": "sync"}
        with nc.Block() as block:
            def run(name):
                def body(e):
                    for waits, fn, sem, inc in self.streams[name]:
                        for s, v in waits:
                            e.wait_ge(s.h, v)
                        if fn is not None:
                            fn(e).then_inc(sem, inc)
                return body

            for name, attr in eng_of.items():
                getattr(block, attr)(run(name))


def MM(out, lhsT, rhs, start=True, stop=True):
    return lambda e: e.matmul(out, lhsT=lhsT, rhs=rhs, start=start, stop=stop)


def TR(out, in_, ident):
    return lambda e: e.transpose(out, in_, ident)


def ACT(out, in_, func, bias=None, scale=None, accum_out=None):
    kw = {}
    if bias is not None:
        kw["bias"] = bias
    if scale is not None:
        kw["scale"] = scale
    if accum_out is not None:
        kw["accum_out"] = accum_out
    return lambda e: e.activation(out=out, in_=in_, func=func, **kw)


def TT(out, in0, in1, op):
    return lambda e: e.tensor_tensor(out=out, in0=in0, in1=in1, op=op)


def TS(out, in0, s1, op0, s2=None, op1=None):
    if op1 is None:
        return lambda e: e.tensor_scalar(out=out, in0=in0, scalar1=s1, scalar2=None, op0=op0)
    return lambda e: e.tensor_scalar(out=out, in0=in0, scalar1=s1, scalar2=s2, op0=op0, op1=op1)


def STT(out, in0, scalar, in1, op0, op1):
    return lambda e: e.scalar_tensor_tensor(out=out, in0=in0, scalar=scalar, in1=in1, op0=op0, op1=op1)


def CP(out, in_):
    return lambda e: e.tensor_copy(out=out, in_=in_)


def RED(out, in_, op, axis=AX.X):
    return lambda e: e.tensor_reduce(out=out, in_=in_, axis=axis, op=op)


def MSET(ap, v):
    return lambda e: e.memset(ap, v)


class Ring:
    def __init__(self, P, name, depth, elems=4096, dtype=BF16, alloc=None):
        self.P = P
        alloc = alloc or P.sbuf
        self.bufs = [alloc("%s%d" % (name, i), [128, elems], dtype) for i in range(depth)]
        self.i = 0

    def load(self, src_ap, nelem, src_buf, q="sp"):
        b = self.bufs[self.i % len(self.bufs)]
        self.i += 1
        self.P.dma(q, b[:, 0:nelem], src_ap, reads=[src_buf], writes=[b], sem_buf=b)
        return b


def stream(ring, items, fn, L=2):
    pend = {}
    n = len(items)
    for i in range(min(L, n)):
        pend[i] = ring.load(*items[i])
    for i in range(n):
        if i + L < n:
            pend[i + L] = ring.load(*items[i + L])
        fn(i, pend.pop(i))


FM_Q, FM_QS, FM_K, FM_KS, FM_GT, FM_XM, FM_OM, FM_GR, FM_GM = 0, 4, 8, 12, 16, 20, 24, 28, 36
NFM = 44


def build_program(cfg):
    NTM = cfg["nt_main"]
    NTP = cfg["nt_pre"]
    NS = cfg.get("n_smp", 2)
    NEC = cfg.get("nec", 128)
    stop_after = cfg.get("stop_after", None)
    nc = bass.Bass("TRN2", target_bir_lowering=False)

    def din(name, shape, dt=F32):
        return nc.dram_tensor(name, list(shape), dt, kind="ExternalInput").ap()

    def dout(name, shape, dt=F32):
        return nc.dram_tensor(name, list(shape), dt, kind="ExternalOutput").ap()

    def dscr(name, shape, dt=BF16):
        return nc.dram_tensor(name, list(shape), dt, kind="Internal").ap()

    TM, TP = NTM * 128, max(NTP, 1) * 128
    I = {}
    I["x_main"] = din("x_main", [TM, D]); I["p_main"] = din("p_main", [TM, 256])
    I["x_pre"] = din("x_pre", [TP, D])
    I["x_smp"] = din("x_smp", [NS * 32, D]); I["p_smp"] = din("p_smp", [NS * 32, 256])
    I["flag"] = din("flag", [128, 1])
    I["st_ret"] = din("st_ret", [NS, 4, 128, 128]); I["st_C"] = din("st_C", [NS, 4, 128, 128])
    I["st_n"] = din("st_n", [NS, 4, 128]); I["st_m"] = din("st_m", [NS, 4]); I["st_conv"] = din("st_conv", [NS, 3, 512])
    I["w_in_fm"] = din("w_in_fm", [D, NFM * 128]); I["w_in_tm"] = din("w_in_tm", [D, 1032])
    for nm, sh in (("g_mix", [D]), ("g_ret_gn", [512]), ("w_mq", [4, 128, 128]), ("w_mk", [4, 128, 128]),
                   ("conv_w", [4, 512]), ("conv_b", [512]), ("b_if", [8]), ("g_ml_gn", [512]), ("w_skip", [512]),
                   ("w_up_r", [512, D]), ("w_up_m", [512, D]), ("w_out", [D, D]), ("g_ffn", [D]),
                   ("w_pq", [D, 2048]), ("peer_keys", [16, 128, 128]), ("peer_u", [NE, D]), ("peer_v", [NE, D]),
                   ("g_ple", [D]), ("w_pg", [D, D]), ("w_ple", [256, D]), ("g_final", [D])):
        I[nm] = din(nm, sh)
    for nm, sh in (("c_ident", [128, 128]), ("c_ones", [128, 128]), ("c_tri", [128, 128]), ("c_negU", [128, 512]),
                   ("c_negL", [128, 512]), ("c_dmT", [128, 512]), ("c_wread", [128, 4]), ("c_wend128", [128, 4]),
                   ("c_wend32", [128, 4]), ("c_sel128", [128, 128]), ("c_sel32", [128, 128]),
                   ("c_cs_main", [128, TM]), ("c_sn_main", [128, TM]), ("c_cs_pre", [128, TP]), ("c_sn_pre", [128, TP]),
                   ("c_cs_smp", [128, 32]), ("c_sn_smp", [128, 32])):
        I[nm] = din(nm, sh)
    O = {}
    O["y_main"] = dout("y_main", [TM, D]); O["y_smp"] = dout("y_smp", [NS * 32, D])
    O["ret_p"] = dout("ret_p", [4, 128, 128]); O["C_p"] = dout("C_p", [4, 128, 128]); O["n_p"] = dout("n_p", [4, 128])
    O["m_p"] = dout("m_p", [1, 4]); O["conv_p"] = dout("conv_p", [3, 512])
    O["ret_s"] = dout("ret_s", [NS, 4, 128, 128]); O["C_s"] = dout("C_s", [NS, 4, 128, 128]); O["n_s"] = dout("n_s", [NS, 4, 128])
    O["m_s"] = dout("m_s", [NS, 4]); O["conv_s"] = dout("conv_s", [NS, 3, 512])
    if stop_after:
        O["dbg"] = dout("dbg", [TM, D])
    S_fm = dscr("s_fm", [NFM // 4, 128, 4 * 8 * 128])
    S_tm = dscr("s_tm", [2, 128, 8 * 512])
    S_up = dscr("s_up", [2, 128, 4 * 1024])
    S_out = dscr("s_out", [2, 128, 8 * 512])
    S_pq = dscr("s_pq", [4, 128, 4 * 8 * 128])
    S_pg = dscr("s_pg", [2, 128, 8 * 512])
    S_ut = dscr("s_ut", [NEC, 128, 8 * 128])
    S_v = dscr("s_v", [NE, D])
    DB = {k: Buf("dram_" + k) for k in list(I) + list(O) + ["s_fm", "s_tm", "s_up", "s_out", "s_pq", "s_pg", "s_ut", "s_v"]}

    st = ExitStack()
    with st:
        P = Prog(nc, st)
        psum_all = st.enter_context(nc.psum_tensor("psum_all", [128, 4096], F32))
        PB = [Buf("psb%d" % i) for i in range(8)]

        def ps(b, n=128, w=512):
            return psum_all[0:n, b * 512:b * 512 + w]

        def psb16(b, n=128, w=1024):
            return psum_all[:, b * 512:(b + 1) * 512].bitcast(BF16)[0:n, 0:w]

        def cload(name, shape, src, dt=F32, q="sp"):
            b = P.sbuf(name, shape, dt)
            P.dma(q, b[:], src, writes=[b], sem_buf=b)
            return b

        ident = cload("ident", [128, 128], I["c_ident"])
        identb = cload("identb", [128, 128], I["c_ident"], BF16, "pool")
        ones = cload("ones", [128, 128], I["c_ones"])
        tri = cload("tri", [128, 128], I["c_tri"])
        negU = cload("negU", [128, 512], I["c_negU"])
        negL = cload("negL", [128, 512], I["c_negL"])
        dmT = cload("dmT", [128, 512], I["c_dmT"])
        wread = cload("wread", [128, 4], I["c_wread"])
        wend = {128: cload("wend128", [128, 4], I["c_wend128"]), 32: cload("wend32", [128, 4], I["c_wend32"])}
        sel = {128: cload("sel128", [128, 128], I["c_sel128"]), 32: cload("sel32", [128, 128], I["c_sel32"])}
        flag = cload("flag", [128, 1], I["flag"])

        def col_load(name, src1d, ncol):
            b = P.sbuf(name, [128, ncol], F32)
            P.dma("sp", b[:], src1d.rearrange("(c p) -> p c", p=128), writes=[b], sem_buf=b, allow_slow_non_contiguous=True)
            return b

        def bc_load(name, src1d, n):
            b = P.sbuf(name, [128, n], F32)
            P.dma("sp", b[:], src1d.partition_broadcast(128), writes=[b], sem_buf=b)
            return b

        gmix_c = col_load("gmix_c", I["g_mix"], 8)
        gffn_c = col_load("gffn_c", I["g_ffn"], 8)
        gple_c = col_load("gple_c", I["g_ple"], 8)
        gret_c = col_load("gret_c", I["g_ret_gn"], 4)
        gml_c = col_load("gml_c", I["g_ml_gn"], 4)
        wskip_c = col_load("wskip_c", I["w_skip"], 4)
        convb_c = col_load("convb_c", I["conv_b"], 4)
        convw_c = P.sbuf("convw_c", [128, 4, 4], F32)
        P.dma("sp", convw_c[:], I["conv_w"].rearrange("j (c p) -> p j c", p=128), writes=[convw_c], sem_buf=convw_c, allow_slow_non_contiguous=True)
        bif_b = bc_load("bif_b", I["b_if"], 8)
        gfin_b = bc_load("gfin_b", I["g_final"], D)

        NB = 256
        ARENA_BYTES = 130 * 1024
        arena_t = P.sbuf("arena", [128, ARENA_BYTES // 2], BF16)
        aoff = [0]

        def arena_reset():
            aoff[0] = 0

        def A(name, shape, dtype):
            esz = 4 if dtype == F32 else 2
            nel = 1
            for d_ in shape[1:]:
                nel *= d_
            nb_ = (nel * esz + 31) // 32 * 32
            o_ = aoff[0]
            aoff[0] += nb_
            assert aoff[0] <= ARENA_BYTES, (name, aoff[0])
            ap = arena_t[:, o_ // 2:(o_ + nel * esz) // 2]
            if dtype == F32:
                ap = ap.bitcast(F32)
            if len(shape) == 3:
                ap = ap.rearrange("p (a b) -> p a b", a=shape[1])
            elif len(shape) == 4:
                ap = ap.rearrange("p (a b c) -> p a b c", a=shape[1], b=shape[2])
            return Buf(name, ap)

        xt = [P.sbuf("xt%d" % j, [128, D], F32) for j in range(2)]
        hb = P.sbuf("hb", [128, D], BF16)
        junk = hb
        hT = P.sbuf("hT", [128, 8, NB], BF16)
        st1 = P.sbuf("st1", [128, 8], F32)
        rstd = P.sbuf("rstd", [128, 1], F32)
        ring = Ring(P, "wr", 3)
        xmh = P.sbuf("xmh", [128, 4, 3], F32)
        wmq = P.sbuf("wmq", [128, 4, 128], BF16)
        wmk = P.sbuf("wmk", [128, 4, 128], BF16)
        wif = P.sbuf("wif", [128, 8, 8], BF16)
        v1 = [P.sbuf("v1_%d" % j, [128, 4, 129], BF16) for j in range(2)]
        gsT = [{k: P.sbuf("gs%d_%s" % (j_, k), [128, 4], F32) for k in
                ("e1", "sp", "g", "M", "Fn", "mm", "nmm", "mt", "a", "enm", "wend", "den", "d2", "ss", "rs", "ssr", "rsr")} for j_ in range(2)]
        mtmm = P.sbuf("mtmm", [128, 8], F32)
        lastb = P.sbuf("lastb", [128, 8], F32)
        aend = P.sbuf("aend", [128, 4], F32)
        mprev = P.sbuf("mprev", [128, 4], F32)
        S32 = P.sbuf("S32", [128, 4, 128], F32)
        Sbf = P.sbuf("Sbf", [128, 4, 128], BF16)
        Cn32 = P.sbuf("Cn32", [128, 4, 129], F32)
        Cnbf = P.sbuf("Cnbf", [128, 4, 129], BF16)
        arena_reset()
        rp_t2 = A("rp_t2", [128, NB], F32)
        cs_t = A("cs_t", [128, NB], F32)
        sn_t = A("sn_t", [128, NB], F32)
        qT = A("qT", [128, 4, NB], BF16)
        kT = A("kT", [128, 4, NB], BF16)
        sgt = A("sgt", [128, 4, NB], BF16)
        sgo = A("sgo", [128, 4, NB], BF16)
        sgr = A("sgr", [128, 8, NB], BF16)
        sgm = A("sgm", [128, 8, NB], BF16)
        xp = [A("xp%d" % j, [128, 4, 131], F32) for j in range(2)]
        cva = A("cva", [128, 4, 128], F32)
        cT = A("cT", [128, 4, NB], BF16)
        qmT = A("qmT", [128, 4, NB], BF16)
        kmT = A("kmT", [128, 4, NB], BF16)
        vr = [A("vr%d" % j, [128, 4, 128], BF16) for j in range(2)]
        gpre = [A("gpre%d" % j, [128, 8], F32) for j in range(2)]
        diag = A("diag", [128, 4, 128], F32)
        DT = A("DT", [128, 4, 128], F32)
        wTr = A("wTr", [128, 4, 128], BF16)
        wTm = A("wTm", [128, 4, 128], BF16)
        kwr = A("kwr", [128, 4, 128], BF16)
        kwm = A("kwm", [128, 4, 128], BF16)
        otmp = A("otmp", [128, 4, 129], F32)
        otmp2 = A("otmp2", [128, 4, 129], F32)
        sqt = A("sqt", [128, 4, 128], F32)
        onb = A("onb", [128, 4, 128], BF16)
        hnb = A("hnb", [128, 4, 128], BF16)
        yrT = A("yrT", [128, 4, NB], BF16)
        ymT = A("ymT", [128, 4, NB], BF16)
        ytmp = A("ytmp", [128, 4, 128], F32)
        ytmp2 = A("ytmp2", [128, 4, 128], F32)
        mg1 = A("mg1", [128, NB], F32)
        mg2 = A("mg2", [128, NB], F32)
        mgT = A("mgT", [128, 8, NB], BF16)
        ropeC = [A("ropeC%d" % i, [128, 4, NB], F32) for i in range(2)]
        mixer_end = aoff[0]
        print("[kernel] arena mixer bytes", aoff[0], flush=True)

        for j in range(2):
            P.op("pool", MSET(v1[j][:, :, 128:129], 1.0), writes=[v1[j]])

        aoff[0] = mixer_end
        stg = [A("stg%d" % i, [128, 8, 512], F32) for i in range(2)]
        stgb = [A("stgb%d" % i, [128, 4096], BF16) for i in range(2)]
        wif32 = A("wif32", [128, 8, 8], F32)
        kraw = A("kraw", [128, 16, 128], BF16)
        wple = P.sbuf("wple", [128, 2, D], BF16)
        keysT = P.sbuf("keysT", [128, 16, 128], BF16)
        sti = [0]
        jobs_now = []
        jobs_later = []

        def fold_panel(src, ncols, gcol, dst_ap, dst_buf, grouped):
            i = sti[0] % 2
            sti[0] += 1
            a, b = stg[i], stgb[i]
            P.dma("sp", a[:, :, 0:ncols], src.rearrange("(kc p) c -> p kc c", p=128), writes=[a], sem_buf=a)
            if grouped:
                ov = b[:, 0:4096].rearrange("p (g kc c) -> p kc g c", g=4, kc=8, c=128)
                iv = a[:, :, 0:512].rearrange("p kc (g c) -> p kc g c", g=4)
            else:
                ov = b[:, 0:8 * ncols].rearrange("p (kc c) -> p kc c", kc=8)
                iv = a[:, :, 0:ncols]
            for kc in range(8):
                if kc % 2 == 0:
                    if gcol is None:
                        P.op("dve", CP(ov[:, kc], iv[:, kc]), reads=[a], writes=[b])
                    else:
                        P.op("dve", TS(ov[:, kc], iv[:, kc], gcol[:, kc:kc + 1], ALU.mult), reads=[a, gcol], writes=[b])
                else:
                    if gcol is None:
                        P.op("act", ACT(ov[:, kc], iv[:, kc], AF.Copy), reads=[a], writes=[b])
                    else:
                        P.op("act", ACT(ov[:, kc], iv[:, kc], AF.Copy, scale=gcol[:, kc:kc + 1]), reads=[a, gcol], writes=[b])
            P.dma("sp", dst_ap, b[:, 0:8 * ncols], reads=[b], writes=[dst_buf], sem_buf=b)

        def job_fm(qd):
            return lambda: fold_panel(I["w_in_fm"][:, qd * 512:(qd + 1) * 512], 512, gmix_c, S_fm[qd], DB["s_fm"], True)

        for qd in (2, 3, 5):
            jobs_now.append(job_fm(qd))
        for qd in range(NFM // 4):
            if qd not in (2, 3, 5):
                jobs_later.append(job_fm(qd))
        for t in range(2):
            jobs_now.append(lambda t=t: fold_panel(I["w_in_tm"][:, t * 512:(t + 1) * 512], 512, gmix_c, S_tm[t], DB["s_tm"], False))

        def job_wif():
            P.dma("sp", wif32[:], I["w_in_tm"][:, 1024:1032].rearrange("(kc p) c -> p kc c", p=128), writes=[wif32], sem_buf=wif32)
            for kc in range(8):
                P.op("dve", TS(wif[:, kc], wif32[:, kc], gmix_c[:, kc:kc + 1], ALU.mult), reads=[wif32, gmix_c], writes=[wif])
            P.dma("pool", wmq[:], I["w_mq"].rearrange("h d e -> d h e"), writes=[wmq], sem_buf=wmq)
            P.dma("pool", wmk[:], I["w_mk"].rearrange("h d e -> d h e"), writes=[wmk], sem_buf=wmk)

        jobs_now.append(job_wif)
        for qd in range(4):
            jobs_later.append(lambda qd=qd: fold_panel(I["w_pq"][:, qd * 512:(qd + 1) * 512], 512, gffn_c, S_pq[qd], DB["s_pq"], True))
        for t in range(2):
            jobs_later.append(lambda t=t: fold_panel(I["w_pg"][:, t * 512:(t + 1) * 512], 512, gple_c, S_pg[t], DB["s_pg"], False))
            jobs_later.append(lambda t=t: fold_panel(I["w_out"][:, t * 512:(t + 1) * 512], 512, None, S_out[t], DB["s_out"], False))

        def job_casts():
            for t, nm in enumerate(("w_up_r", "w_up_m")):
                P.dma("pool", S_up[t].rearrange("p (cc c) -> p cc c", cc=4), I[nm].rearrange("(cc p) c -> p cc c", p=128),
                      writes=[DB["s_up"]], sem_buf=DB["s_up"])
            P.dma("pool", wple[:], I["w_ple"].rearrange("(kc p) c -> p kc c", p=128), writes=[wple], sem_buf=wple)

        jobs_later.append(job_casts)
        for r0 in range(0, NE, 2048):
            jobs_later.append(lambda r0=r0: P.dma("pool", S_v[r0:r0 + 2048, :], I["peer_v"][r0:r0 + 2048, :], writes=[DB["s_v"]], sem_buf=DB["s_v"]))

        def job_keys():
            P.dma("pool", kraw[:], I["peer_keys"].rearrange("g k d -> k g d"), writes=[kraw], sem_buf=kraw)
            for g4 in range(2):
                for g in range(8):
                    P.op("pe", TR(psb16(g4)[:, g * 128:(g + 1) * 128], kraw[:, g4 * 8 + g, :], identb[:]), reads=[kraw, identb], writes=[PB[g4]])
                P.op("dve", CP(keysT[:, g4 * 8:(g4 + 1) * 8, :], psb16(g4).rearrange("p (g k) -> p g k", g=8)), reads=[PB[g4]], writes=[keysT])

        jobs_later.append(job_keys)

        def job_ut(qd):
            def f():
                pb_ = stgb[sti[0] % 2]
                sti[0] += 1
                ov = pb_[:, 0:4096].rearrange("p (g kc e) -> p g kc e", g=4, kc=8)
                for g in range(4):
                    ec = qd * 4 + g
                    usb = stg[ec % 2]
                    us = usb.ap[:, 0:2, :].rearrange("p a b -> p (a b)")
                    P.dma("sp", us, I["peer_u"][ec * 128:(ec + 1) * 128, :], writes=[usb], sem_buf=usb)
                    for hf in range(2):
                        bank = 2 + (ec % 2) * 2 + hf
                        for k4 in range(4):
                            kc = hf * 4 + k4
                            P.op("pe", TR(ps(bank)[:, k4 * 128:(k4 + 1) * 128], us[:, kc * 128:(kc + 1) * 128], ident[:]),
                                 reads=[usb, ident], writes=[PB[bank]])
                        for k4 in range(4):
                            kc = hf * 4 + k4
                            if k4 % 2 == 0:
                                P.op("dve", TS(ov[:, g, kc, :], ps(bank)[:, k4 * 128:(k4 + 1) * 128], gffn_c[:, kc:kc + 1], ALU.mult),
                                     reads=[PB[bank], gffn_c], writes=[pb_])
                            else:
                                P.op("act", ACT(ov[:, g, kc, :], ps(bank)[:, k4 * 128:(k4 + 1) * 128], AF.Copy, scale=gffn_c[:, kc:kc + 1]),
                                     reads=[PB[bank], gffn_c], writes=[pb_])
                P.dma("sp", S_ut[qd * 4:(qd + 1) * 4].rearrange("g p x -> p g x"), pb_[:, 0:4096].rearrange("p (g x) -> p g x", g=4),
                      reads=[pb_], writes=[DB["s_ut"]], sem_buf=pb_)
            return f

        for qd in range(NEC // 4):
            jobs_later.append(job_ut(qd))
        for jb in jobs_now:
            jb()

        def rms_rstd(xtile, n):
            P.op("act", ACT(junk[:n, :], xtile[:n, :], AF.Square, accum_out=st1[:n, 0:1]), reads=[xtile], writes=[junk, st1])
            P.op("dve", TS(st1[:n, 1:2], st1[:n, 0:1], 1.0 / D, ALU.mult, EPS, ALU.add), reads=[st1], writes=[st1])
            P.op("act", ACT(st1[:n, 2:3], st1[:n, 1:2], AF.Sqrt), reads=[st1], writes=[st1])
            P.op("dve", lambda e: e.reciprocal(out=rstd[:n, :], in_=st1[:n, 2:3]), reads=[st1], writes=[rstd])

        def norm_T(xtile, n, dstT, off, bank):
            rms_rstd(xtile, n)
            P.op("act", ACT(hb[:n, :], xtile[:n, :], AF.Copy, scale=rstd[:n, 0:1]), reads=[xtile, rstd], writes=[hb])
            for kc in range(8):
                P.op("pe", TR(psb16(bank)[:, kc * 128:kc * 128 + n], hb[:n, kc * 128:(kc + 1) * 128], identb[:n, :n]),
                     reads=[hb, identb], writes=[PB[bank]])
            src = psb16(bank).rearrange("p (kc t) -> p kc t", kc=8)[:, :, 0:n]
            P.op("dve", CP(dstT[:, :, off:off + n], src), reads=[PB[bank]], writes=[dstT])

        def mixer_block(tiles, full):
            NT = len(tiles)
            offs = []
            o = 0
            for t in tiles:
                offs.append(o)
                o += t["n"]
            NTOT = o
            for j, t in enumerate(tiles):
                n = t["n"]
                P.dma("sp", cs_t[:, offs[j]:offs[j] + n], t["cs"], writes=[cs_t], sem_buf=cs_t)
                P.dma("sp", sn_t[:, offs[j]:offs[j] + n], t["sn"], writes=[sn_t], sem_buf=sn_t)
            for j, t in enumerate(tiles):
                n = t["n"]
                P.dma("sp", xt[j][:n, :], t["x"], reads=[DB[t["xkey"]]], writes=[xt[j]], sem_buf=xt[j])
                norm_T(xt[j], n, hT, offs[j], 0)
            quads = list(range(NFM // 4)) if full else [2, 3, 5]
            items = [(S_fm[qd], 4096, DB["s_fm"]) for qd in quads]
            fmbank = [1, 2]
            cnt = [0]

            def fm_group(g, panel):
                gi = g % 4
                bank = fmbank[cnt[0] % 2]
                cnt[0] += 1
                pv = panel[:, 0:4096].rearrange("p (g kc c) -> p g kc c", g=4, kc=8)
                for kc in range(8):
                    P.op("pe", MM(ps(bank)[:, 0:NTOT], pv[:, gi, kc, :], hT[:, kc, 0:NTOT], kc == 0, kc == 7),
                         reads=[panel, hT], writes=[PB[bank]])
                return bank

            def do_quad(i, panel):
                qd = quads[i]
                for gi in range(4):
                    g = qd * 4 + gi
                    bank = fm_group(g, panel)
                    src = ps(bank)[:, 0:NTOT]
                    if g < FM_GT:
                        which = g // 4
                        h = g % 4
                        if which in (0, 2):
                            P.op("dve", TT(ropeC[which // 2][:, h, 0:NTOT], src, cs_t[:, 0:NTOT], ALU.mult),
                                 reads=[PB[bank], cs_t], writes=[ropeC[which // 2]])
                        else:
                            dst = qT if which == 1 else kT
                            P.op("dve", TT(rp_t2[:, 0:NTOT], src, sn_t[:, 0:NTOT], ALU.mult), reads=[PB[bank], sn_t], writes=[rp_t2])
                            P.op("pool", TT(dst[:, h, 0:NTOT], rp_t2[:, 0:NTOT], ropeC[which // 2][:, h, 0:NTOT], ALU.add),
                                 reads=[rp_t2, ropeC[which // 2]], writes=[dst])
                    elif g < FM_XM:
                        P.op("act", ACT(sgt[:, g - FM_GT, 0:NTOT], src, AF.Silu), reads=[PB[bank]], writes=[sgt])
                    elif g < FM_OM:
                        cc = g - FM_XM
                        for j, t in enumerate(tiles):
                            n = t["n"]
                            P.op("act", ACT(xp[j][:, cc, 3:3 + n], ps(bank)[:, offs[j]:offs[j] + n], AF.Copy), reads=[PB[bank]], writes=[xp[j]])
                    elif g < FM_GR:
                        P.op("act", ACT(sgo[:, g - FM_OM, 0:NTOT], src, AF.Sigmoid), reads=[PB[bank]], writes=[sgo])
                    elif g < FM_GM:
                        P.op("act", ACT(sgr[:, g - FM_GR, 0:NTOT], src, AF.Sigmoid), reads=[PB[bank]], writes=[sgr])
                    else:
                        P.op("act", ACT(sgm[:, g - FM_GM, 0:NTOT], src, AF.Sigmoid), reads=[PB[bank]], writes=[sgm])

            stream(ring, items, do_quad, L=2)
            tm_items = [(S_tm[t], 4096, DB["s_tm"]) for t in ((0, 1) if full else (0, 1))]

            def do_tm(i, panel):
                pv = panel[:, 0:4096].rearrange("p (kc c) -> p kc c", kc=8)
                for j, t in enumerate(tiles):
                    n = t["n"]
                    bank = 3 + j
                    for kc in range(8):
                        P.op("pe", MM(ps(bank)[:n, :], hT[:, kc, offs[j]:offs[j] + n], pv[:, kc, :], kc == 0, kc == 7),
                             reads=[panel, hT], writes=[PB[bank]])
                    src = ps(bank)[:n, :].rearrange("p (h e) -> p h e", h=4)
                    if i == 0:
                        P.op("act", ACT(vr[j][:n], src, AF.Copy), reads=[PB[bank]], writes=[vr[j]])
                    else:
                        P.op("act", ACT(v1[j][:n, :, 0:128], src, AF.Copy), reads=[PB[bank]], writes=[v1[j]])

            stream(ring, tm_items, do_tm, L=2)
            for j, t in enumerate(tiles):
                n = t["n"]
                bank = 5
                for kc in range(8):
                    P.op("pe", MM(ps(bank)[:n, 0:8], hT[:, kc, offs[j]:offs[j] + n], wif[:, kc, :], kc == 0, kc == 7),
                         reads=[wif, hT], writes=[PB[bank]])
                P.op("dve", TT(gpre[j][:n, :], ps(bank)[:n, 0:8], bif_b[:n, :], ALU.add), reads=[PB[bank], bif_b], writes=[gpre[j]])
            for j, t in enumerate(tiles):
                n = t["n"]
                if "load_conv" in t:
                    t["load_conv"]()
                P.op("pool", CP(xp[j][:, :, 0:3], xmh[:]), reads=[xmh], writes=[xp[j]])
                for cc in range(4):
                    P.op("dve", TS(cva[:, cc, 0:n], xp[j][:, cc, 3:3 + n], convw_c[:, 3, cc:cc + 1], ALU.mult, convb_c[:, cc:cc + 1], ALU.add),
                         reads=[xp[j], convw_c, convb_c], writes=[cva])
                    for tap in range(3):
                        P.op("dve", STT(cva[:, cc, 0:n], xp[j][:, cc, tap:tap + n], convw_c[:, tap, cc:cc + 1], cva[:, cc, 0:n], ALU.mult, ALU.add),
                             reads=[xp[j], convw_c, cva], writes=[cva])
                P.op("pool", CP(xmh[:], xp[j][:, :, n:n + 3]), reads=[xp[j]], writes=[xmh])
                if "store_conv" in t:
                    t["store_conv"]()
                P.op("act", ACT(cT[:, :, offs[j]:offs[j] + n], cva[:, :, 0:n], AF.Silu), reads=[cva], writes=[cT])
            for h in range(4):
                if full:
                    P.op("pe", MM(ps(1)[:, 0:NTOT], wmq[:, h, :], cT[:, h, 0:NTOT]), reads=[wmq, cT], writes=[PB[1]])
                    P.op("act", ACT(qmT[:, h, 0:NTOT], ps(1)[:, 0:NTOT], AF.Copy), reads=[PB[1]], writes=[qmT])
                    P.op("pe", MM(ps(2)[:, 0:NTOT], wmk[:, h, :], cT[:, h, 0:NTOT]), reads=[wmk, cT], writes=[PB[2]])
                    P.op("act", ACT(kmT[:, h, 0:NTOT], ps(2)[:, 0:NTOT], AF.Copy, scale=ISQ), reads=[PB[2]], writes=[kmT])
            for j, t in enumerate(tiles):
                core_A(t, j)
            for j, t in enumerate(tiles):
                core_tile(t, j, offs[j], full)
            return offs, NTOT


        def bc3(ap2, n, k):
            return ap2.unsqueeze(2).to_broadcast([n, 4, k])

        def core_A(t, j):
            n = t["n"]
            G = gsT[j]
            gp = gpre[j]
            P.op("act", ACT(G["e1"][:n], gp[:n, 4:8], AF.Exp, scale=-1.0), reads=[gp], writes=[G["e1"]])
            P.op("act", ACT(G["sp"][:n], G["e1"][:n], AF.Ln, bias=1.0), reads=[G["e1"]], writes=[G["sp"]])
            P.op("pe", MM(ps(5)[:n, 0:4], tri[:n, :n], G["sp"][:n]), reads=[tri, G["sp"]], writes=[PB[5]])
            P.op("dve", TT(G["g"][:n], gp[:n, 0:4], ps(5)[:n, 0:4], ALU.add), reads=[gp, PB[5]], writes=[G["g"]])
            P.op("act", ACT(G["Fn"][:n], ps(5)[:n, 0:4], AF.Copy), reads=[PB[5]], writes=[G["Fn"]])
            idb_ = ident[:n, :n].unsqueeze(1).to_broadcast([n, 4, n])
            P.op("dve", TT(diag[:n, :, :n], idb_, bc3(G["g"][:n], n, n), ALU.mult), reads=[ident, G["g"]], writes=[diag])
            negUv = negU[:n].rearrange("p (h s) -> p h s", h=4)[:, :, :n]
            gview = ps(6)[:n, :].rearrange("p (h s) -> p h s", h=4)[:, :, :n]
            for h in range(4):
                P.op("pe", MM(gview[:, h, :], ones[:n, :n], diag[:n, h, :n], True, False), reads=[ones, diag], writes=[PB[6]])
                P.op("pe", MM(gview[:, h, :], ident[:n, :n], negUv[:, h, :], False, True), reads=[ident, negU], writes=[PB[6]])
            P.op("dve", RED(G["M"][:n], gview, ALU.max), reads=[PB[6]], writes=[G["M"]])

        def core_tile(t, j, off, full):
            n = t["n"]
            if "load_state" in t:
                t["load_state"]()
            G = gsT[j]
            gp = gpre[j]
            idb_ = ident[:n, :n].unsqueeze(1).to_broadcast([n, 4, n])
            negLv = negL[:n].rearrange("p (h s) -> p h s", h=4)[:, :, :n]
            P.op("dve", TT(G["mm"][:n], G["M"][:n], mprev[:n], ALU.max), reads=[G["M"], mprev], writes=[G["mm"]])
            P.op("dve", TS(G["nmm"][:n], G["mm"][:n], -1.0, ALU.mult), reads=[G["mm"]], writes=[G["nmm"]])
            P.op("dve", TT(mtmm[:n, 0:4], G["mm"][:n], G["Fn"][:n], ALU.subtract), reads=[G["mm"], G["Fn"]], writes=[mtmm])
            P.op("pool", CP(mtmm[:n, 4:8], G["mm"][:n]), reads=[G["mm"]], writes=[mtmm])
            P.op("pe", MM(ps(5)[:, 8:16], sel[n][:n, :], mtmm[:n, :]), reads=[sel[n], mtmm], writes=[PB[5]])
            P.op("dve", CP(lastb[:], ps(5)[:, 8:16]), reads=[PB[5]], writes=[lastb])
            P.op("dve", TT(G["wend"][:n], G["g"][:n], lastb[:n, 4:8], ALU.subtract), reads=[G["g"], lastb], writes=[G["wend"]])
            P.op("act", ACT(G["wend"][:n], G["wend"][:n], AF.Exp), reads=[G["wend"]], writes=[G["wend"]])
            P.op("dve", TT(aend[:], mprev[:], lastb[:, 4:8], ALU.subtract), reads=[mprev, lastb], writes=[aend])
            P.op("act", ACT(aend[:], aend[:], AF.Exp), reads=[aend], writes=[aend])
            if full:
                P.op("dve", TT(G["a"][:n], mprev[:n], G["mm"][:n], ALU.subtract), reads=[mprev, G["mm"]], writes=[G["a"]])
                P.op("act", ACT(G["a"][:n], G["a"][:n], AF.Exp), reads=[G["a"]], writes=[G["a"]])
                P.op("act", ACT(G["enm"][:n], mtmm[:n, 0:4], AF.Exp, scale=-1.0), reads=[mtmm], writes=[G["enm"]])
                P.op("dve", TT(diag[:n, :, :n], idb_, bc3(G["nmm"][:n], n, n), ALU.mult), reads=[ident, G["nmm"]], writes=[diag])
                dview = ps(7)[:n, :].rearrange("p (h s) -> p h s", h=4)[:, :, :n]
                for h in range(4):
                    P.op("pe", MM(dview[:, h, :], ones[:n, :n], diag[:n, h, :n], True, False), reads=[ones, diag], writes=[PB[7]])
                    P.op("pe", MM(dview[:, h, :], ident[:n, :n], negLv[:, h, :], False, True), reads=[ident, negL], writes=[PB[7]])
                for h in range(4):
                    P.op("act", ACT(DT[:n, h, :n], dview[:, h, :], AF.Exp, bias=G["g"][:n, h:h + 1]), reads=[PB[7], G["g"]], writes=[DT])
            P.op("pool", CP(mprev[:], lastb[:, 0:4]), reads=[lastb], writes=[mprev])
            cs = slice(off, off + n)
            for h in range(4):
                P.op("pe", TR(psb16(6)[:n, h * 128:(h + 1) * 128], kT[:, h, cs], identb[:]), reads=[kT, identb], writes=[PB[6]])
            P.op("dve", TT(kwr[:n], psb16(6)[:n, 0:512].rearrange("p (h d) -> p h d", h=4), bc3(wend[n][:n], n, 128), ALU.mult),
                 reads=[PB[6], wend[n]], writes=[kwr])
            for h in range(4):
                P.op("pe", MM(ps(7)[:n, h * 128:(h + 1) * 128], cT[:, h, cs], wmk[:, h, :]), reads=[cT, wmk], writes=[PB[7]])
            P.op("dve", STT(kwm[:n], ps(7)[:n, :].rearrange("p (h d) -> p h d", h=4), ISQ, bc3(G["wend"][:n], n, 128), ALU.mult, ALU.mult),
                 reads=[PB[7], G["wend"]], writes=[kwm])
            if full:
                scv = ps(5)[:n, :].rearrange("p (h s) -> p h s", h=4)[:, :, :n]
                for h in range(4):
                    P.op("pe", MM(scv[:, h, :], kT[:, h, cs], qT[:, h, cs]), reads=[kT, qT], writes=[PB[5]])
                dmv = dmT[:n].rearrange("p (h s) -> p h s", h=4)[:, :, :n]
                P.op("dve", TT(wTr[:n, :, :n], scv, dmv, ALU.mult), reads=[PB[5], dmT], writes=[wTr])
                for h in range(4):
                    P.op("pe", MM(ps(6)[:n, h * 128:(h + 1) * 128], wTr[:n, h, :n], vr[j][:n, h, :]), reads=[wTr, vr[j]], writes=[PB[6]])
                for h in range(4):
                    P.op("pe", MM(ps(7)[:n, h * 128:(h + 1) * 128], qT[:, h, cs], Sbf[:, h, :]), reads=[qT, Sbf], writes=[PB[7]])
                o1 = otmp[:n, :, 0:128]
                o2 = otmp2[:n, :, 0:128]
                P.op("act", ACT(o1, ps(6)[:n, :].rearrange("p (h e) -> p h e", h=4), AF.Copy), reads=[PB[6]], writes=[otmp])
                P.op("dve", TT(o2, ps(7)[:n, :].rearrange("p (h e) -> p h e", h=4), bc3(wread[:n], n, 128), ALU.mult), reads=[PB[7], wread], writes=[otmp2])
                P.op("pool", TT(o1, o1, o2, ALU.add), reads=[otmp, otmp2], writes=[otmp])
                P.op("dve", TT(sqt[:n], o1, o1, ALU.mult), reads=[otmp], writes=[sqt])
                P.op("dve", RED(G["ssr"][:n], sqt[:n], ALU.add), reads=[sqt], writes=[G["ssr"]])
                P.op("dve", TS(G["ssr"][:n], G["ssr"][:n], 1.0 / 128, ALU.mult, EPS, ALU.add), reads=[G["ssr"]], writes=[G["ssr"]])
                P.op("act", ACT(G["ssr"][:n], G["ssr"][:n], AF.Sqrt), reads=[G["ssr"]], writes=[G["ssr"]])
                P.op("dve", lambda e: e.reciprocal(out=G["rsr"][:n], in_=G["ssr"][:n]), reads=[G["ssr"]], writes=[G["rsr"]])
                P.op("dve", TT(onb[:n], o1, bc3(G["rsr"][:n], n, 128), ALU.mult), reads=[otmp, G["rsr"]], writes=[onb])
                for h in range(4):
                    P.op("pe", TR(psb16(5)[:, h * 128:h * 128 + n], onb[:n, h, :], identb[:n, :n]), reads=[onb, identb], writes=[PB[5]])
                tv = psb16(5)[:, 0:512].rearrange("p (h t) -> p h t", h=4)[:, :, 0:n]
                P.op("dve", TT(ytmp[:, :, 0:n], tv, bc3(gret_c[:], 128, n), ALU.mult), reads=[PB[5], gret_c], writes=[ytmp])
                P.op("pool", TT(yrT[:, :, cs], ytmp[:, :, 0:n], sgt[:, :, cs], ALU.mult), reads=[ytmp, sgt], writes=[yrT])
                scm = ps(6)[:n, :].rearrange("p (h s) -> p h s", h=4)[:, :, :n]
                for h in range(4):
                    P.op("pe", MM(scm[:, h, :], kmT[:, h, cs], qmT[:, h, cs]), reads=[kmT, qmT], writes=[PB[6]])
                P.op("dve", TT(wTm[:n, :, :n], scm, DT[:n, :, :n], ALU.mult), reads=[PB[6], DT], writes=[wTm])
                ndA = psum_all[:n, 0:1024].rearrange("p (h e) -> p h e", h=4)[:, :, 0:129]
                ndB = psum_all[:n, 1024:2048].rearrange("p (h e) -> p h e", h=4)[:, :, 0:129]
                for h in range(4):
                    P.op("pe", MM(ndA[:, h, :], wTm[:n, h, :n], v1[j][:n, h, :]), reads=[wTm, v1[j]], writes=[PB[0], PB[1]])
                for h in range(4):
                    P.op("pe", MM(ndB[:, h, :], qmT[:, h, cs], Cnbf[:, h, :]), reads=[qmT, Cnbf], writes=[PB[2], PB[3]])
                P.op("act", ACT(otmp[:n], ndA, AF.Copy), reads=[PB[0], PB[1]], writes=[otmp])
                P.op("dve", TT(otmp2[:n], ndB, bc3(G["a"][:n], n, 129), ALU.mult), reads=[PB[2], PB[3], G["a"]], writes=[otmp2])
                P.op("pool", TT(otmp[:n], otmp[:n], otmp2[:n], ALU.add), reads=[otmp, otmp2], writes=[otmp])
                num = otmp[:n, :, 0:128]
                P.op("dve", STT(G["den"][:n], otmp[:n, :, 128], -1.0, otmp[:n, :, 128], ALU.mult, ALU.max), reads=[otmp], writes=[G["den"]])
                P.op("dve", TT(G["den"][:n], G["den"][:n], G["enm"][:n], ALU.max), reads=[G["den"], G["enm"]], writes=[G["den"]])
                P.op("dve", TT(G["d2"][:n], G["den"][:n], G["den"][:n], ALU.mult), reads=[G["den"]], writes=[G["d2"]])
                P.op("dve", TT(sqt[:n], num, num, ALU.mult), reads=[otmp], writes=[sqt])
                P.op("dve", RED(G["ss"][:n], sqt[:n], ALU.add), reads=[sqt], writes=[G["ss"]])
                P.op("dve", TS(G["ss"][:n], G["ss"][:n], 1.0 / 128, ALU.mult), reads=[G["ss"]], writes=[G["ss"]])
                P.op("dve", STT(G["ss"][:n], G["d2"][:n], EPS, G["ss"][:n], ALU.mult, ALU.add), reads=[G["d2"], G["ss"]], writes=[G["ss"]])
                P.op("act", ACT(G["ss"][:n], G["ss"][:n], AF.Sqrt), reads=[G["ss"]], writes=[G["ss"]])
                P.op("dve", lambda e: e.reciprocal(out=G["rs"][:n], in_=G["ss"][:n]), reads=[G["ss"]], writes=[G["rs"]])
                P.op("dve", TT(hnb[:n], num, bc3(G["rs"][:n], n, 128), ALU.mult), reads=[otmp, G["rs"]], writes=[hnb])
                for h in range(4):
                    P.op("pe", TR(psb16(7)[:, h * 128:h * 128 + n], hnb[:n, h, :], identb[:n, :n]), reads=[hnb, identb], writes=[PB[7]])
                tv = psb16(7)[:, 0:512].rearrange("p (h t) -> p h t", h=4)[:, :, 0:n]
                P.op("dve", TT(ytmp[:, :, 0:n], tv, bc3(gml_c[:], 128, n), ALU.mult), reads=[PB[7], gml_c], writes=[ytmp])
                P.op("pool", TT(ytmp2[:, :, 0:n], cT[:, :, cs], bc3(wskip_c[:], 128, n), ALU.mult), reads=[cT, wskip_c], writes=[ytmp2])
                P.op("pool", TT(ytmp[:, :, 0:n], ytmp[:, :, 0:n], ytmp2[:, :, 0:n], ALU.add), reads=[ytmp, ytmp2], writes=[ytmp])
                P.op("pool", TT(ymT[:, :, cs], ytmp[:, :, 0:n], sgo[:, :, cs], ALU.mult), reads=[ytmp, sgo], writes=[ymT])
            for h in range(4):
                P.op("pe", MM(ps(4)[:, h * 128:(h + 1) * 128], kwr[:n, h, :], vr[j][:n, h, :]), reads=[kwr, vr[j]], writes=[PB[4]])
            for h in range(4):
                P.op("dve", STT(S32[:, h, :], S32[:, h, :], GAM[h] ** n, ps(4)[:, h * 128:(h + 1) * 128], ALU.mult, ALU.add),
                     reads=[S32, PB[4]], writes=[S32])
            P.op("act", ACT(Sbf[:], S32[:], AF.Copy), reads=[S32], writes=[Sbf])
            cuA = psum_all[:, 0:1024].rearrange("p (h e) -> p h e", h=4)[:, :, 0:129]
            for h in range(4):
                P.op("pe", MM(cuA[:, h, :], kwm[:n, h, :], v1[j][:n, h, :]), reads=[kwm, v1[j]], writes=[PB[0], PB[1]])
            P.op("pool", TT(Cn32[:], Cn32[:], bc3(aend[:], 128, 129), ALU.mult), reads=[Cn32, aend], writes=[Cn32])
            P.op("dve", TT(Cn32[:], Cn32[:], cuA, ALU.add), reads=[Cn32, PB[0], PB[1]], writes=[Cn32])
            P.op("act", ACT(Cnbf[:], Cn32[:], AF.Copy), reads=[Cn32], writes=[Cnbf])
            if "store_state" in t:
                t["store_state"]()

        def zero_state():
            P.op("pool", MSET(S32[:], 0.0), writes=[S32])
            P.op("pool", MSET(Sbf[:], 0.0), writes=[Sbf])
            P.op("pool", MSET(Cn32[:], 0.0), writes=[Cn32])
            P.op("pool", MSET(Cnbf[:], 0.0), writes=[Cnbf])
            P.op("pool", MSET(mprev[:], 0.0), writes=[mprev])
            P.op("pool", MSET(xmh[:], 0.0), writes=[xmh])

        def flag_state():
            fl = flag[:, 0:1]
            P.op("dve", TS(S32[:], S32[:], fl, ALU.mult), reads=[S32, flag], writes=[S32])
            P.op("dve", TS(Cn32[:], Cn32[:], fl, ALU.mult), reads=[Cn32, flag], writes=[Cn32])
            P.op("dve", TS(mprev[:], mprev[:], fl, ALU.mult), reads=[mprev, flag], writes=[mprev])
            P.op("dve", TS(xmh[:], xmh[:], fl, ALU.mult), reads=[xmh, flag], writes=[xmh])
            P.op("act", ACT(Sbf[:], S32[:], AF.Copy), reads=[S32], writes=[Sbf])
            P.op("act", ACT(Cnbf[:], Cn32[:], AF.Copy), reads=[Cn32], writes=[Cnbf])

        def store_state(o_ret, o_C, o_n, o_m, keys):
            P.dma("sp", o_ret.rearrange("h d e -> d h e"), S32[:], reads=[S32], writes=[DB[keys[0]]], sem_buf=S32)
            P.dma("sp", o_C.rearrange("h d e -> d h e"), Cn32[:, :, 0:128], reads=[Cn32], writes=[DB[keys[1]]], sem_buf=Cn32)
            P.dma("sp", o_n.rearrange("h d -> d h"), Cn32[:, :, 128], reads=[Cn32], writes=[DB[keys[2]]], sem_buf=Cn32, allow_slow_non_contiguous=True)
            P.dma("sp", o_m, mprev[0:1, :], reads=[mprev], writes=[DB[keys[3]]], sem_buf=mprev)

        def store_conv(o_conv, key):
            for cc in range(4):
                P.dma("sp", o_conv[:, cc * 128:(cc + 1) * 128].rearrange("j p -> p j"), xmh[:, cc, :], reads=[xmh], writes=[DB[key]],
                      sem_buf=xmh, allow_slow_non_contiguous=True)

        def load_state(s):
            P.dma("sp", S32[:], I["st_ret"][s].rearrange("h d e -> d h e"), writes=[S32], sem_buf=S32)
            P.dma("sp", Cn32[:, :, 0:128], I["st_C"][s].rearrange("h d e -> d h e"), writes=[Cn32], sem_buf=Cn32)
            P.dma("sp", Cn32[:, :, 128], I["st_n"][s].rearrange("h d -> d h"), writes=[Cn32], sem_buf=Cn32, allow_slow_non_contiguous=True)
            P.dma("sp", mprev[:], I["st_m"][s].partition_broadcast(128), writes=[mprev], sem_buf=mprev)
            P.op("act", ACT(Sbf[:], S32[:], AF.Copy), reads=[S32], writes=[Sbf])
            P.op("act", ACT(Cnbf[:], Cn32[:], AF.Copy), reads=[Cn32], writes=[Cnbf])

        def load_conv(s):
            for cc in range(4):
                P.dma("sp", xmh[:, cc, :], I["st_conv"][s][:, cc * 128:(cc + 1) * 128].rearrange("j p -> p j"), writes=[xmh],
                      sem_buf=xmh, allow_slow_non_contiguous=True)

        def merge_block(tiles, offs, NTOT):
            up = [ring.load(S_up[t], 4096, DB["s_up"]) for t in range(2)]
            upv = [u[:, 0:4096].rearrange("p (cc c) -> p cc c", cc=4) for u in up]
            for dmc in range(8):
                for t, yT in enumerate((yrT, ymT)):
                    bank = 1 + t
                    for cc in range(4):
                        P.op("pe", MM(ps(bank)[:, 0:NTOT], upv[t][:, cc, dmc * 128:(dmc + 1) * 128], yT[:, cc, 0:NTOT], cc == 0, cc == 3),
                             reads=[up[t], yT], writes=[PB[bank]])
                P.op("dve", TT(mg1[:, 0:NTOT], ps(1)[:, 0:NTOT], sgr[:, dmc, 0:NTOT], ALU.mult), reads=[PB[1], sgr], writes=[mg1])
                P.op("dve", TT(mg2[:, 0:NTOT], ps(2)[:, 0:NTOT], sgm[:, dmc, 0:NTOT], ALU.mult), reads=[PB[2], sgm], writes=[mg2])
                P.op("pool", TT(mgT[:, dmc, 0:NTOT], mg1[:, 0:NTOT], mg2[:, 0:NTOT], ALU.add), reads=[mg1, mg2], writes=[mgT])
            wo = [ring.load(S_out[t], 4096, DB["s_out"]) for t in range(2)]
            for j, t in enumerate(tiles):
                n = t["n"]
                for hf in range(2):
                    bank = 3 + hf
                    wv = wo[hf][:, 0:4096].rearrange("p (kc c) -> p kc c", kc=8)
                    for kc in range(8):
                        P.op("pe", MM(ps(bank)[:n, :], mgT[:, kc, offs[j]:offs[j] + n], wv[:, kc, :], kc == 0, kc == 7),
                             reads=[wo[hf], mgT], writes=[PB[bank]])
                    xs = xt[j][:n, hf * 512:(hf + 1) * 512]
                    P.op("dve", TT(xs, xs, ps(bank)[:n, :], ALU.add), reads=[xt[j], PB[bank]], writes=[xt[j]])

        xnT = hT
        arena_reset()
        qpT = A("qpT", [128, 16, NB], BF16)
        sc_s = [A("sc_s%d" % j, [128, 16, 128], F32) for j in range(2)]
        nb1 = [A("nb1_%d" % j, [128, 8, 128], F32) for j in range(2)]
        E1 = [A("E1_%d" % j, [128, 8, 128], F32) for j in range(2)]
        E2 = [A("E2_%d" % j, [128, 8, 128], BF16) for j in range(2)]
        tk = A("tk", [128, 16, 16], F32)
        tkw = A("tkw", [128, 128], F32)
        cand = A("cand", [128, 8, 256], F32)
        candw = A("candw", [128, 256], F32)
        ctop = A("ctop", [128, 8, 24], F32)
        pst = {k: A("pst_" + k, [128, 8], F32) for k in ("th", "Z", "nm1", "nm2", "rz", "lz")}
        cexp = A("cexp", [128, 8, 16], F32)
        IG = 4
        NG = NEC // IG
        Wn = [[[A("Wn%d_%d_%d" % (b, j, h_), [128, IG * 128], BF16) for h_ in range(8)] for j in range(2)] for b in range(2)]
        NMK = 12
        Mk = [A("Mk%d" % i, [128, IG * 128], BF16) for i in range(NMK)]
        Gt = [A("Gt%d" % i, [128, NB], BF16) for i in range(2)]
        WH = [A("WH%d" % i, [128, NB], BF16) for i in range(2)]
        uring = Ring(P, "ur", 4, elems=1024, alloc=A)
        vring = Ring(P, "vr_", 4, elems=1024, alloc=A)
        pt = A("pt", [128, 256], BF16)
        pT = A("pT", [128, 2, 128], BF16)
        pgs = A("pgs", [128, 512], F32)
        yo = [A("yo%d" % j, [128, D], F32) for j in range(1)]
        print("[kernel] arena peer bytes", aoff[0], flush=True)

        def peer_prep_tile(t, j, off):
            n = t["n"]
            for g4 in range(4):
                bank = 4 + g4
                for g in range(4):
                    gg = g4 * 4 + g
                    P.op("pe", MM(ps(bank)[:n, g * 128:(g + 1) * 128], qpT[:, gg, off:off + n], keysT[:, gg, :]), reads=[qpT, keysT], writes=[PB[bank]])
                eng = "act" if g4 % 2 == 0 else "dve"
                dst = sc_s[j][:n, g4 * 4:(g4 + 1) * 4, :]
                src = ps(bank)[:n, :].rearrange("p (g k) -> p g k", g=4)
                if eng == "act":
                    P.op("act", ACT(dst, src, AF.Copy), reads=[PB[bank]], writes=[sc_s[j]])
                else:
                    P.op("dve", CP(dst, src), reads=[PB[bank]], writes=[sc_s[j]])
            S = sc_s[j]
            for g in range(16):
                P.op("dve", lambda e, g=g: e.max(out=tk[:n, g, 0:8], in_=S[:n, g, :]), reads=[S], writes=[tk])
                P.op("dve", lambda e, g=g: e.match_replace(out=tkw[:n, :], in_to_replace=tk[:n, g, 0:8], in_values=S[:n, g, :], imm_value=-1e30),
                     reads=[S, tk], writes=[tkw])
                P.op("dve", lambda e, g=g: e.max(out=tk[:n, g, 8:16], in_=tkw[:n, :]), reads=[tkw], writes=[tk])
            tkv = tk[:n].rearrange("p (hd hf) k -> p hd hf k", hf=2)
            for hd in range(8):
                a_ = tkv[:, hd, 0, :].unsqueeze(2).to_broadcast([n, 16, 16])
                b_ = tkv[:, hd, 1, :].unsqueeze(1).to_broadcast([n, 16, 16])
                P.op("dve", TT(cand[:n, hd, :].rearrange("p (a b) -> p a b", a=16), a_, b_, ALU.add), reads=[tk], writes=[cand])
            for hd in range(8):
                P.op("dve", lambda e, hd=hd: e.max(out=ctop[:n, hd, 0:8], in_=cand[:n, hd, :]), reads=[cand], writes=[ctop])
                P.op("dve", lambda e, hd=hd: e.match_replace(out=candw[:n, :], in_to_replace=ctop[:n, hd, 0:8], in_values=cand[:n, hd, :], imm_value=-1e30),
                     reads=[cand, ctop], writes=[candw])
                P.op("dve", lambda e, hd=hd: e.max(out=ctop[:n, hd, 8:16], in_=candw[:n, :]), reads=[candw], writes=[ctop])
                P.op("dve", lambda e, hd=hd: e.match_replace(out=candw[:n, :], in_to_replace=ctop[:n, hd, 8:16], in_values=candw[:n, :], imm_value=-1e30),
                     reads=[candw, ctop], writes=[candw])
                P.op("dve", lambda e, hd=hd: e.max(out=ctop[:n, hd, 16:24], in_=candw[:n, :]), reads=[candw], writes=[ctop])
            P.op("dve", TT(pst["th"][:n], ctop[:n, :, 15], ctop[:n, :, 16], ALU.add), reads=[ctop], writes=[pst["th"]])
            P.op("dve", TS(pst["th"][:n], pst["th"][:n], 0.5, ALU.mult), reads=[pst["th"]], writes=[pst["th"]])
            P.op("dve", TT(cexp[:n], ctop[:n, :, 0:16], ctop[:n, :, 0:1].to_broadcast([n, 8, 16]), ALU.subtract), reads=[ctop], writes=[cexp])
            P.op("act", ACT(cexp[:n], cexp[:n], AF.Exp), reads=[cexp], writes=[cexp])
            P.op("dve", RED(pst["Z"][:n], cexp[:n], ALU.add), reads=[cexp], writes=[pst["Z"]])
            P.op("act", ACT(pst["lz"][:n], pst["Z"][:n], AF.Ln), reads=[pst["Z"]], writes=[pst["lz"]])
            Sv = S[:n].rearrange("p (hd hf) k -> p hd hf k", hf=2)
            P.op("dve", TT(nb1[j][:n], pst["th"][:n].unsqueeze(2).to_broadcast([n, 8, 128]), Sv[:, :, 0, :], ALU.subtract),
                 reads=[pst["th"], S], writes=[nb1[j]])
            P.op("dve", TT(cand[:n, :, 0:128], Sv[:, :, 0, :], tkv[:, :, 0, 0:1].to_broadcast([n, 8, 128]), ALU.subtract), reads=[S, tk], writes=[cand])
            P.op("act", ACT(E1[j][:n], cand[:n, :, 0:128], AF.Exp), reads=[cand], writes=[E1[j]])
            P.op("dve", TT(pst["nm2"][:n], tkv[:, :, 1, 0], pst["lz"][:n], ALU.add), reads=[tk, pst["lz"]], writes=[pst["nm2"]])
            P.op("dve", TT(cand[:n, :, 128:256], Sv[:, :, 1, :], pst["nm2"][:n].unsqueeze(2).to_broadcast([n, 8, 128]), ALU.subtract),
                 reads=[S, pst["nm2"]], writes=[cand])
            P.op("act", ACT(E2[j][:n], cand[:n, :, 128:256], AF.Exp), reads=[cand], writes=[E2[j]])

        mkctr = [0]

        def mask_slices(tiles, grp, buf, nsl):
            items = [(j, t["n"], hd) for j, t in enumerate(tiles) for hd in range(8)]
            per = (len(items) + nsl - 1) // nsl
            assert per <= NMK // 2
            slices = []
            for si in range(nsl):
                chunk = items[si * per:(si + 1) * per]

                def f(chunk=chunk):
                    i0 = grp * IG
                    st_ = []
                    for (j, n, hd) in chunk:
                        k = mkctr[0]
                        mkctr[0] += 1
                        mk = Mk[k % NMK]
                        mkv = mk[:n, :].rearrange("p (a b) -> p a b", a=IG)
                        Sv = sc_s[j][:n].rearrange("p (hd hf) k -> p hd hf k", hf=2)
                        s2b = Sv[:, hd, 1, :].unsqueeze(1).to_broadcast([n, IG, 128])
                        thb = nb1[j][:n, hd, i0:i0 + IG].unsqueeze(2).to_broadcast([n, IG, 128])
                        P.op("dve", TT(mkv, s2b, thb, ALU.is_ge), reads=[sc_s[j], nb1[j]], writes=[mk])
                        st_.append((j, n, hd, k, mk, mkv))
                    for (j, n, hd, k, mk, mkv) in st_:
                        e2b = E2[j][:n, hd, :].unsqueeze(1).to_broadcast([n, IG, 128])
                        eng = "dve"
                        P.op(eng, TT(mkv, mkv, e2b, ALU.mult), reads=[mk, E2[j]], writes=[mk])
                    for (j, n, hd, k, mk, mkv) in st_:
                        wb__ = Wn[buf][j][hd]
                        wv = wb__[:n, :].rearrange("p (a b) -> p a b", a=IG)
                        if k % 3 == 0:
                            e1b = E1[j][:n, hd, i0:i0 + IG].unsqueeze(2).to_broadcast([n, IG, 128])
                            P.op("dve", TT(wv, mkv, e1b, ALU.mult), reads=[mk, E1[j]], writes=[wb__])
                        else:
                            for gi_ in range(IG):
                                P.op("act", ACT(wv[:, gi_, :], mkv[:, gi_, :], AF.Copy, scale=E1[j][:n, hd, i0 + gi_:i0 + gi_ + 1]),
                                     reads=[mk, E1[j]], writes=[wb__])
                slices.append(f)
            return slices

        def peer_block(tiles, offs, NTOT):
            for j, t in enumerate(tiles):
                norm_T(xt[j], t["n"], xnT, offs[j], 0)
            items = [(S_pq[qd], 4096, DB["s_pq"]) for qd in range(4)]

            def do_q(i, panel):
                pv = panel[:, 0:4096].rearrange("p (g kc c) -> p g kc c", g=4, kc=8)
                for gi in range(4):
                    bank = 1 + (gi % 2)
                    for kc in range(8):
                        P.op("pe", MM(ps(bank)[:, 0:NTOT], pv[:, gi, kc, :], xnT[:, kc, 0:NTOT], kc == 0, kc == 7), reads=[panel, xnT], writes=[PB[bank]])
                    P.op("act", ACT(qpT[:, i * 4 + gi, 0:NTOT], ps(bank)[:, 0:NTOT], AF.Copy), reads=[PB[bank]], writes=[qpT])

            stream(ring, items, do_q, L=2)
            for j, t in enumerate(tiles):
                peer_prep_tile(t, j, offs[j])
            for f in mask_slices(tiles, 0, 0, IG):
                f()
            ybank = [[0, 1], [2, 3]]
            upend = {}
            vpend = {}

            def uload(ec):
                upend[ec] = uring.load(S_ut[ec], 1024, DB["s_ut"])

            def vload(ec):
                vpend[ec] = vring.load(S_v[ec * 128:(ec + 1) * 128, :], 1024, DB["s_v"])

            for e0 in range(min(3, NEC)):
                uload(e0)
                vload(e0)

            def emit_U(ec):
                up_ = upend.pop(ec)
                uv = up_[:, 0:1024].rearrange("p (kc e) -> p kc e", kc=8)
                hb_ = 4 + (ec % 2)
                for kc in range(8):
                    P.op("pe", MM(ps(hb_)[:, 0:NTOT], uv[:, kc, :], xnT[:, kc, 0:NTOT], kc == 0, kc == 7), reads=[up_, xnT], writes=[PB[hb_]])
                gt = Gt[ec % 2]
                P.op("act", ACT(gt[:, 0:NTOT], ps(hb_)[:, 0:NTOT], AF.Gelu), reads=[PB[hb_]], writes=[gt])

            def emit_T(ec):
                grp, gi = ec // IG, ec % IG
                wb_ = 6 + (ec % 2)
                for j, t in enumerate(tiles):
                    n = t["n"]
                    for hd in range(8):
                        P.op("pe", MM(ps(wb_)[:, offs[j]:offs[j] + n], Wn[grp % 2][j][hd][:n, gi * 128:(gi + 1) * 128], identb[:n, :n], hd == 0, hd == 7),
                             reads=[Wn[grp % 2][j][hd], identb], writes=[PB[wb_]])

            def emit_WH(ec):
                wb_ = 6 + (ec % 2)
                P.op("dve", TT(WH[ec % 2][:, 0:NTOT], ps(wb_)[:, 0:NTOT], Gt[ec % 2][:, 0:NTOT], ALU.mult), reads=[PB[wb_], Gt[ec % 2]], writes=[WH[ec % 2]])

            def emit_V(ec):
                vp = vpend.pop(ec)
                wh = WH[ec % 2]
                for j, t in enumerate(tiles):
                    n = t["n"]
                    for hf in range(2):
                        yb = ybank[j][hf]
                        P.op("pe", MM(ps(yb)[:n, :], wh[:, offs[j]:offs[j] + n], vp[:, hf * 512:(hf + 1) * 512], ec == 0, ec == NEC - 1),
                             reads=[wh, vp], writes=[PB[yb]])

            emit_U(0)
            emit_T(0)
            emit_WH(0)
            nxt = []
            for ec in range(NEC):
                grp, gi = ec // IG, ec % IG
                if gi == 0:
                    nxt = mask_slices(tiles, grp + 1, (grp + 1) % 2, IG - 1) if grp + 1 < NG else []
                if ec + 3 < NEC:
                    uload(ec + 3)
                    vload(ec + 3)
                if ec + 1 < NEC:
                    emit_U(ec + 1)
                if gi < len(nxt):
                    nxt[gi]()
                if ec + 1 < NEC:
                    emit_T(ec + 1)
                emit_V(ec)
                if ec + 1 < NEC:
                    emit_WH(ec + 1)
            for j, t in enumerate(tiles):
                n = t["n"]
                for hf in range(2):
                    xs = xt[j][:n, hf * 512:(hf + 1) * 512]
                    P.op("dve", TT(xs, xs, ps(ybank[j][hf])[:n, :], ALU.add), reads=[xt[j], PB[ybank[j][hf]]], writes=[xt[j]])


        def ple_block(tiles, offs, NTOT):
            for j, t in enumerate(tiles):
                norm_T(xt[j], t["n"], xnT, offs[j], 0)
            wg = [ring.load(S_pg[t], 4096, DB["s_pg"]) for t in range(2)]
            for j, t in enumerate(tiles):
                n = t["n"]
                P.dma("pool", pt[:n, :], t["p"], reads=[DB[t["pkey"]]], writes=[pt], sem_buf=pt)
                for kc in range(2):
                    P.op("pe", TR(psb16(1)[:, kc * 128:kc * 128 + n], pt[:n, kc * 128:(kc + 1) * 128], identb[:n, :n]), reads=[pt, identb], writes=[PB[1]])
                P.op("dve", CP(pT[:, :, 0:n], psb16(1)[:, 0:256].rearrange("p (kc t) -> p kc t", kc=2)[:, :, 0:n]), reads=[PB[1]], writes=[pT])
                for hf in range(2):
                    wv = wg[hf][:, 0:4096].rearrange("p (kc c) -> p kc c", kc=8)
                    for kc in range(8):
                        P.op("pe", MM(ps(2)[:n, :], xnT[:, kc, offs[j]:offs[j] + n], wv[:, kc, :], kc == 0, kc == 7), reads=[wg[hf], xnT], writes=[PB[2]])
                    P.op("act", ACT(pgs[:n, :], ps(2)[:n, :], AF.Sigmoid), reads=[PB[2]], writes=[pgs])
                    for kc in range(2):
                        P.op("pe", MM(ps(3)[:n, :], pT[:, kc, 0:n], wple[:, kc, hf * 512:(hf + 1) * 512], kc == 0, kc == 1), reads=[pT, wple], writes=[PB[3]])
                    P.op("dve", TT(pgs[:n, :], pgs[:n, :], ps(3)[:n, :], ALU.mult), reads=[pgs, PB[3]], writes=[pgs])
                    xs = xt[j][:n, hf * 512:(hf + 1) * 512]
                    P.op("pool", TT(xs, xs, pgs[:n, :], ALU.add), reads=[xt[j], pgs], writes=[xt[j]])
                rms_rstd(xt[j], n)
                P.op("dve", STT(yo[0][:n, :], xt[j][:n, :], rstd[:n, 0:1], gfin_b[:n, :], ALU.mult, ALU.mult), reads=[xt[j], rstd, gfin_b], writes=[yo[0]])
                P.dma("sp", t["y"], yo[0][:n, :], reads=[yo[0]], writes=[DB[t["ykey"]]], sem_buf=yo[0])

        zero_state()
        if NTP > 0:
            nblk = NTP // 2
            per_blk = (len(jobs_later) + nblk - 1) // nblk
            for b in range(nblk):
                tiles = []
                for jj in range(2):
                    ti = b * 2 + jj
                    tiles.append(dict(n=128, x=I["x_pre"][ti * 128:(ti + 1) * 128, :], xkey="x_pre",
                                      cs=I["c_cs_pre"][:, ti * 128:(ti + 1) * 128], sn=I["c_sn_pre"][:, ti * 128:(ti + 1) * 128]))
                mixer_block(tiles, False)
                for jb in jobs_later[b * per_blk:(b + 1) * per_blk]:
                    jb()
            jobs_later = jobs_later[nblk * per_blk:]
            flag_state()
        for jb in jobs_later:
            jb()
        for b in range(NTM // 2):
            tiles = []
            for jj in range(2):
                ti = b * 2 + jj
                rows = slice(ti * 128, (ti + 1) * 128)
                tiles.append(dict(n=128, x=I["x_main"][rows, :], xkey="x_main", p=I["p_main"][rows, :], pkey="p_main",
                                  y=O["y_main"][rows, :], ykey="y_main",
                                  cs=I["c_cs_main"][:, rows], sn=I["c_sn_main"][:, rows]))
            offs, NTOT = mixer_block(tiles, True)
            if b == NTM // 2 - 1:
                store_state(O["ret_p"], O["C_p"], O["n_p"], O["m_p"], ("ret_p", "C_p", "n_p", "m_p"))
                store_conv(O["conv_p"], "conv_p")
            merge_block(tiles, offs, NTOT)
            P.barrier()
            if stop_after == "mixer":
                for j, t in enumerate(tiles):
                    P.dma("sp", O["dbg"][(b * 2 + j) * 128:(b * 2 + j + 1) * 128, :], xt[j][:], reads=[xt[j]], writes=[DB["dbg"]], sem_buf=xt[j])
                continue
            peer_block(tiles, offs, NTOT)
            if stop_after == "peer":
                for j, t in enumerate(tiles):
                    P.dma("sp", O["dbg"][(b * 2 + j) * 128:(b * 2 + j + 1) * 128, :], xt[j][:], reads=[xt[j]], writes=[DB["dbg"]], sem_buf=xt[j])
                continue
            ple_block(tiles, offs, NTOT)
            P.barrier()
        if NS > 0 and not stop_after:
            tiles = []
            for s in range(NS):
                rows = slice(s * 32, (s + 1) * 32)
                tiles.append(dict(
                    n=32, x=I["x_smp"][rows, :], xkey="x_smp", p=I["p_smp"][rows, :], pkey="p_smp", y=O["y_smp"][rows, :], ykey="y_smp",
                    cs=I["c_cs_smp"], sn=I["c_sn_smp"],
                    load_state=(lambda s=s: load_state(s)), load_conv=(lambda s=s: load_conv(s)),
                    store_state=(lambda s=s: store_state(O["ret_s"][s], O["C_s"][s], O["n_s"][s], O["m_s"][s:s + 1, :], ("ret_s", "C_s", "n_s", "m_s"))),
                    store_conv=(lambda s=s: store_conv(O["conv_s"][s], "conv_s"))))
            offs, NTOT = mixer_block(tiles, True)
            merge_block(tiles, offs, NTOT)
            for s in range(1, NS):
                P.dma("sp", xt[0][s * 32:(s + 1) * 32, :], xt[s][0:32, :], reads=[xt[s]], writes=[xt[0]], sem_buf=xt[0])
            P.barrier()
            nn = NS * 32
            tiles2 = [dict(n=nn, p=I["p_smp"][0:nn, :], pkey="p_smp", y=O["y_smp"][0:nn, :], ykey="y_smp")]
            peer_block(tiles2, [0], nn)
            ple_block(tiles2, [0], nn)
        P.finish([DB[k] for k in O])
        P.replay()
        print("[kernel] instructions:", P.ninst, {k: len(v) for k, v in P.streams.items()}, "sems:", P.nsem, flush=True)
    return nc


def _consts(pos_main, pos_pre, pos_smp):
    c = {}
    c["c_ident"] = np.eye(128, dtype=np.float32)
    c["c_ones"] = np.ones((128, 128), np.float32)
    pi = np.arange(128)[:, None]
    fi = np.arange(128)[None, :]
    c["c_tri"] = (pi <= fi).astype(np.float32)
    negU = np.where(fi > pi, NEG, 0.0).astype(np.float32)
    negL = np.where(pi > fi, NEG, 0.0).astype(np.float32)
    c["c_negU"] = np.tile(negU, (1, 4))
    c["c_negL"] = np.tile(negL, (1, 4))
    dm = np.zeros((128, 4, 128), np.float64)
    for h in range(4):
        dm[:, h, :] = np.where(fi >= pi, GAM[h] ** np.maximum(fi - pi, 0), 0.0) * ISQ
    c["c_dmT"] = dm.reshape(128, 512).astype(np.float32)
    c["c_wread"] = np.stack([GAM[h] ** (np.arange(128) + 1.0) for h in range(4)], 1).astype(np.float32)
    c["c_wend128"] = (np.stack([GAM[h] ** (127.0 - np.arange(128)) for h in range(4)], 1) * ISQ).astype(np.float32)
    c["c_wend32"] = (np.stack([GAM[h] ** np.maximum(31.0 - np.arange(128), 0.0) for h in range(4)], 1) * ISQ).astype(np.float32)
    s128 = np.zeros((128, 128), np.float32); s128[127, :] = 1.0
    s32 = np.zeros((128, 128), np.float32); s32[31, :] = 1.0
    c["c_sel128"] = s128; c["c_sel32"] = s32

    def rope_tabs(pos):
        half = 64
        inv = (10000.0 ** (-np.arange(half, dtype=np.float32) / half)).astype(np.float32)
        ang = pos.astype(np.float32)[:, None] * inv[None, :]
        cos = np.cos(ang).T.astype(np.float32)
        sin = np.sin(ang).T.astype(np.float32)
        return (np.ascontiguousarray(np.concatenate([cos, cos], 0)),
                np.ascontiguousarray(np.concatenate([-sin, sin], 0)))

    c["c_cs_main"], c["c_sn_main"] = rope_tabs(pos_main)
    c["c_cs_pre"], c["c_sn_pre"] = rope_tabs(pos_pre)
    c["c_cs_smp"], c["c_sn_smp"] = rope_tabs(pos_smp)
    return c


def _perm_w_in(w_in):
    q, k, v, gt = w_in[:, 0:512], w_in[:, 512:1024], w_in[:, 1024:1536], w_in[:, 1536:2048]
    xm, vm, om = w_in[:, 2048:2560], w_in[:, 2560:3072], w_in[:, 3072:3584]
    gi, gf = w_in[:, 3584:3588], w_in[:, 3588:3592]
    gr, gm = w_in[:, 3592:4616], w_in[:, 4616:5640]

    def swap(w):
        w4 = w.reshape(1024, 4, 2, 64)
        return w4[:, :, ::-1, :].reshape(1024, 512)

    fm = np.concatenate([q, swap(q), k, swap(k), gt, xm, om, gr, gm], axis=1)
    tm = np.concatenate([v, vm, gi, gf], axis=1)
    return np.ascontiguousarray(fm), np.ascontiguousarray(tm)


_CACHE = {}


def kernel(x_prompt, x_sample, p_prompt, p_sample, state_ret, state_mlstm_C, state_mlstm_n,
           state_mlstm_m, state_conv, g_mix, w_in, g_ret_gn, w_mq, w_mk, conv_w, conv_b, b_i, b_f,
           g_ml_gn, w_skip, w_up_r, w_up_m, w_out, g_ffn, w_pq, peer_keys, peer_u, peer_v,
           g_ple, w_pg, w_ple, g_final, _cfg=None):
    f = lambda a: np.ascontiguousarray(np.asarray(a, dtype=np.float32))
    x_prompt, x_sample, p_prompt, p_sample = f(x_prompt), f(x_sample), f(p_prompt), f(p_sample)
    B, SEQ, _ = x_prompt.shape
    DB_, DS = x_sample.shape[0], x_sample.shape[1]
    assert B == 4 and DS == 32 and DB_ == 16
    HALF = SEQ // 2
    cfg = dict(nt_main=HALF // 128, nt_pre=HALF // 128, n_smp=2, nec=128)
    if _cfg:
        cfg.update(_cfg)
    key = tuple(sorted(cfg.items()))
    if key not in _CACHE:
        _CACHE[key] = build_program(cfg)
    nc = _CACHE[key]
    past_len = 2048
    fm, tm = _perm_w_in(f(w_in)[0])
    shared = {
        "w_in_fm": fm, "w_in_tm": tm, "g_mix": f(g_mix)[0], "g_ret_gn": f(g_ret_gn)[0], "w_mq": f(w_mq)[0], "w_mk": f(w_mk)[0],
        "conv_w": f(conv_w)[0], "conv_b": f(conv_b)[0], "b_if": np.concatenate([f(b_i)[0], f(b_f)[0]]),
        "g_ml_gn": f(g_ml_gn)[0], "w_skip": f(w_skip)[0], "w_up_r": f(w_up_r)[0], "w_up_m": f(w_up_m)[0], "w_out": f(w_out)[0],
        "g_ffn": f(g_ffn)[0], "w_pq": f(w_pq)[0], "peer_keys": f(peer_keys)[0].reshape(16, 128, 128), "peer_u": f(peer_u)[0],
        "peer_v": f(peer_v)[0], "g_ple": f(g_ple)[0], "w_pg": f(w_pg)[0], "w_ple": f(w_ple)[0], "g_final": f(g_final),
    }
    in_maps = []
    for c in range(8):
        b, half = c // 2, c % 2
        rows = slice(half * HALF, (half + 1) * HALF)
        pos_main = np.arange(half * HALF, (half + 1) * HALF)
        pos_pre = np.arange(0, HALF)
        pos_smp = past_len + np.arange(DS)
        m = dict(shared)
        m.update(_consts(pos_main, pos_pre, pos_smp))
        m["x_main"] = x_prompt[b, rows]
        m["p_main"] = p_prompt[0, b, rows]
        m["x_pre"] = x_prompt[b, 0:HALF]
        m["flag"] = np.full((128, 1), float(half), np.float32)
        ss = slice(2 * c, 2 * c + 2)
        m["x_smp"] = x_sample[ss].reshape(64, 1024)
        m["p_smp"] = p_sample[0, ss].reshape(64, 256)
        m["st_ret"] = f(state_ret)[0, ss]
        m["st_C"] = f(state_mlstm_C)[0, ss]
        m["st_n"] = f(state_mlstm_n)[0, ss]
        m["st_m"] = f(state_mlstm_m)[0, ss]
        m["st_conv"] = f(state_conv)[0, ss]
        in_maps.append({k: np.ascontiguousarray(v) for k, v in m.items()})
    res = run_bass_kernel_spmd(nc, in_maps, core_ids=list(range(8)))
    R = res.results
    y_prompt = np.stack([np.concatenate([R[2 * b]["y_main"], R[2 * b + 1]["y_main"]], 0) for b in range(4)], 0)
    y_sample = np.concatenate([R[c]["y_smp"].reshape(2, 32, 1024) for c in range(8)], 0)
    gp = lambda k: np.stack([R[2 * b + 1][k] for b in range(4)], 0)[None]
    ret_p, C_p, n_p = gp("ret_p"), gp("C_p"), gp("n_p")
    m_p = np.stack([R[2 * b + 1]["m_p"][0] for b in range(4)], 0)[None]
    conv_p = gp("conv_p")
    gsm = lambda k: np.concatenate([R[c][k] for c in range(8)], 0)[None]
    outs = (y_prompt, y_sample, ret_p, C_p, n_p, m_p, conv_p, gsm("ret_s"), gsm("C_s"), gsm("n_s"), gsm("m_s"), gsm("conv_s"))
    if _cfg and _cfg.get("stop_after"):
        return outs, [R[c]["dbg"] for c in range(8)]
    return tuple(np.ascontiguousarray(o, dtype=np.float32) for o in outs)
```
